# Optimizing a Trainium2 kernel written in Bass

```python
import math
import jax, jax.numpy as jnp
from jax import lax
import numpy as np

D_MODEL = 1024
BATCH = 4
SEQ = 8192
DEPTH = 2

N_A = DEPTH // 2
N_B = DEPTH - N_A
CHUNK = 128
D_GATE = D_MODEL
N_GROUPS = 8
GROUP_DIM = D_GATE // N_GROUPS
N_HEADS = 8
HEAD_DIM = 64
V_DIM = 2 * HEAD_DIM
QK_WIDTH = N_HEADS * 2 * HEAD_DIM
V_WIDTH = N_HEADS * V_DIM
Q_BLOCK = 128
D_FF = int(math.ceil(8 * D_MODEL / 3 / 256) * 256)
EPS = 1e-6

kernel_name = "yoco_gmlp_diffattn_alibi_swiglu"


def _alibi_slopes(n):
    return np.array([2.0 ** (-8.0 * (i + 1) / n) for i in range(n)], dtype=np.float32)


def _lambda_init(layer_idx):
    return 0.8 - 0.6 * math.exp(-0.3 * layer_idx)


def rms_norm(x, g):
    xf = x.astype(jnp.float32)
    y = xf * lax.rsqrt(jnp.mean(xf * xf, axis=-1, keepdims=True) + EPS)
    return (y * g.astype(jnp.float32)).astype(x.dtype)


def layer_norm_gain(x, g):
    xf = x.astype(jnp.float32)
    mu = jnp.mean(xf, axis=-1, keepdims=True)
    var = jnp.mean(jnp.square(xf - mu), axis=-1, keepdims=True)
    return ((xf - mu) * lax.rsqrt(var + EPS) * g.astype(jnp.float32)).astype(x.dtype)


def swiglu(h, w_gu, w_down):
    gu = h @ w_gu
    g, u = jnp.split(gu, 2, axis=-1)
    return (jax.nn.silu(g) * u) @ w_down


def gmlp_mixer(h, w_in, v_gain, w_sp, b_sp, w_out):
    B, S, _ = h.shape
    uv = jax.nn.gelu(h @ w_in)
    u, v = jnp.split(uv, 2, axis=-1)
    v = layer_norm_gain(v, v_gain)
    v = v.reshape(B, S // CHUNK, CHUNK, N_GROUPS, GROUP_DIM)
    causal = jnp.tril(jnp.ones((CHUNK, CHUNK), dtype=w_sp.dtype))
    z = jnp.einsum('gts,bcsgd->bctgd', w_sp * causal, v)
    z = z + b_sp.T[None, None, :, :, None]
    return (u * z.reshape(B, S, D_GATE)) @ w_out


def shared_kv(x, g, w_kv):
    B, S, _ = x.shape
    kv = rms_norm(x, g) @ w_kv
    k = kv[..., :QK_WIDTH].reshape(B, S, N_HEADS, 2, HEAD_DIM).transpose(0, 2, 3, 1, 4)
    v = kv[..., QK_WIDTH:].reshape(B, S, N_HEADS, V_DIM).transpose(0, 2, 1, 3)
    return k, v


def diff_attention(h, k, v, w_q, lam_vecs, subln, w_o, lambda_init):
    B, S, _ = h.shape
    n_blk = S // Q_BLOCK
    q = (h @ w_q).reshape(B, S, N_HEADS, 2, HEAD_DIM)
    q = q.reshape(B, n_blk, Q_BLOCK, N_HEADS, 2, HEAD_DIM).transpose(1, 0, 3, 4, 2, 5)
    lf = lam_vecs.astype(jnp.float32)
    lam = jnp.exp(jnp.sum(lf[0] * lf[1])) - jnp.exp(jnp.sum(lf[2] * lf[3])) + lambda_init
    slopes = jnp.asarray(_alibi_slopes(N_HEADS))
    scale = HEAD_DIM ** -0.5
    kf = k.astype(jnp.float32)
    vf = v.astype(jnp.float32)
    key_pos = jnp.arange(S, dtype=jnp.int32)

    def block(args):
        qb, bi = args
        q_pos = bi * Q_BLOCK + jnp.arange(Q_BLOCK, dtype=jnp.int32)
        dist = (q_pos[:, None] - key_pos[None, :]).astype(jnp.float32)
        bias = -slopes[:, None, None] * dist[None]
        s = jnp.einsum('bhitd,bhisd->bhits', qb.astype(jnp.float32), kf) * scale
        s = s + bias[None, :, None]
        s = jnp.where((dist >= 0)[None, None, None], s, -jnp.inf)
        p = jax.nn.softmax(s, axis=-1)
        a = p[:, :, 0] - lam * p[:, :, 1]
        return jnp.einsum('bhts,bhsd->bhtd', a, vf)

    o = lax.map(block, (q, jnp.arange(n_blk, dtype=jnp.int32)))
    o = o.transpose(1, 0, 3, 2, 4).reshape(B, S, N_HEADS, V_DIM).astype(h.dtype)
    o = rms_norm(o, subln) * (1.0 - lambda_init)
    return o.reshape(B, S, V_WIDTH) @ w_o


def setup_inputs(seed: int = 0) -> dict:
    key = jax.random.key(seed)
    ks = jax.random.split(key, 20)
    f32 = jnp.float32

    def w(k, shape, fan_in):
        return jax.random.normal(k, shape, f32) * (fan_in ** -0.5)

    def gain(k, shape):
        return 1.0 + 0.05 * jax.random.normal(k, shape, f32)

    return {
        "x": jax.random.normal(ks[0], (BATCH, SEQ, D_MODEL), f32),
        "a_norm": gain(ks[1], (N_A, D_MODEL)),
        "a_w_in": w(ks[2], (N_A, D_MODEL, 2 * D_GATE), D_MODEL),
        "a_v_norm": gain(ks[3], (N_A, D_GATE)),
        "a_w_sp": w(ks[4], (N_A, N_GROUPS, CHUNK, CHUNK), CHUNK),
        "a_b_sp": 1.0 + 0.1 * jax.random.normal(ks[5], (N_A, N_GROUPS, CHUNK), f32),
        "a_w_out": w(ks[6], (N_A, D_GATE, D_MODEL), D_GATE),
        "ffn_norm": gain(ks[7], (DEPTH, D_MODEL)),
        "ffn_w_gu": w(ks[8], (DEPTH, D_MODEL, 2 * D_FF), D_MODEL),
        "ffn_w_down": w(ks[9], (DEPTH, D_FF, D_MODEL), D_FF),
        "kv_norm": gain(ks[10], (D_MODEL,)),
        "kv_w": w(ks[11], (D_MODEL, QK_WIDTH + V_WIDTH), D_MODEL),
        "b_norm": gain(ks[12], (N_B, D_MODEL)),
        "b_w_q": w(ks[13], (N_B, D_MODEL, QK_WIDTH), D_MODEL),
        "b_lambda": 0.1 * jax.random.normal(ks[14], (N_B, 4, HEAD_DIM), f32),
        "b_subln": gain(ks[15], (N_B, V_DIM)),
        "b_w_o": w(ks[16], (N_B, V_WIDTH, D_MODEL), V_WIDTH),
        "final_norm": gain(ks[17], (D_MODEL,)),
    }


def reference(x, a_norm, a_w_in, a_v_norm, a_w_sp, a_b_sp, a_w_out, ffn_norm, ffn_w_gu, ffn_w_down,
              kv_norm, kv_w, b_norm, b_w_q, b_lambda, b_subln, b_w_o, final_norm):
    k = v = None
    for l in range(DEPTH):
        if l < N_A:
            x = x + gmlp_mixer(rms_norm(x, a_norm[l]), a_w_in[l], a_v_norm[l],
                               a_w_sp[l], a_b_sp[l], a_w_out[l])
        else:
            j = l - N_A
            if j == 0:
                k, v = shared_kv(x, kv_norm, kv_w)
            x = x + diff_attention(rms_norm(x, b_norm[j]), k, v, b_w_q[j], b_lambda[j],
                                   b_subln[j], b_w_o[j], _lambda_init(l))
        x = x + swiglu(rms_norm(x, ffn_norm[l]), ffn_w_gu[l], ffn_w_down[l])
    return rms_norm(x, final_norm)
```

```python
import math
import numpy as np
import ml_dtypes
import concourse.bass as bass
import concourse.mybir as mybir
from concourse.bass_utils import run_bass_kernel_spmd

F32 = mybir.dt.float32
BF16 = mybir.dt.bfloat16
AF = mybir.ActivationFunctionType
ALU = mybir.AluOpType
NPBF = ml_dtypes.bfloat16

D = 1024
S = 8192
B = 4
DFF = 2816
NJ = DFF // 128
T = 512
NT = 8
TOK = NT * T
EPS = 1e-6
NH = 8
KROWS = 80
LAMBDA_INIT = 0.8 - 0.6 * math.exp(-0.3 * 1)
STAGES = {"gmlp": True, "ffn": True, "dbg": False, "attn": True, "ffnB": True, "dbgB": False}


def tile_index(role, j):
    if role == 0:
        return 2 * j if j % 2 == 0 else 2 * j + 1
    return 2 * j + 1 if j % 2 == 0 else 2 * j


class Res:
    __slots__ = ("w", "r")

    def __init__(self):
        self.w = None
        self.r = {}


COMPUTE = ("pe", "act", "dve", "pool")


class Sched:
    ND = 12

    def __init__(self):
        self.q = {e: [] for e in COMPUTE + ("sp",)}
        self.cnt = {e: 0 for e in COMPUTE}
        self.waited = {e: {} for e in COMPUTE + ("sp",)}
        self.pending = {e: [] for e in COMPUTE}
        self.dn = {"sp": 0, "pool": 0, "act": 0}
        self.dcnt = {}
        self.n_inst = 0

    def _deps(self, reads, writes):
        toks = []
        for r in reads:
            if r.w is not None:
                toks.append(r.w)
        for w in writes:
            if w.w is not None:
                toks.append(w.w)
            toks.extend(w.r.items())
        return toks

    def _wait(self, eng, toks):
        wd = self.waited[eng]
        for key, val in toks:
            if eng == "pe" and key == "pe":
                continue
            if wd.get(key, 0) < val:
                wd[key] = val
                self.q[eng].append(("wait", key, val))

    @staticmethod
    def _mark(tok, reads, writes):
        key, val = tok
        for r in reads:
            if r.r.get(key, 0) < val:
                r.r[key] = val
        for w in writes:
            w.w = tok
            w.r = {}

    def op(self, eng, fn, reads=(), writes=(), inc=True):
        self._wait(eng, self._deps(reads, writes))
        self.n_inst += 1
        if not inc:
            self._mark((eng, self.cnt[eng] + 1), reads, writes)
            self.q[eng].append(("op", fn, None))
            return None
        self.cnt[eng] += 1
        tok = (eng, self.cnt[eng])
        self.q[eng].append(("op", fn, tok))
        self._mark(tok, reads, writes)
        return tok

    def dma(self, queue, fn, reads=(), writes=()):
        toks = self._deps(reads, writes)
        nd = 6 if queue == "pool" else self.ND
        idx = self.dn[queue] % nd
        self.dn[queue] += 1
        key = ("d", queue, idx)
        c = self.dcnt.get(key, 0)
        if c > 0:
            toks.append((key, 16 * c))
        self._wait(queue, toks)
        self.dcnt[key] = c + 1
        tok = (key, 16 * (c + 1))
        self.q[queue].append(("op", fn, tok))
        self._mark(tok, reads, writes)
        self.n_inst += 1
        return tok

    def finish(self, eng="sp"):
        toks = [(e, self.cnt[e]) for e in COMPUTE if self.cnt[e] > 0]
        toks += [(k, (16 * c if k[0] == "d" else c)) for k, c in self.dcnt.items()]
        for e in COMPUTE + ("sp",):
            self._wait(e, [t for t in toks if t[0] != e])

    def replay(self, nc):
        sem_names = list(COMPUTE) + [k for k in self.dcnt]
        import contextlib
        with contextlib.ExitStack() as es:
            sems = {}
            for i, k in enumerate(sem_names):
                sems[k] = es.enter_context(nc.semaphore("s%d" % i))
            block = es.enter_context(nc.Block())
            engmap = {"pe": block.tensor, "act": block.scalar, "dve": block.vector,
                      "pool": block.gpsimd, "sp": block.sync}
            for ename, starter in engmap.items():
                items = self.q[ename]

                def body(eng, items=items):
                    for it in items:
                        if it[0] == "wait":
                            eng.wait_ge(sems[it[1]], it[2])
                        else:
                            inst = it[1](eng)
                            tok = it[2]
                            if tok is not None:
                                key = tok[0]
                                if isinstance(key, tuple) and key[0] == "c":
                                    inst.then_inc(sems[key])
                                else:
                                    inst.then_inc(sems[key], 16 if isinstance(key, tuple) else 1)
                starter(body)


class Ctx:
    pass


def emit_square(sc, c, t, k):
    ts = slice(t * T, (t + 1) * T)
    sc.op("dve", lambda e, k=k: e.tensor_tensor(out=c.sq[:, k, :], in0=c.xT[:, k, ts], in1=c.xT[:, k, ts], op=ALU.mult),
          reads=[c.r_x[k][t]], writes=[c.r_sq[k]])


def emit_norm(sc, c, t, gcol, squares_done=False):
    ts = slice(t * T, (t + 1) * T)
    for k in range(8):
        if not squares_done:
            emit_square(sc, c, t, k)
    bank = c.stat_bank
    for k in range(8):
        sc.op("pe", lambda e, k=k: e.matmul(c.ps[:, bank, :], lhsT=c.ones[:], rhs=c.sq[:, k, :], start=(k == 0), stop=(k == 7)),
              reads=[c.r_sq[k], c.r_const], writes=[c.r_ps[bank]], inc=(k == 7))
    sc.op("act", lambda e: e.activation(out=c.ms[:], in_=c.ps[:, bank, :], func=AF.Sqrt, scale=1.0 / D, bias=EPS),
          reads=[c.r_ps[bank]], writes=[c.r_ms])
    sc.op("dve", lambda e: e.reciprocal(out=c.rstd[:], in_=c.ms[:]),
          reads=[c.r_ms], writes=[c.r_rstd])
    for k in range(8):
        sc.op("dve", lambda e, k=k: e.scalar_tensor_tensor(out=c.hT[:, k, :], in0=c.xT[:, k, ts], scalar=gcol[:, k:k + 1], in1=c.rstd[:],
                                                          op0=ALU.mult, op1=ALU.mult),
              reads=[c.r_x[k][t], c.r_rstd, c.r_const], writes=[c.r_h[k]])


class WRing:
    def __init__(self, sc, tens, n):
        self.sc = sc
        self.t = tens
        self.n = n
        self.res = [Res() for _ in range(n)]
        self.i = 0

    def load(self, parts, src_res):
        b = self.i % self.n
        self.i += 1
        slab = self.t[:, b]
        r = self.res[b]
        for dst_fn, src in parts:
            self.sc.dma("sp", lambda e, dst_fn=dst_fn, src=src, slab=slab: e.dma_start(out=dst_fn(slab), in_=src),
                        reads=[src_res], writes=[r])
        return slab, r


def wview(w):
    return w.rearrange("(k p) n -> p k n", p=128)


def emit_ffn(sc, c, t, wr, wgu, wdn, r_wgu, r_wdn, post_add=None):
    ts = slice(t * T, (t + 1) * T)
    wguv = wview(wgu)
    wdnv = wview(wdn)
    for j2 in range(NJ // 2):
        slab, rs = wr.load([(lambda s: s[:, :, 0:256], wguv[:, :, j2 * 256:(j2 + 1) * 256]),
                            (lambda s: s[:, :, 256:512], wguv[:, :, DFF + j2 * 256:DFF + (j2 + 1) * 256])], r_wgu)
        for jj in range(2):
            j = 2 * j2 + jj
            bg = 2 * (j % 2)
            bu = bg + 1
            for k in range(8):
                sc.op("pe", lambda e, k=k, jj=jj, bg=bg, slab=slab: e.matmul(c.ps[:, bg, :], lhsT=slab[:, k, jj * 128:(jj + 1) * 128], rhs=c.hT[:, k, :],
                                                                              start=(k == 0), stop=(k == 7)),
                      reads=[rs, c.r_h[k]], writes=[c.r_ps[bg]], inc=(k == 7))
            for k in range(8):
                sc.op("pe", lambda e, k=k, jj=jj, bu=bu, slab=slab: e.matmul(c.ps[:, bu, :], lhsT=slab[:, k, 256 + jj * 128:256 + (jj + 1) * 128], rhs=c.hT[:, k, :],
                                                                              start=(k == 0), stop=(k == 7)),
                      reads=[rs, c.r_h[k]], writes=[c.r_ps[bu]], inc=(k == 7))
            sb = j % 2
            sc.op("act", lambda e, bg=bg, sb=sb: e.activation(out=c.tmp[:, sb, :], in_=c.ps[:, bg, :], func=AF.Silu),
                  reads=[c.r_ps[bg]], writes=[c.r_tmp[sb]])
            sc.op("dve", lambda e, j=j, bu=bu, sb=sb: e.tensor_tensor(out=c.arena[:, j, :], in0=c.tmp[:, sb, :], in1=c.ps[:, bu, :], op=ALU.mult),
                  reads=[c.r_tmp[sb], c.r_ps[bu]], writes=[c.r_ar[j]])
    kgroups = [(0, 8), (8, 8), (16, 6)]
    for hf in range(2):
        for (k0, nk) in kgroups:
            slab, rs = wr.load([(lambda s, nk=nk: s[:, 0:nk, :], wdnv[:, k0:k0 + nk, hf * 512:(hf + 1) * 512])], r_wdn)
            for kk in range(nk):
                k = k0 + kk
                for n in range(4):
                    sc.op("pe", lambda e, kk=kk, k=k, n=n, slab=slab: e.matmul(c.ps[:, 4 + n, :], lhsT=slab[:, kk, n * 128:(n + 1) * 128], rhs=c.arena[:, k, :],
                                                                                start=(k == 0), stop=(k == NJ - 1)),
                          reads=[rs, c.r_ar[k]], writes=[c.r_ps[4 + n]], inc=(k == NJ - 1 or (kk == nk - 1 and n == 3)))
        for n in range(4):
            nn = hf * 4 + n
            sc.op("dve", lambda e, n=n, nn=nn: e.tensor_tensor(out=c.xT[:, nn, ts], in0=c.xT[:, nn, ts], in1=c.ps[:, 4 + n, :], op=ALU.add),
                  reads=[c.r_ps[4 + n], c.r_x[nn][t]], writes=[c.r_x[nn][t]])
            if post_add is not None:
                post_add(nn)


def emit_proj_fm(sc, c, wr, w, r_w, col0, ncols, evac, src=None, r_src=None, kouter=False):
    src = c.hT if src is None else src
    r_src = c.r_h if r_src is None else r_src
    wv = wview(w)
    nchunks = ncols // 128
    for s0 in range(0, nchunks, 4):
        slab, rs = wr.load([(lambda s: s, wv[:, :, col0 + s0 * 128:col0 + s0 * 128 + 512])], r_w)
        if s0 == 0 and kouter:
            banks = [(c.mm_rot + q) % 4 for q in range(4)]
            c.mm_rot += 4
            for k in range(8):
                for nn in range(4):
                    sc.op("pe", lambda e, k=k, nn=nn, bank=banks[nn], slab=slab: e.matmul(c.ps[:, bank, :], lhsT=slab[:, k, nn * 128:(nn + 1) * 128], rhs=src[:, k, :],
                                                                                         start=(k == 0), stop=(k == 7)),
                          reads=[rs, r_src[k]], writes=[c.r_ps[banks[nn]]], inc=(k == 7))
            for nn in range(4):
                evac(nn, banks[nn])
            continue
        for nn in range(4):
            n = s0 + nn
            bank = c.mm_rot % 4
            c.mm_rot += 1
            for k in range(8):
                sc.op("pe", lambda e, k=k, nn=nn, bank=bank, slab=slab: e.matmul(c.ps[:, bank, :], lhsT=slab[:, k, nn * 128:(nn + 1) * 128], rhs=src[:, k, :],
                                                                                  start=(k == 0), stop=(k == 7)),
                      reads=[rs, r_src[k]], writes=[c.r_ps[bank]], inc=(k == 7))
            evac(n, bank)


def emit_proj_tm(sc, c, wr, w, r_w, col0, evac):
    wv = wview(w)
    slabs = []
    for hf in range(2):
        slabs.append(wr.load([(lambda s: s, wv[:, :, col0 + hf * 512:col0 + (hf + 1) * 512])], r_w))
    for ch in range(4):
        b0 = 4 + 2 * (ch % 2)
        for hf in range(2):
            slab, rs = slabs[hf]
            for k in range(8):
                sc.op("pe", lambda e, k=k, ch=ch, hf=hf, b0=b0, slab=slab: e.matmul(c.ps[:, b0 + hf, :], lhsT=c.hT[:, k, ch * 128:(ch + 1) * 128], rhs=slab[:, k, :],
                                                                                     start=(k == 0), stop=(k == 7)),
                      reads=[rs, c.r_h[k]], writes=[c.r_ps[b0 + hf]], inc=(k == 7))
        evac(ch, b0)


def alloc_common(nc, es, c, sc, nwslab=3):
    c.xT = es.enter_context(nc.sbuf_tensor("s_xT", [128, 8, TOK], F32))
    c.hT = es.enter_context(nc.sbuf_tensor("s_hT", [128, 8, T], BF16))
    c.sq = es.enter_context(nc.sbuf_tensor("s_sq", [128, 8, T], BF16))
    c.arena = es.enter_context(nc.sbuf_tensor("s_arena", [128, NJ, T], BF16))
    c.ms = es.enter_context(nc.sbuf_tensor("s_ms", [128, T], F32))
    c.rstd = es.enter_context(nc.sbuf_tensor("s_rstd", [128, T], F32))
    c.tmp = es.enter_context(nc.sbuf_tensor("s_tmp", [128, 2, T], F32))
    c.nh1 = es.enter_context(nc.sbuf_tensor("s_nh1", [128, 1], F32))
    c.ones = es.enter_context(nc.sbuf_tensor("s_ones", [128, 128], BF16))
    c.wslab = es.enter_context(nc.sbuf_tensor("s_wslab", [128, nwslab, 8, 512], BF16))
    c.ps = es.enter_context(nc.psum_tensor("p_ps", [128, 8, 512], F32))
    c.r_x = [[Res() for _ in range(NT)] for _ in range(8)]
    c.r_h = [Res() for _ in range(8)]
    c.r_sq = [Res() for _ in range(8)]
    c.r_ar = [Res() for _ in range(NJ)]
    c.r_ms = Res()
    c.r_rstd = Res()
    c.r_tmp = [Res(), Res()]
    c.r_const = Res()
    c.r_ps = [Res() for _ in range(8)]
    c.stat_bank = 3
    c.mm_rot = 0
    sc.op("pool", lambda e: e.memset(c.nh1[:], -0.5), writes=[c.r_const])
    sc.op("pool", lambda e: e.memset(c.ones[:], 1.0), writes=[c.r_const])
    return WRing(sc, c.wslab, nwslab)


def cast_weight(sc, dst, src, r_dst, rows_per_dma=1024):
    K, N = src.shape
    b = max(d for d in range(1, 1025) if N % d == 0)
    a = N // b
    sv = src.rearrange("r (a b) -> (r a) b", b=b)
    dv = dst.rearrange("r (a b) -> (r a) b", b=b)
    rows = K * a
    for r0 in range(0, rows, rows_per_dma):
        r1 = min(rows, r0 + rows_per_dma)
        sc.dma("pool", lambda e, r0=r0, r1=r1: e.dma_start(out=dv[r0:r1, :], in_=sv[r0:r1, :]), writes=[r_dst])


def build_A():
    import contextlib
    nc = bass.Bass("TRN2", target_bir_lowering=False)
    sc = Sched()
    c = Ctx()
    din = lambda name, shape, dt=F32: nc.dram_tensor(name, shape, dt, kind="ExternalInput").ap()
    xT_d = din("xT", [D, TOK])
    vecs_d = din("vecs", [128, 4, 8])
    w_in_d = din("w_in", [D, 2 * D])
    w_spT_d = din("w_spT", [128, 8, 128])
    tril_d = din("trilT", [128, 128])
    bsp_d = din("bsp", [128, 8, 128])
    w_out_d = din("w_out", [D, D])
    w_gu_d = din("w_gu", [D, 2 * DFF])
    w_dn_d = din("w_dn", [DFF, D])
    w_kv_d = din("w_kv", [D, 2 * D])
    x1T_o = nc.dram_tensor("x1T", [D, TOK], F32, kind="ExternalOutput").ap()
    kT_o = nc.dram_tensor("kT", [D, TOK], BF16, kind="ExternalOutput").ap()
    v_o = nc.dram_tensor("v", [TOK, D], BF16, kind="ExternalOutput").ap()
    dint = lambda name, shape: nc.dram_tensor(name, shape, BF16, kind="Internal").ap()
    if STAGES["dbg"]:
        dbg_u = nc.dram_tensor("dbg_u", [128, 8, T], BF16, kind="ExternalOutput").ap()
        dbg_vn = nc.dram_tensor("dbg_vn", [128, 8, T], BF16, kind="ExternalOutput").ap()
        dbg_uz = nc.dram_tensor("dbg_uz", [128, 8, T], BF16, kind="ExternalOutput").ap()
        dbg_z = nc.dram_tensor("dbg_z", [128, 8, T], F32, kind="ExternalOutput").ap()
        dbg_vf = nc.dram_tensor("dbg_vf", [128, 4, 1024], F32, kind="ExternalOutput").ap()
        dbg_mv = nc.dram_tensor("dbg_mv", [128, 4, 4], F32, kind="ExternalOutput").ap()
    w_in_b = dint("w_in_b", [D, 2 * D])
    w_out_b = dint("w_out_b", [D, D])
    w_gu_b = dint("w_gu_b", [D, 2 * DFF])
    w_dn_b = dint("w_dn_b", [DFF, D])
    w_kv_b = dint("w_kv_b", [D, 2 * D])

    with contextlib.ExitStack() as es:
        wr = alloc_common(nc, es, c, sc, nwslab=3)
        vecs = es.enter_context(nc.sbuf_tensor("s_vecs", [128, 4, 8], F32))
        wspT_f = c.tmp[:].rearrange("p a (b c) -> p (a b) c", c=128)
        tril = es.enter_context(nc.sbuf_tensor("s_tril", [128, 128], F32))
        wcT = es.enter_context(nc.sbuf_tensor("s_wcT", [128, 8, 128], BF16))
        bsp = es.enter_context(nc.sbuf_tensor("s_bsp", [128, 8, 128], F32))
        vstat = es.enter_context(nc.sbuf_tensor("s_vstat", [128, 2, 6], F32))
        vmv = es.enter_context(nc.sbuf_tensor("s_vmv", [128, 4], F32))
        nh1 = c.nh1
        r_vf = Res()
        r_vstat = Res()
        r_vmv = Res()
        r_w = {n: Res() for n in ("in", "out", "gu", "dn", "kv")}

        rc = c.r_const
        r_c2, r_c3, r_c4 = Res(), Res(), Res()
        sc.dma("sp", lambda e: e.dma_start(out=vecs[:], in_=vecs_d[:, :, :]), writes=[rc])
        sc.dma("sp", lambda e: e.dma_start(out=wspT_f, in_=w_spT_d[:, :, :]), writes=[rc] + c.r_tmp)
        sc.dma("sp", lambda e: e.dma_start(out=tril[:], in_=tril_d[:, :]), writes=[r_c2])
        sc.dma("sp", lambda e: e.dma_start(out=bsp[:], in_=bsp_d[:, :, :]), writes=[r_c3])
        for g in range(8):
            sc.op("dve", lambda e, g=g: e.tensor_tensor(out=wcT[:, g, :], in0=wspT_f[:, g, :], in1=tril[:], op=ALU.mult),
                  reads=[r_c2] + c.r_tmp, writes=[r_c4])
        xTv = xT_d.rearrange("(k p) t -> p k t", p=128)
        for t in range(NT):
            sc.dma("sp", lambda e, t=t: e.dma_start(out=c.xT[:, :, t * T:(t + 1) * T], in_=xTv[:, :, t * T:(t + 1) * T]),
                   writes=[c.r_x[k][t] for k in range(8)])
        cast_weight(sc, w_in_b, w_in_d, r_w["in"])
        cast_weight(sc, w_out_b, w_out_d, r_w["out"])
        cast_weight(sc, w_gu_b, w_gu_d, r_w["gu"])
        cast_weight(sc, w_dn_b, w_dn_d, r_w["dn"])
        cast_weight(sc, w_kv_b, w_kv_d, r_w["kv"])

        x1Tv = x1T_o.rearrange("(k p) t -> p k t", p=128)
        kTv = kT_o.rearrange("(k p) t -> p k t", p=128)
        vov = v_o.rearrange("(c p) f -> p c f", p=128)
        vf = c.arena[:, 16:20, :].rearrange("p a b -> p (a b)").bitcast(F32)
        r_vfslots = c.r_ar[16:20]

        def emit_gmlp(t, ts):
            emit_norm(sc, c, t, vecs[:, 0, :])

            def evac_u(n, bank):
                sc.op("act", lambda e, n=n, bank=bank: e.activation(out=c.arena[:, n, :], in_=c.ps[:, bank, :], func=AF.Gelu_apprx_tanh),
                      reads=[c.r_ps[bank]], writes=[c.r_ar[n]])
            emit_proj_fm(sc, c, wr, w_in_b, r_w["in"], 0, D, evac_u)
            if STAGES["dbg"] and t == 0:
                sc.dma("pool", lambda e: e.dma_start(out=dbg_u[:, :, :], in_=c.arena[:, 0:8, :]), reads=c.r_ar[0:8])

            def evac_v(ch, b0):
                for hf in range(2):
                    sc.op("act", lambda e, hf=hf, b0=b0: e.activation(out=vf[:, hf * 512:(hf + 1) * 512], in_=c.ps[:, b0 + hf, :], func=AF.Gelu_apprx_tanh),
                          reads=[c.r_ps[b0 + hf]], writes=r_vfslots[2 * hf:2 * hf + 2])
                for hf in range(2):
                    sc.op("dve", lambda e, hf=hf: e.bn_stats(out=vstat[:, hf, :], in_=vf[:, hf * 512:(hf + 1) * 512]),
                          reads=r_vfslots[2 * hf:2 * hf + 2], writes=[r_vstat])
                sc.op("dve", lambda e: e.bn_aggr(out=vmv[:, 0:2], in_=vstat[:].rearrange("p a b -> p (a b)")), reads=[r_vstat], writes=[r_vmv])
                sc.op("dve", lambda e: e.tensor_scalar(out=vmv[:, 2:3], in0=vmv[:, 1:2], scalar1=EPS, scalar2=None, op0=ALU.add),
                      reads=[r_vmv], writes=[r_vmv])
                sc.op("pool", lambda e: e.tensor_tensor(out=vmv[:, 3:4], in0=vmv[:, 2:3], in1=nh1[:], op=ALU.pow),
                      reads=[r_vmv, rc], writes=[r_vmv])
                if STAGES["dbg"] and t == 0:
                    sc.dma("pool", lambda e, ch=ch: e.dma_start(out=dbg_vf[:, ch, :], in_=vf), reads=r_vfslots)
                    sc.dma("pool", lambda e, ch=ch: e.dma_start(out=dbg_mv[:, ch, :], in_=vmv[:]), reads=[r_vmv])
                vn = c.arena[:, 8 + 2 * ch:10 + 2 * ch, :].rearrange("p a b -> p (a b)")
                sc.op("dve", lambda e, vn=vn: e.tensor_scalar(out=vn, in0=vf, scalar1=vmv[:, 0:1], scalar2=vmv[:, 3:4], op0=ALU.subtract, op1=ALU.mult),
                      reads=[r_vmv] + r_vfslots, writes=c.r_ar[8 + 2 * ch:10 + 2 * ch])
            emit_proj_tm(sc, c, wr, w_in_b, r_w["in"], D, evac_v)

            if STAGES["dbg"] and t == 0:
                sc.dma("pool", lambda e: e.dma_start(out=dbg_vn[:, :, :], in_=c.arena[:, 8:16, :]), reads=c.r_ar[8:16])
            for g in range(8):
                bank = c.mm_rot % 4
                c.mm_rot += 1
                for ch in range(4):
                    vn = c.arena[:, 8 + 2 * ch:10 + 2 * ch, :].rearrange("p a b -> p (a b)")
                    sc.op("pe", lambda e, g=g, ch=ch, bank=bank, vn=vn: e.matmul(c.ps[:, bank, ch * 128:(ch + 1) * 128], lhsT=vn[:, g * 128:(g + 1) * 128], rhs=wcT[:, g, :],
                                                                                  start=True, stop=True),
                          reads=c.r_ar[8 + 2 * ch:10 + 2 * ch] + [r_c4], writes=[c.r_ps[bank]], inc=(ch == 3))
                sb = g % 2
                sc.op("dve", lambda e, g=g, bank=bank, sb=sb: e.scalar_tensor_tensor(
                    out=c.tmp[:, sb, :].rearrange("p (a b) -> p a b", a=4), in0=c.ps[:, bank, :].rearrange("p (a b) -> p a b", a=4),
                    scalar=vecs[:, 1, g:g + 1], in1=bsp[:, g, :].unsqueeze(1).broadcast_to([128, 4, 128]), op0=ALU.mult, op1=ALU.add),
                    reads=[c.r_ps[bank], rc, r_c3], writes=[c.r_tmp[sb]])
                if STAGES["dbg"] and t == 0:
                    sc.dma("pool", lambda e, g=g, sb=sb: e.dma_start(out=dbg_z[:, g, :], in_=c.tmp[:, sb, :]), reads=[c.r_tmp[sb]])
                sc.op("dve", lambda e, g=g, sb=sb: e.tensor_tensor(out=c.arena[:, g, :], in0=c.tmp[:, sb, :], in1=c.arena[:, g, :], op=ALU.mult),
                      reads=[c.r_tmp[sb], c.r_ar[g]], writes=[c.r_ar[g]])

            if STAGES["dbg"] and t == 0:
                sc.dma("pool", lambda e: e.dma_start(out=dbg_uz[:, :, :], in_=c.arena[:, 0:8, :]), reads=c.r_ar[0:8])

            def evac_res(n, bank):
                sc.op("dve", lambda e, n=n, bank=bank: e.tensor_tensor(out=c.xT[:, n, ts], in0=c.xT[:, n, ts], in1=c.ps[:, bank, :], op=ALU.add),
                      reads=[c.r_ps[bank], c.r_x[n][t]], writes=[c.r_x[n][t]])
            emit_proj_fm(sc, c, wr, w_out_b, r_w["out"], 0, D, evac_res, src=c.arena, r_src=c.r_ar)

        def emit_tail(t, ts):
            sc.dma("pool", lambda e, ts=ts: e.dma_start(out=x1Tv[:, :, ts], in_=c.xT[:, :, ts]), reads=[c.r_x[k][t] for k in range(8)])

            emit_norm(sc, c, t, vecs[:, 3, :])

            def evac_k(n, bank):
                sc.op("act", lambda e, n=n, bank=bank: e.copy(out=c.arena[:, n, :], in_=c.ps[:, bank, :]),
                      reads=[c.r_ps[bank]], writes=[c.r_ar[n]])
            emit_proj_fm(sc, c, wr, w_kv_b, r_w["kv"], 0, D, evac_k)
            sc.dma("pool", lambda e, ts=ts: e.dma_start(out=kTv[:, :, ts], in_=c.arena[:, 0:8, :]), reads=c.r_ar[0:8])

            def evac_vv(ch, b0):
                vst = c.arena[:, 8 + 2 * ch:10 + 2 * ch, :].rearrange("p a b -> p (a b)")
                sc.op("act", lambda e, vst=vst, b0=b0: e.copy(out=vst[:, 0:512], in_=c.ps[:, b0, :]),
                      reads=[c.r_ps[b0]], writes=[c.r_ar[8 + 2 * ch]])
                sc.op("dve", lambda e, vst=vst, b0=b0: e.tensor_copy(out=vst[:, 512:1024], in_=c.ps[:, b0 + 1, :]),
                      reads=[c.r_ps[b0 + 1]], writes=[c.r_ar[9 + 2 * ch]])
            emit_proj_tm(sc, c, wr, w_kv_b, r_w["kv"], D, evac_vv)
            sc.dma("pool", lambda e, t=t: e.dma_start(out=vov[:, 4 * t:4 * t + 4, :],
                                                      in_=c.arena[:, 8:16, :].rearrange("p (c a) b -> p c (a b)", a=2)),
                   reads=c.r_ar[8:16])

        for t in range(1 if STAGES["dbg"] else NT):
            ts = slice(t * T, (t + 1) * T)
            if STAGES["gmlp"]:
                emit_gmlp(t, ts)
            if STAGES["ffn"]:
                emit_norm(sc, c, t, vecs[:, 2, :])
                emit_ffn(sc, c, t, wr, w_gu_b, w_dn_b, r_w["gu"], r_w["dn"])
            emit_tail(t, ts)

        sc.finish("sp")
        sc.replay(nc)
    return nc, sc


def prep_A(inputs, core):
    b, role = core // 2, core % 2
    f = np.float32
    x = np.asarray(inputs["x"])
    toks = np.concatenate([np.arange(tile_index(role, j) * T, (tile_index(role, j) + 1) * T) for j in range(NT)])
    xT = np.ascontiguousarray(x[b][toks].T)
    col = lambda v: np.ascontiguousarray(np.asarray(v, f).reshape(8, 128).T)
    vecs = np.stack([col(inputs["a_norm"][0]), col(inputs["a_v_norm"][0]), col(inputs["ffn_norm"][0]), col(inputs["kv_norm"])], axis=1)
    w_spT = np.ascontiguousarray(np.transpose(np.asarray(inputs["a_w_sp"][0], f), (2, 0, 1)))
    trilT = np.triu(np.ones((128, 128), f))
    bsp = np.ascontiguousarray(np.broadcast_to(np.asarray(inputs["a_b_sp"][0], f)[None], (128, 8, 128)))
    return {
        "xT": xT, "vecs": np.ascontiguousarray(vecs), "w_in": np.asarray(inputs["a_w_in"][0], f),
        "w_spT": w_spT, "trilT": trilT, "bsp": bsp, "w_out": np.asarray(inputs["a_w_out"][0], f),
        "w_gu": np.asarray(inputs["ffn_w_gu"][0], f), "w_dn": np.asarray(inputs["ffn_w_down"][0], f),
        "w_kv": np.asarray(inputs["kv_w"], f),
    }


_CACHE = {}


def run_A(inputs):
    if "A" not in _CACHE:
        _CACHE["A"] = build_A()[0]
    nc = _CACHE["A"]
    in_maps = [prep_A(inputs, cidx) for cidx in range(8)]
    res = run_bass_kernel_spmd(nc, in_maps, core_ids=list(range(8)))
    return res.results


def build_B():
    import contextlib
    nc = bass.Bass("TRN2", target_bir_lowering=False)
    sc = Sched()
    c = Ctx()
    din = lambda name, shape, dt=F32: nc.dram_tensor(name, shape, dt, kind="ExternalInput").ap()
    x1T_d = din("x1T", [D, TOK])
    vecs_d = din("vecs", [128, 3, 8])
    w_q_d = din("w_q", [D, D])
    w_o_d = din("w_o", [D, D])
    w_gu_d = din("w_gu", [D, 2 * DFF])
    w_dn_d = din("w_dn", [DFF, D])
    kd_d = [din("kd_e", [NH, KROWS, 2, S], BF16), din("kd_o", [NH, KROWS, 2, S], BF16)]
    vd_d = [din("vd_e", [NH, 128, 64, 128], BF16), din("vd_o", [NH, 128, 64, 128], BF16)]
    qaug_d = din("qaug", [NT, 4, NH, T], BF16)
    masks_d = din("masks", [128, 4, T], BF16)
    ident_d = din("ident", [128, 128], BF16)
    lamb_d = din("lamb", [128, 256])
    subln_d = din("subln", [128, 1])
    outT_o = nc.dram_tensor("outT", [D, TOK], F32, kind="ExternalOutput").ap()
    if STAGES["dbgB"]:
        dbg_on = nc.dram_tensor("dbg_on", [128, 8, T], BF16, kind="ExternalOutput").ap()
        dbg_q = nc.dram_tensor("dbg_q", [128, 16, T], BF16, kind="ExternalOutput").ap()
        dbg_sm = nc.dram_tensor("dbg_sm", [128, 8], F32, kind="ExternalOutput").ap()
    dint = lambda name, shape: nc.dram_tensor(name, shape, BF16, kind="Internal").ap()
    w_q_b = dint("w_q_b", [D, D])
    w_o_b = dint("w_o_b", [D, D])
    w_gu_b = dint("w_gu_b", [D, 2 * DFF])
    w_dn_b = dint("w_dn_b", [DFF, D])

    with contextlib.ExitStack() as es:
        wr = alloc_common(nc, es, c, sc, nwslab=2)
        vecs = es.enter_context(nc.sbuf_tensor("s_vecs", [128, 3, 8], F32))
        kring = es.enter_context(nc.sbuf_tensor("s_kring", [KROWS, 2, 2, 1024], BF16))
        vring = es.enter_context(nc.sbuf_tensor("s_vring", [128, 2, 8, 128], BF16))
        masks = es.enter_context(nc.sbuf_tensor("s_masks", [128, 4, T], BF16))
        ident = es.enter_context(nc.sbuf_tensor("s_ident", [128, 128], BF16))
        lamb = es.enter_context(nc.sbuf_tensor("s_lamb", [128, 256], F32))
        sm = es.enter_context(nc.sbuf_tensor("s_sm", [128, 8], F32))
        r_kv = [Res(), Res()]
        r_qaug = Res()
        r_pt = c.r_ar[16:20]
        r_sm = Res()
        r_w = {n: Res() for n in ("q", "o", "gu", "dn")}
        rc = c.r_const
        r_c2, r_c3 = Res(), Res()
        Qt = c.arena[:, 0:16, :].rearrange("p (h i) t -> p h i t", i=2)
        onT, r_on = c.sq, c.r_sq

        sc.dma("sp", lambda e: e.dma_start(out=vecs[:], in_=vecs_d[:, :, :]), writes=[rc])
        sc.dma("sp", lambda e: e.dma_start(out=masks[:], in_=masks_d[:, :, :]), writes=[r_c2])
        r_c5 = Res()
        sc.dma("sp", lambda e: e.dma_start(out=ident[:], in_=ident_d[:, :]), writes=[r_c5])
        sc.dma("sp", lambda e: e.dma_start(out=lamb[:], in_=lamb_d[:, :]), writes=[r_c3])
        r_c4 = Res()
        sc.dma("sp", lambda e: e.dma_start(out=sm[:, 7:8], in_=subln_d[:, :]), writes=[r_c4])
        x1Tv = x1T_d.rearrange("(k p) t -> p k t", p=128)
        for t in range(NT):
            sc.dma("sp", lambda e, t=t: e.dma_start(out=c.xT[:, :, t * T:(t + 1) * T], in_=x1Tv[:, :, t * T:(t + 1) * T]),
                   writes=[c.r_x[k][t] for k in range(8)])
        cast_weight(sc, w_q_b, w_q_d, r_w["q"])
        cast_weight(sc, w_o_b, w_o_d, r_w["o"])
        cast_weight(sc, w_gu_b, w_gu_d, r_w["gu"])
        cast_weight(sc, w_dn_b, w_dn_d, r_w["dn"])
        scr = c.tmp[:, 0, 0:64]
        sc.op("dve", lambda e: e.scalar_tensor_tensor(out=scr, in0=lamb[:, 0:64], scalar=1.0, in1=lamb[:, 64:128], op0=ALU.mult, op1=ALU.mult, accum_out=sm[:, 0:1]),
              reads=[r_c3], writes=[r_sm, c.r_tmp[0]])
        sc.op("dve", lambda e: e.scalar_tensor_tensor(out=scr, in0=lamb[:, 128:192], scalar=1.0, in1=lamb[:, 192:256], op0=ALU.mult, op1=ALU.mult, accum_out=sm[:, 1:2]),
              reads=[r_c3, r_sm], writes=[r_sm, c.r_tmp[0]])
        sc.op("act", lambda e: e.activation(out=sm[:, 2:4], in_=sm[:, 0:2], func=AF.Exp), reads=[r_sm], writes=[r_sm])
        sc.op("dve", lambda e: e.tensor_tensor(out=sm[:, 4:5], in0=sm[:, 3:4], in1=sm[:, 2:3], op=ALU.subtract), reads=[r_sm], writes=[r_sm])
        sc.op("dve", lambda e: e.tensor_scalar(out=sm[:, 4:5], in0=sm[:, 4:5], scalar1=-LAMBDA_INIT, scalar2=None, op0=ALU.add), reads=[r_sm], writes=[r_sm])
        sc.op("dve", lambda e: e.tensor_scalar(out=sm[:, 5:6], in0=sm[:, 7:8], scalar1=1.0 - LAMBDA_INIT, scalar2=None, op0=ALU.mult), reads=[r_sm, r_c4], writes=[r_sm])

        outTv = outT_o.rearrange("(k p) t -> p k t", p=128)
        strot = [0]
        ptrot = [0]
        kvn = [0]

        def load_kv(var, h, cp):
            b = kvn[0] % 2
            kvn[0] += 1
            r = r_kv[b]
            sc.dma("sp", lambda e, b=b: e.dma_start(out=kring[:, b], in_=kd_d[var][h, :, :, cp * 1024:(cp + 1) * 1024]), writes=[r])
            sc.dma("sp", lambda e, b=b: e.dma_start(out=vring[:, b], in_=vd_d[var][h, :, cp * 8:(cp + 1) * 8, :]), writes=[r])
            return b, r

        def emit_attention(j, ts):
            var = j % 2
            chunks = [(h, ci) for h in range(NH) for ci in range(j + 1)]
            loaded = {}

            def ensure(idx):
                if idx < len(chunks) and idx not in loaded:
                    h, ci = chunks[idx]
                    loaded[idx] = load_kv(var, h, j - ci)
            steps = []
            for idx, (h, ci) in enumerate(chunks):
                for o in range(8):
                    for i in range(2):
                        steps.append((idx, h, ci, o, i))
            nsteps_h = (j + 1) * 16
            pend = []
            LAG = 2

            def emit_av(item):
                (idx, h, ci, o, i, pt, first, last) = item
                b, rkv = loaded[idx]
                sc.op("pe", lambda e, b=b, o=o, i=i, pt=pt, first=first, last=last: e.matmul(c.ps[:, 4 + i, :], lhsT=vring[:, b, o, :], rhs=c.arena[:, 16 + pt, :], start=first, stop=last),
                      reads=[rkv, r_pt[pt]], writes=[c.r_ps[4 + i]], inc=False)
                sc.op("pe", lambda e, i=i, pt=pt, first=first, last=last: e.matmul(c.ps[:, 6 + i, :], lhsT=c.ones[:], rhs=c.arena[:, 16 + pt, :], start=first, stop=last),
                      reads=[r_pt[pt], rc], writes=[c.r_ps[6 + i]], inc=True)
                if last and i == 1:
                    emit_head_post(h)

            def emit_head_post(h):
                a_, b_ = c.tmp[:, 0, :], c.tmp[:, 1, :]
                sc.op("dve", lambda e: e.reciprocal(out=c.ms[:], in_=c.ps[:, 6, :]), reads=[c.r_ps[6]], writes=[c.r_ms])
                sc.op("dve", lambda e: e.tensor_tensor(out=a_, in0=c.ps[:, 4, :], in1=c.ms[:], op=ALU.mult), reads=[c.r_ps[4], c.r_ms], writes=[c.r_tmp[0]])
                sc.op("dve", lambda e: e.reciprocal(out=c.rstd[:], in_=c.ps[:, 7, :]), reads=[c.r_ps[7]], writes=[c.r_rstd])
                sc.op("dve", lambda e: e.tensor_tensor(out=b_, in0=c.ps[:, 5, :], in1=c.rstd[:], op=ALU.mult), reads=[c.r_ps[5], c.r_rstd], writes=[c.r_tmp[1]])
                sc.op("dve", lambda e: e.scalar_tensor_tensor(out=a_, in0=b_, scalar=sm[:, 4:5], in1=a_, op0=ALU.mult, op1=ALU.add),
                      reads=[c.r_tmp[1], c.r_tmp[0], r_sm], writes=[c.r_tmp[0]])
                sc.op("dve", lambda e: e.tensor_tensor(out=c.arena[:, 20, :], in0=a_, in1=a_, op=ALU.mult), reads=[c.r_tmp[0]], writes=[c.r_ar[20]])
                bank = strot[0] % 4
                strot[0] += 1
                sc.op("pe", lambda e, bank=bank: e.matmul(c.ps[:, bank, :], lhsT=c.ones[:], rhs=c.arena[:, 20, :], start=True, stop=True),
                      reads=[c.r_ar[20], rc], writes=[c.r_ps[bank]])
                sc.op("act", lambda e, bank=bank: e.activation(out=c.ms[:], in_=c.ps[:, bank, :], func=AF.Sqrt, scale=1.0 / 128, bias=EPS),
                      reads=[c.r_ps[bank]], writes=[c.r_ms])
                sc.op("dve", lambda e: e.reciprocal(out=c.rstd[:], in_=c.ms[:]),
                      reads=[c.r_ms], writes=[c.r_rstd])
                sc.op("dve", lambda e, h=h: e.scalar_tensor_tensor(out=onT[:, h, :], in0=a_, scalar=sm[:, 5:6], in1=c.rstd[:], op0=ALU.mult, op1=ALU.mult),
                      reads=[c.r_tmp[0], c.r_rstd, r_sm], writes=[r_on[h]])

            for sidx, (idx, h, ci, o, i) in enumerate(steps):
                if o == 0 and i == 0:
                    ensure(idx)
                if o == 1 and i == 0:
                    ensure(idx + 1)
                b, rkv = loaded[idx]
                bank = strot[0] % 4
                strot[0] += 1
                pt = ptrot[0] % 4
                ptrot[0] += 1
                diag = (ci == 0 and o >= 4)
                sc.op("pe", lambda e, b=b, h=h, o=o, i=i, bank=bank, diag=diag: e.matmul(c.ps[:, bank, :], lhsT=kring[0:68, b, i, o * 128:(o + 1) * 128], rhs=Qt[0:68, h, i, :], start=True, stop=(not diag)),
                      reads=[rkv, c.r_ar[2 * h + i], r_qaug], writes=[c.r_ps[bank]], inc=(not diag))
                if diag:
                    dd = o - 4
                    sc.op("pe", lambda e, bank=bank, dd=dd: e.matmul(c.ps[:, bank, :], lhsT=ident[:], rhs=masks[:, dd, :], start=False, stop=True),
                          reads=[r_c2, r_c5], writes=[c.r_ps[bank]])
                sc.op("act", lambda e, bank=bank, pt=pt: e.activation(out=c.arena[:, 16 + pt, :], in_=c.ps[:, bank, :], func=AF.Exp, scale=0.125),
                      reads=[c.r_ps[bank]], writes=[r_pt[pt]])
                hs = sidx - h * nsteps_h
                first = hs < 2
                last = hs >= nsteps_h - 2
                pend.append((idx, h, ci, o, i, pt, first, last))
                if len(pend) > LAG:
                    emit_av(pend.pop(0))
            while pend:
                emit_av(pend.pop(0))

        def emit_tile(t):
            ts = slice(t * T, (t + 1) * T)
            emit_norm(sc, c, t, vecs[:, 0, :])
            for i in range(2):
                sc.dma("sp", lambda e, t=t, i=i: e.dma_start(out=Qt[64:68, :, i, :], in_=qaug_d[t, :, :, :]), writes=[r_qaug] + c.r_ar[0:16])

            def evac_q(h, bank):
                sc.op("act", lambda e, h=h, bank=bank: e.copy(out=Qt[0:64, h, 0, :], in_=c.ps[0:64, bank, :]), reads=[c.r_ps[bank]], writes=[c.r_ar[2 * h]])
                sc.op("act", lambda e, h=h, bank=bank: e.copy(out=Qt[0:64, h, 1, :], in_=c.ps[64:128, bank, :]), reads=[c.r_ps[bank]], writes=[c.r_ar[2 * h + 1]])
            emit_proj_fm(sc, c, wr, w_q_b, r_w["q"], 0, D, evac_q)
            if STAGES["dbgB"] and t == 0:
                sc.dma("pool", lambda e: e.dma_start(out=dbg_q[:, :, :], in_=c.arena[:, 0:16, :]), reads=c.r_ar[0:16] + [r_qaug])
                sc.dma("pool", lambda e: e.dma_start(out=dbg_sm[:, :], in_=sm[:]), reads=[r_sm])
            if STAGES["attn"]:
                emit_attention(t, ts)
            if STAGES["dbgB"] and t == 0:
                sc.dma("pool", lambda e: e.dma_start(out=dbg_on[:, :, :], in_=onT[:]), reads=r_on)

            def evac_res(n, bank):
                sc.op("dve", lambda e, n=n, bank=bank: e.tensor_tensor(out=c.xT[:, n, ts], in0=c.xT[:, n, ts], in1=c.ps[:, bank, :], op=ALU.add),
                      reads=[c.r_ps[bank], c.r_x[n][t]], writes=[c.r_x[n][t]])
            if STAGES["attn"]:
                emit_proj_fm(sc, c, wr, w_o_b, r_w["o"], 0, D, evac_res, src=onT, r_src=r_on)
            if STAGES["ffnB"]:
                emit_norm(sc, c, t, vecs[:, 1, :])
                emit_ffn(sc, c, t, wr, w_gu_b, w_dn_b, r_w["gu"], r_w["dn"])
            for k in range(8):
                sc.op("dve", lambda e, k=k: e.tensor_tensor(out=c.sq[:, k, :], in0=c.xT[:, k, ts], in1=c.xT[:, k, ts], op=ALU.mult),
                      reads=[c.r_x[k][t]], writes=[c.r_sq[k]])
            bank = c.stat_bank
            for k in range(8):
                sc.op("pe", lambda e, k=k: e.matmul(c.ps[:, bank, :], lhsT=c.ones[:], rhs=c.sq[:, k, :], start=(k == 0), stop=(k == 7)),
                      reads=[c.r_sq[k], rc], writes=[c.r_ps[bank]], inc=(k == 7))
            sc.op("act", lambda e: e.activation(out=c.ms[:], in_=c.ps[:, bank, :], func=AF.Sqrt, scale=1.0 / D, bias=EPS),
                  reads=[c.r_ps[bank]], writes=[c.r_ms])
            sc.op("dve", lambda e: e.reciprocal(out=c.rstd[:], in_=c.ms[:]),
                  reads=[c.r_ms], writes=[c.r_rstd])
            for k in range(8):
                sb = k % 2
                sc.op("dve", lambda e, k=k, sb=sb: e.scalar_tensor_tensor(out=c.tmp[:, sb, :], in0=c.xT[:, k, ts], scalar=vecs[:, 2, k:k + 1], in1=c.rstd[:],
                                                                       op0=ALU.mult, op1=ALU.mult),
                      reads=[c.r_x[k][t], c.r_rstd, rc], writes=[c.r_tmp[sb]])
                sc.dma("pool", lambda e, k=k, sb=sb, ts=ts: e.dma_start(out=outTv[:, k, ts], in_=c.tmp[:, sb, :]), reads=[c.r_tmp[sb]])

        for t in range(1 if STAGES["dbgB"] else NT):
            emit_tile(t)

        sc.finish("sp")
        sc.replay(nc)
    return nc, sc


def alibi_slopes():
    return np.array([2.0 ** (-8.0 * (i + 1) / NH) for i in range(NH)], dtype=np.float64)


def prep_B(inputs, core, x1T, KT_full, V_full):
    b, role = core // 2, core % 2
    f = np.float32
    col = lambda v: np.ascontiguousarray(np.asarray(v, f).reshape(8, 128).T)
    vecs = np.stack([col(inputs["b_norm"][0]), col(inputs["ffn_norm"][1]), col(inputs["final_norm"])], axis=1)
    slopes = alibi_slopes()
    pos = np.arange(S)
    khi, klo = pos // 128, pos % 128
    out = {}
    K4 = KT_full.reshape(NH, 2, 64, S)
    V4 = V_full.reshape(64, 128, NH, 128)
    for par, name in ((0, "e"), (1, "o")):
        shifted = (tile_index(role, par) != 2 * par + 1)
        kd = np.zeros((NH, KROWS, 2, S), NPBF)
        vd = np.zeros((NH, 128, 64, 128), NPBF)
        aug = np.zeros((NH, 4, S), np.float64)
        aug[:, 0, :] = 1.0
        aug[:, 1, :] = 1.0
        if not shifted:
            kd[:, 0:64, :, :] = K4.transpose(0, 2, 1, 3)
            vd[:] = V4.transpose(2, 1, 0, 3)
            aug[:, 2, :] = slopes[:, None] * 128.0 * khi[None, :]
            aug[:, 3, :] = slopes[:, None] * klo[None, :]
        else:
            kd[:, 0:64, :, 512:] = K4.transpose(0, 2, 1, 3)[:, :, :, :S - 512]
            vd[:, :, 4:, :] = V4.transpose(2, 1, 0, 3)[:, :, :60, :]
            aug[:, 2, 512:] = slopes[:, None] * 128.0 * khi[None, :S - 512]
            aug[:, 3, 512:] = slopes[:, None] * klo[None, :S - 512]
            aug[:, 2, :512] = slopes[:, None] * 128.0 * (-200.0)
        kd[:, 64:68, 0, :] = aug.astype(NPBF)
        kd[:, 64:68, 1, :] = aug.astype(NPBF)
        out["kd_" + name] = kd
        out["vd_" + name] = vd
    qaug = np.zeros((NT, 4, NH, T), np.float64)
    for j in range(NT):
        qpos = tile_index(role, j) * T + np.arange(T)
        qhi, qlo = qpos // 128, qpos % 128
        qaug[j, 0] = -8.0 * slopes[:, None] * 128.0 * qhi[None, :]
        qaug[j, 1] = -8.0 * slopes[:, None] * qlo[None, :]
        qaug[j, 2] = 8.0
        qaug[j, 3] = 8.0
    kk = np.arange(128)[:, None, None]
    dd = np.arange(4)[None, :, None]
    qq = np.arange(T)[None, None, :]
    masks = np.where(qq - 128 * dd - kk >= 0, 0.0, -240000.0).astype(NPBF)
    out.update({
        "x1T": x1T, "vecs": np.ascontiguousarray(vecs), "w_q": np.asarray(inputs["b_w_q"][0], f), "w_o": np.asarray(inputs["b_w_o"][0], f),
        "w_gu": np.asarray(inputs["ffn_w_gu"][1], f), "w_dn": np.asarray(inputs["ffn_w_down"][1], f),
        "qaug": qaug.astype(NPBF), "masks": masks, "ident": np.eye(128, dtype=np.float32).astype(NPBF),
        "lamb": np.ascontiguousarray(np.broadcast_to(np.asarray(inputs["b_lambda"][0], f).reshape(1, 256), (128, 256))),
        "subln": np.ascontiguousarray(np.asarray(inputs["b_subln"][0], f).reshape(128, 1)),
    })
    return out


def run_B(inputs, resA):
    if "B" not in _CACHE:
        _CACHE["B"] = build_B()[0]
    nc = _CACHE["B"]
    in_maps = []
    for b in range(B):
        KT_full = np.zeros((D, S), NPBF)
        V_full = np.zeros((S, D), NPBF)
        for role in range(2):
            r = resA[2 * b + role]
            for j in range(NT):
                i = tile_index(role, j)
                KT_full[:, i * T:(i + 1) * T] = r["kT"][:, j * T:(j + 1) * T]
                V_full[i * T:(i + 1) * T, :] = r["v"][j * T:(j + 1) * T, :]
        for role in range(2):
            in_maps.append(prep_B(inputs, 2 * b + role, np.asarray(resA[2 * b + role]["x1T"]), KT_full, V_full))
    res = run_bass_kernel_spmd(nc, in_maps, core_ids=list(range(8)))
    return res.results


def kernel_unfused(**inputs):
    resA = run_A(inputs)
    resB = run_B(inputs, resA)
    out = np.zeros((B, S, D), np.float32)
    for core in range(8):
        b, role = core // 2, core % 2
        oT = np.asarray(resB[core]["outT"])
        for j in range(NT):
            i = tile_index(role, j)
            out[b, i * T:(i + 1) * T, :] = oT[:, j * T:(j + 1) * T].T
    return out


def sched_coll(sc, fn, reads=(), writes=()):
    toks = sc._deps(reads, writes)
    idx = sc.dn.setdefault("coll", 0) % 4
    sc.dn["coll"] += 1
    key = ("c", "pool", idx)
    cnt = sc.dcnt.get(key, 0)
    if cnt > 0:
        toks.append((key, cnt))
    sc._wait("pool", toks)
    sc.dcnt[key] = cnt + 1
    tok = (key, cnt + 1)
    sc.q["pool"].append(("op", fn, tok))
    sc._mark(tok, reads, writes)
    sc.n_inst += 1
    return tok


def build_F():
    import contextlib
    nc = bass.Bass("TRN2", target_bir_lowering=False)
    sc = Sched()
    c = Ctx()
    din = lambda name, shape, dt=F32: nc.dram_tensor(name, shape, dt, kind="ExternalInput").ap()
    xT_d = din("xT", [D, TOK])
    vecs_d = din("vecs", [128, 7, 8])
    w_in_d = din("w_in", [D, 2 * D])
    w_spT_d = din("w_spT", [128, 8, 128])
    tril_d = din("trilT", [128, 128])
    bsp_d = din("bsp", [128, 8, 128])
    w_out_d = din("w_out", [D, D])
    w_gu0_d = din("w_gu0", [D, 2 * DFF])
    w_dn0_d = din("w_dn0", [DFF, D])
    w_kv_d = din("w_kv", [D, 2 * D])
    w_q_d = din("w_q", [D, D])
    w_o_d = din("w_o", [D, D])
    w_gu1_d = din("w_gu1", [D, 2 * DFF])
    w_dn1_d = din("w_dn1", [DFF, D])
    kaug_d = din("kaug", [2, NH, 5, 2, S], BF16)
    qaug_d = din("qaug", [NT, 5, NH, T], BF16)
    wmask_d = din("wmask", [128, 1408], BF16)
    sel_d = din("sel", [128, 4, 128], BF16)
    lamb_d = din("lamb", [128, 256])
    subln_d = din("subln", [128, 1])
    outT_o = nc.dram_tensor("outT", [D, TOK], F32, kind="ExternalOutput").ap()
    dint = lambda name, shape: nc.dram_tensor(name, shape, BF16, kind="Internal").ap()
    wb = {}
    for name, src in (("in", w_in_d), ("out", w_out_d), ("gu0", w_gu0_d), ("dn0", w_dn0_d), ("kv", w_kv_d),
                      ("q", w_q_d), ("o", w_o_d), ("gu1", w_gu1_d), ("dn1", w_dn1_d)):
        wb[name] = (dint("wb_" + name, list(src.shape)), src)
    snd = [nc.dram_tensor("snd%d" % t, [2048, T], BF16) for t in range(NT)]
    gat = [nc.dram_tensor("gat%d" % t, [4096, T], BF16) for t in range(NT)]

    with contextlib.ExitStack() as es:
        wr = alloc_common(nc, es, c, sc, nwslab=2)
        vecs = es.enter_context(nc.sbuf_tensor("s_vecs", [128, 7, 8], F32))
        wmask = es.enter_context(nc.sbuf_tensor("s_wmask", [128, 1408], BF16))
        sel = es.enter_context(nc.sbuf_tensor("s_sel", [128, 4, 128], BF16))
        sm = es.enter_context(nc.sbuf_tensor("s_sm", [128, 8], F32))
        rc = c.r_const
        r_w = {n: Res() for n in wb}
        r_c2, r_c3, r_c4, r_c5, r_c6, r_c7 = [Res() for _ in range(6)]
        r_sm = Res()
        r_snd = [Res() for _ in range(NT)]
        r_gat = [Res() for _ in range(NT)]

        sc.dma("sp", lambda e: e.dma_start(out=vecs[:], in_=vecs_d[:, :, :]), writes=[rc])
        sc.dma("sp", lambda e: e.dma_start(out=wmask[:], in_=wmask_d[:, :]), writes=[r_c5])
        sc.dma("sp", lambda e: e.dma_start(out=sel[:], in_=sel_d[:, :, :]), writes=[r_c6])
        sc.dma("sp", lambda e: e.dma_start(out=sm[:, 7:8], in_=subln_d[:, :]), writes=[r_c7])
        lamb = c.tmp[:, 0, 0:256]
        scr = c.tmp[:, 1, 0:64]
        sc.dma("sp", lambda e: e.dma_start(out=lamb, in_=lamb_d[:, :]), writes=[c.r_tmp[0]])
        sc.op("dve", lambda e: e.scalar_tensor_tensor(out=scr, in0=lamb[:, 0:64], scalar=1.0, in1=lamb[:, 64:128], op0=ALU.mult, op1=ALU.mult, accum_out=sm[:, 0:1]),
              reads=[c.r_tmp[0]], writes=[r_sm, c.r_tmp[1]])
        sc.op("dve", lambda e: e.scalar_tensor_tensor(out=scr, in0=lamb[:, 128:192], scalar=1.0, in1=lamb[:, 192:256], op0=ALU.mult, op1=ALU.mult, accum_out=sm[:, 1:2]),
              reads=[c.r_tmp[0], r_sm], writes=[r_sm, c.r_tmp[1]])
        sc.op("act", lambda e: e.activation(out=sm[:, 2:4], in_=sm[:, 0:2], func=AF.Exp), reads=[r_sm], writes=[r_sm])
        sc.op("dve", lambda e: e.tensor_tensor(out=sm[:, 4:5], in0=sm[:, 3:4], in1=sm[:, 2:3], op=ALU.subtract), reads=[r_sm], writes=[r_sm])
        sc.op("dve", lambda e: e.tensor_scalar(out=sm[:, 4:5], in0=sm[:, 4:5], scalar1=-LAMBDA_INIT, scalar2=None, op0=ALU.add), reads=[r_sm], writes=[r_sm])
        sc.op("dve", lambda e: e.tensor_scalar(out=sm[:, 5:6], in0=sm[:, 7:8], scalar1=1.0 - LAMBDA_INIT, scalar2=None, op0=ALU.mult), reads=[r_sm, r_c7], writes=[r_sm])

        esA = es.enter_context(contextlib.ExitStack())
        tril = esA.enter_context(nc.sbuf_tensor("s_tril", [128, 128], F32))
        wcT = esA.enter_context(nc.sbuf_tensor("s_wcT", [128, 8, 128], BF16))
        bsp = esA.enter_context(nc.sbuf_tensor("s_bsp", [128, 8, 128], F32))
        vstat = esA.enter_context(nc.sbuf_tensor("s_vstat", [128, 2, 6], F32))
        vmv = esA.enter_context(nc.sbuf_tensor("s_vmv", [128, 4], F32))
        nh1 = c.nh1
        r_vstat, r_vmv = Res(), Res()
        wspT_f = c.tmp[:].rearrange("p a (b c) -> p (a b) c", c=128)
        sc.dma("sp", lambda e: e.dma_start(out=wspT_f, in_=w_spT_d[:, :, :]), writes=c.r_tmp)
        sc.dma("sp", lambda e: e.dma_start(out=tril[:], in_=tril_d[:, :]), writes=[r_c2])
        sc.dma("sp", lambda e: e.dma_start(out=bsp[:], in_=bsp_d[:, :, :]), writes=[r_c3])
        for g in range(8):
            sc.op("dve", lambda e, g=g: e.tensor_tensor(out=wcT[:, g, :], in0=wspT_f[:, g, :], in1=tril[:], op=ALU.mult),
                  reads=[r_c2] + c.r_tmp, writes=[r_c4])
        xTv = xT_d.rearrange("(k p) t -> p k t", p=128)
        def load_x(t):
            sc.dma("sp", lambda e, t=t: e.dma_start(out=c.xT[:, :, t * T:(t + 1) * T], in_=xTv[:, :, t * T:(t + 1) * T]),
                   writes=[c.r_x[k][t] for k in range(8)])
        load_x(0)
        for name in ("in", "out", "gu0", "dn0", "kv"):
            cast_weight(sc, wb[name][0], wb[name][1], r_w[name])

        vf = c.arena[:, 16:20, :].rearrange("p a b -> p (a b)").bitcast(F32)
        r_vfslots = c.r_ar[16:20]
        groups = [[0, 1], [2, 3], [4, 5], [6, 7]]
        deferred = []

        def emit_gmlp(t, ts):
            emit_norm(sc, c, t, vecs[:, 0, :])
            while deferred:
                deferred.pop(0)()

            def evac_v(ch, b0):
                for hf in range(2):
                    sc.op("act", lambda e, hf=hf, b0=b0: e.activation(out=vf[:, hf * 512:(hf + 1) * 512], in_=c.ps[:, b0 + hf, :], func=AF.Gelu_apprx_tanh),
                          reads=[c.r_ps[b0 + hf]], writes=r_vfslots[2 * hf:2 * hf + 2])
                for hf in range(2):
                    sc.op("dve", lambda e, hf=hf: e.bn_stats(out=vstat[:, hf, :], in_=vf[:, hf * 512:(hf + 1) * 512]),
                          reads=r_vfslots[2 * hf:2 * hf + 2], writes=[r_vstat])
                sc.op("dve", lambda e: e.bn_aggr(out=vmv[:, 0:2], in_=vstat[:].rearrange("p a b -> p (a b)")), reads=[r_vstat], writes=[r_vmv])
                sc.op("dve", lambda e: e.tensor_scalar(out=vmv[:, 2:3], in0=vmv[:, 1:2], scalar1=EPS, scalar2=None, op0=ALU.add),
                      reads=[r_vmv], writes=[r_vmv])
                sc.op("pool", lambda e: e.tensor_tensor(out=vmv[:, 3:4], in0=vmv[:, 2:3], in1=nh1[:], op=ALU.pow),
                      reads=[r_vmv, rc], writes=[r_vmv])
                vn = c.arena[:, 8 + 2 * ch:10 + 2 * ch, :].rearrange("p a b -> p (a b)")
                sc.op("dve", lambda e, vn=vn: e.tensor_scalar(out=vn, in0=vf, scalar1=vmv[:, 0:1], scalar2=vmv[:, 3:4], op0=ALU.subtract, op1=ALU.mult),
                      reads=[r_vmv] + r_vfslots, writes=c.r_ar[8 + 2 * ch:10 + 2 * ch])
            emit_proj_tm(sc, c, wr, wb["in"][0], r_w["in"], D, evac_v)

            def evac_u(n, bank):
                sc.op("act", lambda e, n=n, bank=bank: e.activation(out=c.arena[:, n, :], in_=c.ps[:, bank, :], func=AF.Gelu_apprx_tanh),
                      reads=[c.r_ps[bank]], writes=[c.r_ar[n]])
                g = n
                zb = 4 + (g % 4)
                for ch in range(4):
                    vn = c.arena[:, 8 + 2 * ch:10 + 2 * ch, :].rearrange("p a b -> p (a b)")
                    sc.op("pe", lambda e, g=g, ch=ch, zb=zb, vn=vn: e.matmul(c.ps[:, zb, ch * 128:(ch + 1) * 128], lhsT=vn[:, g * 128:(g + 1) * 128], rhs=wcT[:, g, :],
                                                                              start=True, stop=True),
                          reads=c.r_ar[8 + 2 * ch:10 + 2 * ch] + [r_c4], writes=[c.r_ps[zb]], inc=(ch == 3))
                sb = g % 2
                sc.op("dve", lambda e, g=g, zb=zb, sb=sb: e.scalar_tensor_tensor(
                    out=c.tmp[:, sb, :].rearrange("p (a b) -> p a b", a=4), in0=c.ps[:, zb, :].rearrange("p (a b) -> p a b", a=4),
                    scalar=vecs[:, 1, g:g + 1], in1=bsp[:, g, :].unsqueeze(1).broadcast_to([128, 4, 128]), op0=ALU.mult, op1=ALU.add),
                    reads=[c.r_ps[zb], rc, r_c3], writes=[c.r_tmp[sb]])
                sc.op("dve", lambda e, g=g, sb=sb: e.tensor_tensor(out=c.arena[:, g, :], in0=c.tmp[:, sb, :], in1=c.arena[:, g, :], op=ALU.mult),
                      reads=[c.r_tmp[sb], c.r_ar[g]], writes=[c.r_ar[g]])
            emit_proj_fm(sc, c, wr, wb["in"][0], r_w["in"], 0, D, evac_u)

            def evac_res(n, bank):
                sc.op("dve", lambda e, n=n, bank=bank: e.tensor_tensor(out=c.xT[:, n, ts], in0=c.xT[:, n, ts], in1=c.ps[:, bank, :], op=ALU.add),
                      reads=[c.r_ps[bank], c.r_x[n][t]], writes=[c.r_x[n][t]])
            def evac_res_sq(n, bank):
                evac_res(n, bank)
                emit_square(sc, c, t, n)
            emit_proj_fm(sc, c, wr, wb["out"][0], r_w["out"], 0, D, evac_res_sq, src=c.arena, r_src=c.r_ar)

        def emit_kv(t, ts):
            emit_norm(sc, c, t, vecs[:, 3, :], squares_done=True)
            sndk = snd[t][0:1024, :].rearrange("(k p) t -> p k t", p=128)
            sndv = snd[t][1024:2048, :].rearrange("(h p) (c d) -> p c h d", p=128, d=128)

            def evac_k(n, bank):
                sc.op("act", lambda e, n=n, bank=bank: e.copy(out=c.arena[:, n, :], in_=c.ps[:, bank, :]),
                      reads=[c.r_ps[bank]], writes=[c.r_ar[n]])
            emit_proj_fm(sc, c, wr, wb["kv"][0], r_w["kv"], 0, D, evac_k, kouter=True)
            sc.dma("act", lambda e: e.dma_start(out=sndk, in_=c.arena[:, 0:8, :]), reads=c.r_ar[0:8], writes=[r_snd[t]])

            def evac_vv(ch, b0):
                vst = c.arena[:, 8 + 2 * ch:10 + 2 * ch, :].rearrange("p a b -> p (a b)")
                sc.op("act", lambda e, vst=vst, b0=b0: e.copy(out=vst[:, 0:512], in_=c.ps[:, b0, :]),
                      reads=[c.r_ps[b0]], writes=[c.r_ar[8 + 2 * ch]])
                sc.op("dve", lambda e, vst=vst, b0=b0: e.tensor_copy(out=vst[:, 512:1024], in_=c.ps[:, b0 + 1, :]),
                      reads=[c.r_ps[b0 + 1]], writes=[c.r_ar[9 + 2 * ch]])
            emit_proj_tm(sc, c, wr, wb["kv"][0], r_w["kv"], D, evac_vv)
            for ch in range(4):
                sc.dma("act", lambda e, ch=ch: e.dma_start(out=sndv[:, ch], in_=c.arena[:, 8 + 2 * ch:10 + 2 * ch, :].rearrange("p a (h d) -> p (a h) d", d=128)),
                       reads=c.r_ar[8 + 2 * ch:10 + 2 * ch], writes=[r_snd[t]])

            def do_gather(t=t):
                sched_coll(sc, lambda e, t=t: e.collective_compute("AllGather", ALU.bypass, replica_groups=groups,
                                                                   ins=[snd[t].ap().opt()], outs=[gat[t].ap().opt()]),
                           reads=[r_snd[t]], writes=[r_gat[t]])
            deferred.append(do_gather)

        for t in range(NT):
            ts = slice(t * T, (t + 1) * T)
            emit_gmlp(t, ts)
            if t == NT - 1:
                snap_a = {e_: sc.cnt[e_] for e_ in COMPUTE if sc.cnt[e_] > 0}
                snap_a.update({k_: (16 * v_ if k_[0] == "d" else v_) for k_, v_ in sc.dcnt.items()})
            if t + 1 < NT:
                load_x(t + 1)
            if t == 1:
                for name in ("q", "o", "gu1", "dn1"):
                    cast_weight(sc, wb[name][0], wb[name][1], r_w[name])
            emit_norm(sc, c, t, vecs[:, 2, :], squares_done=True)
            emit_ffn(sc, c, t, wr, wb["gu0"][0], wb["dn0"][0], r_w["gu0"], r_w["dn0"], post_add=lambda nn, t=t: emit_square(sc, c, t, nn))
            emit_kv(t, ts)
        while deferred:
            deferred.pop(0)()

        esA.close()
        NKV = 4
        kring = es.enter_context(nc.sbuf_tensor("s_kring", [69, NKV, 2, 512], BF16))
        vring = es.enter_context(nc.sbuf_tensor("s_vring", [128, NKV, 4, 128], BF16))
        snapshot = snap_a
        r_kv = [Res() for _ in range(NKV)]
        for r_ in r_kv:
            r_.r = dict(snapshot)
        r_qaug = Res()
        r_pt = c.r_ar[16:20]
        Qt = c.arena[:, 0:16, :].rearrange("p (h i) t -> p h i t", i=2)
        onT, r_on = c.sq, c.r_sq
        outTv = outT_o.rearrange("(k p) t -> p k t", p=128)
        strot = [0]
        ptrot = [0]
        kvn = [0]

        def load_kv(h, cp, hf, dg):
            b = kvn[0] % NKV
            kvn[0] += 1
            r = r_kv[b]
            ranks = (0, 1) if cp % 2 == 0 else (1, 0)
            rk = ranks[hf]
            g_ = gat[cp]
            ksrc = g_[rk * 2048 + h * 128:rk * 2048 + (h + 1) * 128, :].rearrange("(i d) t -> d i t", d=64)
            sc.dma("sp", lambda e, b=b, ksrc=ksrc: e.dma_start(out=kring[0:64, b, :, :], in_=ksrc), reads=[r_gat[cp]], writes=[r])
            vsrc = g_[rk * 2048 + 1024 + h * 128:rk * 2048 + 1024 + (h + 1) * 128, :].rearrange("p (c d) -> p c d", d=128)
            sc.dma("sp", lambda e, b=b, vsrc=vsrc: e.dma_start(out=vring[:, b, :, :], in_=vsrc), reads=[r_gat[cp]], writes=[r])
            sc.dma("sp", lambda e, b=b: e.dma_start(out=kring[64:69, b, :, :], in_=kaug_d[dg, h, :, :, cp * 1024 + hf * 512:cp * 1024 + (hf + 1) * 512]), writes=[r])
            return b, r

        def emit_attention(j):
            par = j % 2
            chunks = [(h, ci, hf) for h in range(NH) for ci in range(j + 1) for hf in range(2)]
            loaded = {}

            def ensure(idx):
                if idx < len(chunks) and idx not in loaded:
                    h, ci, hf = chunks[idx]
                    loaded[idx] = load_kv(h, j - ci, hf, 1 if ci == 0 else 0)
            steps = []
            for idx, (h, ci, hf) in enumerate(chunks):
                for o4 in range(4):
                    for i in range(2):
                        steps.append((idx, h, ci, hf * 4 + o4, i))
            nsteps_h = (j + 1) * 16
            pend = []
            LAG = 2

            def emit_av(item):
                (idx, h, ci, o, i, pt, first, last) = item
                b, rkv = loaded[idx]
                sc.op("pe", lambda e, b=b, o=o, i=i, pt=pt, first=first, last=last: e.matmul(c.ps[:, 4 + i, :], lhsT=vring[:, b, o % 4, :], rhs=c.arena[:, 16 + pt, :], start=first, stop=last),
                      reads=[rkv, r_pt[pt]], writes=[c.r_ps[4 + i]], inc=True)
                if i == 0:
                    sc.op("pe", lambda e, i=i, pt=pt, first=first, last=last: e.matmul(c.ps[:, 6 + i, :], lhsT=c.ones[:], rhs=c.arena[:, 16 + pt, :], start=first, stop=last),
                          reads=[r_pt[pt], rc], writes=[c.r_ps[6 + i]], inc=True)
                elif first:
                    sc.op("dve", lambda e, i=i, pt=pt: e.tensor_copy(out=c.ps[:, 6 + i, :], in_=c.arena[:, 16 + pt, :]),
                          reads=[r_pt[pt]], writes=[c.r_ps[6 + i]])
                else:
                    sc.op("dve", lambda e, i=i, pt=pt: e.tensor_tensor(out=c.ps[:, 6 + i, :], in0=c.ps[:, 6 + i, :], in1=c.arena[:, 16 + pt, :], op=ALU.add),
                          reads=[r_pt[pt], c.r_ps[6 + i]], writes=[c.r_ps[6 + i]])
                if last and i == 1:
                    emit_head_post(h)

            def emit_head_post(h):
                a_, b_ = c.tmp[:, 0, :], c.tmp[:, 1, :]
                sc.op("act", lambda e: e.copy(out=c.arena[:, 21, :], in_=c.ps[:, 7, :]), reads=[c.r_ps[7]], writes=[c.r_ar[21]])
                sc.op("act", lambda e: e.copy(out=a_, in_=c.ps[:, 4, :]), reads=[c.r_ps[4]], writes=[c.r_tmp[0]])
                sc.op("dve", lambda e: e.tensor_copy(out=b_, in_=c.ps[:, 5, :]), reads=[c.r_ps[5]], writes=[c.r_tmp[1]])
                sc.op("dve", lambda e: e.reciprocal(out=c.ms[:], in_=c.ps[:, 6, :]), reads=[c.r_ps[6]], writes=[c.r_ms])
                bank = strot[0] % 4
                strot[0] += 1
                sc.op("pe", lambda e, bank=bank: e.matmul(c.ps[:, bank, :], lhsT=c.ones[:], rhs=c.arena[:, 21, :], start=True, stop=True),
                      reads=[c.r_ar[21], rc], writes=[c.r_ps[bank]])
                sc.op("dve", lambda e: e.tensor_tensor(out=a_, in0=a_, in1=c.ms[:], op=ALU.mult), reads=[c.r_tmp[0], c.r_ms], writes=[c.r_tmp[0]])
                sc.op("dve", lambda e, bank=bank: e.reciprocal(out=c.rstd[:], in_=c.ps[:, bank, :]), reads=[c.r_ps[bank]], writes=[c.r_rstd])
                sc.op("dve", lambda e: e.tensor_tensor(out=b_, in0=b_, in1=c.rstd[:], op=ALU.mult), reads=[c.r_tmp[1], c.r_rstd], writes=[c.r_tmp[1]])
                sc.op("dve", lambda e: e.scalar_tensor_tensor(out=a_, in0=b_, scalar=sm[:, 4:5], in1=a_, op0=ALU.mult, op1=ALU.add),
                      reads=[c.r_tmp[1], c.r_tmp[0], r_sm], writes=[c.r_tmp[0]])
                sc.op("dve", lambda e: e.tensor_tensor(out=c.arena[:, 20, :], in0=a_, in1=a_, op=ALU.mult), reads=[c.r_tmp[0]], writes=[c.r_ar[20]])
                bank = strot[0] % 4
                strot[0] += 1
                sc.op("pe", lambda e, bank=bank: e.matmul(c.ps[:, bank, :], lhsT=c.ones[:], rhs=c.arena[:, 20, :], start=True, stop=True),
                      reads=[c.r_ar[20], rc], writes=[c.r_ps[bank]])
                sc.op("act", lambda e, bank=bank: e.activation(out=c.ms[:], in_=c.ps[:, bank, :], func=AF.Sqrt, scale=1.0 / 128, bias=EPS),
                      reads=[c.r_ps[bank]], writes=[c.r_ms])
                sc.op("dve", lambda e: e.reciprocal(out=c.rstd[:], in_=c.ms[:]),
                      reads=[c.r_ms], writes=[c.r_rstd])
                sc.op("dve", lambda e, h=h: e.scalar_tensor_tensor(out=onT[:, h, :], in0=a_, scalar=sm[:, 5:6], in1=c.rstd[:], op0=ALU.mult, op1=ALU.mult),
                      reads=[c.r_tmp[0], c.r_rstd, r_sm], writes=[r_on[h]])

            def diag_pat(d):
                off = 896 - 128 * d
                return wmask[:, off:off + 512]
            negpat = wmask[:, 0:512]
            selA, selB = sel[:, 2 * par, :], sel[:, 2 * par + 1, :]

            for sidx, (idx, h, ci, o, i) in enumerate(steps):
                if o % 4 == 0 and i == 0:
                    ensure(idx)
                    ensure(idx + 1)
                    ensure(idx + 2)
                if o % 4 == 1 and i == 0:
                    ensure(idx + 3)
                b, rkv = loaded[idx]
                bank = strot[0] % 4
                strot[0] += 1
                pt = ptrot[0] % 4
                ptrot[0] += 1
                diag = (ci == 0)
                sc.op("pe", lambda e, b=b, h=h, o=o, i=i, bank=bank, diag=diag: e.matmul(c.ps[:, bank, :], lhsT=kring[0:69, b, i, (o % 4) * 128:(o % 4 + 1) * 128], rhs=Qt[0:69, h, i, :], start=True, stop=(not diag)),
                      reads=[rkv, c.r_ar[2 * h + i], r_qaug], writes=[c.r_ps[bank]], inc=(not diag))
                if diag:
                    if o < 4:
                        mm = [(selA, diag_pat(o))]
                    else:
                        mm = [(selB, diag_pat(o - 4))]
                    for mi, (lt, rh) in enumerate(mm):
                        lastm = (mi == len(mm) - 1)
                        sc.op("pe", lambda e, bank=bank, lt=lt, rh=rh, lastm=lastm: e.matmul(c.ps[:, bank, :], lhsT=lt, rhs=rh, start=False, stop=lastm),
                              reads=[r_c5, r_c6], writes=[c.r_ps[bank]], inc=lastm)
                sc.op("act", lambda e, bank=bank, pt=pt: e.activation(out=c.arena[:, 16 + pt, :], in_=c.ps[:, bank, :], func=AF.Exp, scale=0.125),
                      reads=[c.r_ps[bank]], writes=[r_pt[pt]])
                hs = sidx - h * nsteps_h
                first = hs < 2
                last = hs >= nsteps_h - 2
                pend.append((idx, h, ci, o, i, pt, first, last))
                if len(pend) > LAG:
                    emit_av(pend.pop(0))
            while pend:
                emit_av(pend.pop(0))

        def emit_tile_B(t):
            ts = slice(t * T, (t + 1) * T)
            emit_norm(sc, c, t, vecs[:, 4, :])
            for i in range(2):
                sc.dma("sp", lambda e, t=t, i=i: e.dma_start(out=Qt[64:69, :, i, :], in_=qaug_d[t, :, :, :]), writes=[r_qaug] + c.r_ar[0:16])

            def evac_q(h, bank):
                sc.op("act", lambda e, h=h, bank=bank: e.copy(out=Qt[0:64, h, 0, :], in_=c.ps[0:64, bank, :]), reads=[c.r_ps[bank]], writes=[c.r_ar[2 * h]])
                sc.op("act", lambda e, h=h, bank=bank: e.copy(out=Qt[0:64, h, 1, :], in_=c.ps[64:128, bank, :]), reads=[c.r_ps[bank]], writes=[c.r_ar[2 * h + 1]])
            emit_proj_fm(sc, c, wr, wb["q"][0], r_w["q"], 0, D, evac_q, kouter=True)
            emit_attention(t)

            def evac_res(n, bank):
                sc.op("dve", lambda e, n=n, bank=bank: e.tensor_tensor(out=c.xT[:, n, ts], in0=c.xT[:, n, ts], in1=c.ps[:, bank, :], op=ALU.add),
                      reads=[c.r_ps[bank], c.r_x[n][t]], writes=[c.r_x[n][t]])
            emit_proj_fm(sc, c, wr, wb["o"][0], r_w["o"], 0, D, evac_res, src=onT, r_src=r_on)
            emit_norm(sc, c, t, vecs[:, 5, :])
            emit_ffn(sc, c, t, wr, wb["gu1"][0], wb["dn1"][0], r_w["gu1"], r_w["dn1"], post_add=lambda nn, t=t: emit_square(sc, c, t, nn))
            bank = c.stat_bank
            for k in range(8):
                sc.op("pe", lambda e, k=k: e.matmul(c.ps[:, bank, :], lhsT=c.ones[:], rhs=c.sq[:, k, :], start=(k == 0), stop=(k == 7)),
                      reads=[c.r_sq[k], rc], writes=[c.r_ps[bank]], inc=(k == 7))
            sc.op("act", lambda e: e.activation(out=c.ms[:], in_=c.ps[:, bank, :], func=AF.Sqrt, scale=1.0 / D, bias=EPS),
                  reads=[c.r_ps[bank]], writes=[c.r_ms])
            sc.op("dve", lambda e: e.reciprocal(out=c.rstd[:], in_=c.ms[:]),
                  reads=[c.r_ms], writes=[c.r_rstd])
            for k in range(8):
                sb = k % 2
                sc.op("dve", lambda e, k=k, sb=sb: e.scalar_tensor_tensor(out=c.tmp[:, sb, :], in0=c.xT[:, k, ts], scalar=vecs[:, 6, k:k + 1], in1=c.rstd[:],
                                                                       op0=ALU.mult, op1=ALU.mult),
                      reads=[c.r_x[k][t], c.r_rstd, rc], writes=[c.r_tmp[sb]])
                sc.dma("act", lambda e, k=k, sb=sb: e.dma_start(out=outTv[:, k, ts], in_=c.tmp[:, sb, :]), reads=[c.r_tmp[sb]])

        for t in range(NT):
            emit_tile_B(t)

        sc.finish("sp")
        sc.replay(nc)
    return nc, sc


def prep_F(inputs, core):
    b, role = core // 2, core % 2
    f = np.float32
    a = prep_A(inputs, core)
    col = lambda v: np.ascontiguousarray(np.asarray(v, f).reshape(8, 128).T)
    vecs = np.stack([col(inputs["a_norm"][0]), col(inputs["a_v_norm"][0]), col(inputs["ffn_norm"][0]), col(inputs["kv_norm"]),
                     col(inputs["b_norm"][0]), col(inputs["ffn_norm"][1]), col(inputs["final_norm"])], axis=1)
    slopes = alibi_slopes()
    pos = np.arange(S)
    khi, klo = pos // 128, pos % 128
    kaug = np.zeros((2, NH, 5, 2, S), np.float64)
    kaug[:, :, 0] = 1.0
    kaug[:, :, 1] = 1.0
    kaug[:, :, 2] = (slopes[:, None] * 128.0 * khi[None, :])[None, :, None, :]
    kaug[:, :, 3] = (slopes[:, None] * klo[None, :])[None, :, None, :]
    kaug[1, :, 4] = ((pos % 1024) >= 512).astype(np.float64)[None, None, :]
    qaug = np.zeros((NT, 5, NH, T), np.float64)
    for j in range(NT):
        qpos = tile_index(role, j) * T + np.arange(T)
        qhi, qlo = qpos // 128, qpos % 128
        qaug[j, 0] = -8.0 * slopes[:, None] * 128.0 * qhi[None, :]
        qaug[j, 1] = -8.0 * slopes[:, None] * qlo[None, :]
        qaug[j, 2] = 8.0
        qaug[j, 3] = 8.0
        qaug[j, 4] = -240000.0 if tile_index(role, j) == 2 * j else 0.0
    kk = np.arange(128)[:, None]
    cc = np.arange(1408)[None, :]
    wmask = np.where(cc - 896 - kk >= 0, 0.0, -240000.0).astype(NPBF)
    sel = np.zeros((128, 4, 128), np.float32)
    eye = np.eye(128, dtype=np.float32)
    for par in range(2):
        case_a = (tile_index(role, par) == 2 * par)
        sel[:, 2 * par, :] = eye if case_a else 0.0
        sel[:, 2 * par + 1, :] = 0.0 if case_a else eye
    return {
        "xT": a["xT"], "vecs": np.ascontiguousarray(vecs), "w_in": a["w_in"], "w_spT": a["w_spT"], "trilT": a["trilT"], "bsp": a["bsp"],
        "w_out": a["w_out"], "w_gu0": a["w_gu"], "w_dn0": a["w_dn"], "w_kv": a["w_kv"],
        "w_q": np.asarray(inputs["b_w_q"][0], f), "w_o": np.asarray(inputs["b_w_o"][0], f),
        "w_gu1": np.asarray(inputs["ffn_w_gu"][1], f), "w_dn1": np.asarray(inputs["ffn_w_down"][1], f),
        "kaug": kaug.astype(NPBF), "qaug": qaug.astype(NPBF), "wmask": wmask, "sel": sel.astype(NPBF),
        "lamb": np.ascontiguousarray(np.broadcast_to(np.asarray(inputs["b_lambda"][0], f).reshape(1, 256), (128, 256))),
        "subln": np.ascontiguousarray(np.asarray(inputs["b_subln"][0], f).reshape(128, 1)),
    }


def kernel(**inputs):
    if "F" not in _CACHE:
        _CACHE["F"] = build_F()[0]
    nc = _CACHE["F"]
    in_maps = [prep_F(inputs, cidx) for cidx in range(8)]
    res = run_bass_kernel_spmd(nc, in_maps, core_ids=list(range(8)))
    out = np.zeros((B, S, D), np.float32)
    for core in range(8):
        b, role = core // 2, core % 2
        oT = np.asarray(res.results[core]["outT"])
        for j in range(NT):
            i = tile_index(role, j)
            out[b, i * T:(i + 1) * T, :] = oT[:, j * T:(j + 1) * T].T
    return out
```

```python
import math
import numpy as np
import ml_dtypes
import concourse.bass as bass
import concourse.mybir as mybir
from concourse.bass_utils import run_bass_kernel_spmd

F32 = mybir.dt.float32
BF16 = mybir.dt.bfloat16
AF = mybir.ActivationFunctionType
ALU = mybir.AluOpType
NPBF = ml_dtypes.bfloat16

D = 1024
S = 8192
B = 4
DFF = 2816
NJ = DFF // 128
T = 512
NT = 8
TOK = NT * T
EPS = 1e-6
NH = 8
KROWS = 80
LAMBDA_INIT = 0.8 - 0.6 * math.exp(-0.3 * 1)
STAGES = {"gmlp": True, "ffn": True, "dbg": False, "attn": True, "ffnB": True, "dbgB": False}


def tile_index(role, j):
    if role == 0:
        return 2 * j if j % 2 == 0 else 2 * j + 1
    return 2 * j + 1 if j % 2 == 0 else 2 * j


class Res:
    __slots__ = ("w", "r")

    def __init__(self):
        self.w = None
        self.r = {}


COMPUTE = ("pe", "act", "dve", "pool")


class Sched:
    ND = 12

    def __init__(self):
        self.q = {e: [] for e in COMPUTE + ("sp",)}
        self.cnt = {e: 0 for e in COMPUTE}
        self.waited = {e: {} for e in COMPUTE + ("sp",)}
        self.pending = {e: [] for e in COMPUTE}
        self.dn = {"sp": 0, "pool": 0, "act": 0}
        self.dcnt = {}
        self.n_inst = 0

    def _deps(self, reads, writes):
        toks = []
        for r in reads:
            if r.w is not None:
                toks.append(r.w)
        for w in writes:
            if w.w is not None:
                toks.append(w.w)
            toks.extend(w.r.items())
        return toks

    def _wait(self, eng, toks):
        wd = self.waited[eng]
        for key, val in toks:
            if eng == "pe" and key == "pe":
                continue
            if wd.get(key, 0) < val:
                wd[key] = val
                self.q[eng].append(("wait", key, val))

    @staticmethod
    def _mark(tok, reads, writes):
        key, val = tok
        for r in reads:
            if r.r.get(key, 0) < val:
                r.r[key] = val
        for w in writes:
            w.w = tok
            w.r = {}

    def op(self, eng, fn, reads=(), writes=(), inc=True):
        self._wait(eng, self._deps(reads, writes))
        self.n_inst += 1
        if not inc:
            self._mark((eng, self.cnt[eng] + 1), reads, writes)
            self.q[eng].append(("op", fn, None))
            return None
        self.cnt[eng] += 1
        tok = (eng, self.cnt[eng])
        self.q[eng].append(("op", fn, tok))
        self._mark(tok, reads, writes)
        return tok

    def dma(self, queue, fn, reads=(), writes=()):
        toks = self._deps(reads, writes)
        nd = 6 if queue == "pool" else self.ND
        idx = self.dn[queue] % nd
        self.dn[queue] += 1
        key = ("d", queue, idx)
        c = self.dcnt.get(key, 0)
        if c > 0:
            toks.append((key, 16 * c))
        self._wait(queue, toks)
        self.dcnt[key] = c + 1
        tok = (key, 16 * (c + 1))
        self.q[queue].append(("op", fn, tok))
        self._mark(tok, reads, writes)
        self.n_inst += 1
        return tok

    def finish(self, eng="sp"):
        toks = [(e, self.cnt[e]) for e in COMPUTE if self.cnt[e] > 0]
        toks += [(k, (16 * c if k[0] == "d" else c)) for k, c in self.dcnt.items()]
        for e in COMPUTE + ("sp",):
            self._wait(e, [t for t in toks if t[0] != e])

    def replay(self, nc):
        sem_names = list(COMPUTE) + [k for k in self.dcnt]
        import contextlib
        with contextlib.ExitStack() as es:
            sems = {}
            for i, k in enumerate(sem_names):
                sems[k] = es.enter_context(nc.semaphore("s%d" % i))
            block = es.enter_context(nc.Block())
            engmap = {"pe": block.tensor, "act": block.scalar, "dve": block.vector,
                      "pool": block.gpsimd, "sp": block.sync}
            for ename, starter in engmap.items():
                items = self.q[ename]

                def body(eng, items=items):
                    for it in items:
                        if it[0] == "wait":
                            eng.wait_ge(sems[it[1]], it[2])
                        else:
                            inst = it[1](eng)
                            tok = it[2]
                            if tok is not None:
                                key = tok[0]
                                if isinstance(key, tuple) and key[0] == "c":
                                    inst.then_inc(sems[key])
                                else:
                                    inst.then_inc(sems[key], 16 if isinstance(key, tuple) else 1)
                starter(body)


class Ctx:
    pass


def emit_square(sc, c, t, k):
    ts = slice(t * T, (t + 1) * T)
    sc.op("dve", lambda e, k=k: e.tensor_tensor(out=c.sq[:, k, :], in0=c.xT[:, k, ts], in1=c.xT[:, k, ts], op=ALU.mult),
          reads=[c.r_x[k][t]], writes=[c.r_sq[k]])


def emit_norm(sc, c, t, gcol, squares_done=False):
    ts = slice(t * T, (t + 1) * T)
    for k in range(8):
        if not squares_done:
            emit_square(sc, c, t, k)
    bank = c.stat_bank
    for k in range(8):
        sc.op("pe", lambda e, k=k: e.matmul(c.ps[:, bank, :], lhsT=c.ones[:], rhs=c.sq[:, k, :], start=(k == 0), stop=(k == 7)),
              reads=[c.r_sq[k], c.r_const], writes=[c.r_ps[bank]], inc=(k == 7))
    sc.op("act", lambda e: e.activation(out=c.ms[:], in_=c.ps[:, bank, :], func=AF.Sqrt, scale=1.0 / D, bias=EPS),
          reads=[c.r_ps[bank]], writes=[c.r_ms])
    sc.op("dve", lambda e: e.reciprocal(out=c.rstd[:], in_=c.ms[:]),
          reads=[c.r_ms], writes=[c.r_rstd])
    for k in range(8):
        sc.op("dve", lambda e, k=k: e.scalar_tensor_tensor(out=c.hT[:, k, :], in0=c.xT[:, k, ts], scalar=gcol[:, k:k + 1], in1=c.rstd[:],
                                                          op0=ALU.mult, op1=ALU.mult),
              reads=[c.r_x[k][t], c.r_rstd, c.r_const], writes=[c.r_h[k]])


class WRing:
    def __init__(self, sc, tens, n):
        self.sc = sc
        self.t = tens
        self.n = n
        self.res = [Res() for _ in range(n)]
        self.i = 0

    def load(self, parts, src_res):
        b = self.i % self.n
        self.i += 1
        slab = self.t[:, b]
        r = self.res[b]
        for dst_fn, src in parts:
            self.sc.dma("sp", lambda e, dst_fn=dst_fn, src=src, slab=slab: e.dma_start(out=dst_fn(slab), in_=src),
                        reads=[src_res], writes=[r])
        return slab, r


def wview(w):
    return w.rearrange("(k p) n -> p k n", p=128)


def emit_ffn(sc, c, t, wr, wgu, wdn, r_wgu, r_wdn, post_add=None):
    ts = slice(t * T, (t + 1) * T)
    wguv = wview(wgu)
    wdnv = wview(wdn)
    for j2 in range(NJ // 2):
        slab, rs = wr.load([(lambda s: s[:, :, 0:256], wguv[:, :, j2 * 256:(j2 + 1) * 256]),
                            (lambda s: s[:, :, 256:512], wguv[:, :, DFF + j2 * 256:DFF + (j2 + 1) * 256])], r_wgu)
        for jj in range(2):
            j = 2 * j2 + jj
            bg = 2 * (j % 2)
            bu = bg + 1
            for k in range(8):
                sc.op("pe", lambda e, k=k, jj=jj, bg=bg, slab=slab: e.matmul(c.ps[:, bg, :], lhsT=slab[:, k, jj * 128:(jj + 1) * 128], rhs=c.hT[:, k, :],
                                                                              start=(k == 0), stop=(k == 7)),
                      reads=[rs, c.r_h[k]], writes=[c.r_ps[bg]], inc=(k == 7))
            for k in range(8):
                sc.op("pe", lambda e, k=k, jj=jj, bu=bu, slab=slab: e.matmul(c.ps[:, bu, :], lhsT=slab[:, k, 256 + jj * 128:256 + (jj + 1) * 128], rhs=c.hT[:, k, :],
                                                                              start=(k == 0), stop=(k == 7)),
                      reads=[rs, c.r_h[k]], writes=[c.r_ps[bu]], inc=(k == 7))
            sb = j % 2
            sc.op("act", lambda e, bg=bg, sb=sb: e.activation(out=c.tmp[:, sb, :], in_=c.ps[:, bg, :], func=AF.Silu),
                  reads=[c.r_ps[bg]], writes=[c.r_tmp[sb]])
            sc.op("dve", lambda e, j=j, bu=bu, sb=sb: e.tensor_tensor(out=c.arena[:, j, :], in0=c.tmp[:, sb, :], in1=c.ps[:, bu, :], op=ALU.mult),
                  reads=[c.r_tmp[sb], c.r_ps[bu]], writes=[c.r_ar[j]])
    kgroups = [(0, 8), (8, 8), (16, 6)]
    for hf in range(2):
        for (k0, nk) in kgroups:
            slab, rs = wr.load([(lambda s, nk=nk: s[:, 0:nk, :], wdnv[:, k0:k0 + nk, hf * 512:(hf + 1) * 512])], r_wdn)
            for kk in range(nk):
                k = k0 + kk
                for n in range(4):
                    sc.op("pe", lambda e, kk=kk, k=k, n=n, slab=slab: e.matmul(c.ps[:, 4 + n, :], lhsT=slab[:, kk, n * 128:(n + 1) * 128], rhs=c.arena[:, k, :],
                                                                                start=(k == 0), stop=(k == NJ - 1)),
                          reads=[rs, c.r_ar[k]], writes=[c.r_ps[4 + n]], inc=(k == NJ - 1 or (kk == nk - 1 and n == 3)))
        for n in range(4):
            nn = hf * 4 + n
            sc.op("dve", lambda e, n=n, nn=nn: e.tensor_tensor(out=c.xT[:, nn, ts], in0=c.xT[:, nn, ts], in1=c.ps[:, 4 + n, :], op=ALU.add),
                  reads=[c.r_ps[4 + n], c.r_x[nn][t]], writes=[c.r_x[nn][t]])
            if post_add is not None:
                post_add(nn)


def emit_proj_fm(sc, c, wr, w, r_w, col0, ncols, evac, src=None, r_src=None, kouter=False):
    src = c.hT if src is None else src
    r_src = c.r_h if r_src is None else r_src
    wv = wview(w)
    nchunks = ncols // 128
    for s0 in range(0, nchunks, 4):
        slab, rs = wr.load([(lambda s: s, wv[:, :, col0 + s0 * 128:col0 + s0 * 128 + 512])], r_w)
        if s0 == 0 and kouter:
            banks = [(c.mm_rot + q) % 4 for q in range(4)]
            c.mm_rot += 4
            for k in range(8):
                for nn in range(4):
                    sc.op("pe", lambda e, k=k, nn=nn, bank=banks[nn], slab=slab: e.matmul(c.ps[:, bank, :], lhsT=slab[:, k, nn * 128:(nn + 1) * 128], rhs=src[:, k, :],
                                                                                         start=(k == 0), stop=(k == 7)),
                          reads=[rs, r_src[k]], writes=[c.r_ps[banks[nn]]], inc=(k == 7))
            for nn in range(4):
                evac(nn, banks[nn])
            continue
        for nn in range(4):
            n = s0 + nn
            bank = c.mm_rot % 4
            c.mm_rot += 1
            for k in range(8):
                sc.op("pe", lambda e, k=k, nn=nn, bank=bank, slab=slab: e.matmul(c.ps[:, bank, :], lhsT=slab[:, k, nn * 128:(nn + 1) * 128], rhs=src[:, k, :],
                                                                                  start=(k == 0), stop=(k == 7)),
                      reads=[rs, r_src[k]], writes=[c.r_ps[bank]], inc=(k == 7))
            evac(n, bank)


def emit_proj_tm(sc, c, wr, w, r_w, col0, evac):
    wv = wview(w)
    slabs = []
    for hf in range(2):
        slabs.append(wr.load([(lambda s: s, wv[:, :, col0 + hf * 512:col0 + (hf + 1) * 512])], r_w))
    for ch in range(4):
        b0 = 4 + 2 * (ch % 2)
        for hf in range(2):
            slab, rs = slabs[hf]
            for k in range(8):
                sc.op("pe", lambda e, k=k, ch=ch, hf=hf, b0=b0, slab=slab: e.matmul(c.ps[:, b0 + hf, :], lhsT=c.hT[:, k, ch * 128:(ch + 1) * 128], rhs=slab[:, k, :],
                                                                                     start=(k == 0), stop=(k == 7)),
                      reads=[rs, c.r_h[k]], writes=[c.r_ps[b0 + hf]], inc=(k == 7))
        evac(ch, b0)


def alloc_common(nc, es, c, sc, nwslab=3):
    c.xT = es.enter_context(nc.sbuf_tensor("s_xT", [128, 8, TOK], F32))
    c.hT = es.enter_context(nc.sbuf_tensor("s_hT", [128, 8, T], BF16))
    c.sq = es.enter_context(nc.sbuf_tensor("s_sq", [128, 8, T], BF16))
    c.arena = es.enter_context(nc.sbuf_tensor("s_arena", [128, NJ, T], BF16))
    c.ms = es.enter_context(nc.sbuf_tensor("s_ms", [128, T], F32))
    c.rstd = es.enter_context(nc.sbuf_tensor("s_rstd", [128, T], F32))
    c.tmp = es.enter_context(nc.sbuf_tensor("s_tmp", [128, 2, T], F32))
    c.nh1 = es.enter_context(nc.sbuf_tensor("s_nh1", [128, 1], F32))
    c.ones = es.enter_context(nc.sbuf_tensor("s_ones", [128, 128], BF16))
    c.wslab = es.enter_context(nc.sbuf_tensor("s_wslab", [128, nwslab, 8, 512], BF16))
    c.ps = es.enter_context(nc.psum_tensor("p_ps", [128, 8, 512], F32))
    c.r_x = [[Res() for _ in range(NT)] for _ in range(8)]
    c.r_h = [Res() for _ in range(8)]
    c.r_sq = [Res() for _ in range(8)]
    c.r_ar = [Res() for _ in range(NJ)]
    c.r_ms = Res()
    c.r_rstd = Res()
    c.r_tmp = [Res(), Res()]
    c.r_const = Res()
    c.r_ps = [Res() for _ in range(8)]
    c.stat_bank = 3
    c.mm_rot = 0
    sc.op("pool", lambda e: e.memset(c.nh1[:], -0.5), writes=[c.r_const])
    sc.op("pool", lambda e: e.memset(c.ones[:], 1.0), writes=[c.r_const])
    return WRing(sc, c.wslab, nwslab)


def cast_weight(sc, dst, src, r_dst, rows_per_dma=1024):
    K, N = src.shape
    b = max(d for d in range(1, 1025) if N % d == 0)
    a = N // b
    sv = src.rearrange("r (a b) -> (r a) b", b=b)
    dv = dst.rearrange("r (a b) -> (r a) b", b=b)
    rows = K * a
    for r0 in range(0, rows, rows_per_dma):
        r1 = min(rows, r0 + rows_per_dma)
        sc.dma("pool", lambda e, r0=r0, r1=r1: e.dma_start(out=dv[r0:r1, :], in_=sv[r0:r1, :]), writes=[r_dst])


def build_A():
    import contextlib
    nc = bass.Bass("TRN2", target_bir_lowering=False)
    sc = Sched()
    c = Ctx()
    din = lambda name, shape, dt=F32: nc.dram_tensor(name, shape, dt, kind="ExternalInput").ap()
    xT_d = din("xT", [D, TOK])
    vecs_d = din("vecs", [128, 4, 8])
    w_in_d = din("w_in", [D, 2 * D])
    w_spT_d = din("w_spT", [128, 8, 128])
    tril_d = din("trilT", [128, 128])
    bsp_d = din("bsp", [128, 8, 128])
    w_out_d = din("w_out", [D, D])
    w_gu_d = din("w_gu", [D, 2 * DFF])
    w_dn_d = din("w_dn", [DFF, D])
    w_kv_d = din("w_kv", [D, 2 * D])
    x1T_o = nc.dram_tensor("x1T", [D, TOK], F32, kind="ExternalOutput").ap()
    kT_o = nc.dram_tensor("kT", [D, TOK], BF16, kind="ExternalOutput").ap()
    v_o = nc.dram_tensor("v", [TOK, D], BF16, kind="ExternalOutput").ap()
    dint = lambda name, shape: nc.dram_tensor(name, shape, BF16, kind="Internal").ap()
    if STAGES["dbg"]:
        dbg_u = nc.dram_tensor("dbg_u", [128, 8, T], BF16, kind="ExternalOutput").ap()
        dbg_vn = nc.dram_tensor("dbg_vn", [128, 8, T], BF16, kind="ExternalOutput").ap()
        dbg_uz = nc.dram_tensor("dbg_uz", [128, 8, T], BF16, kind="ExternalOutput").ap()
        dbg_z = nc.dram_tensor("dbg_z", [128, 8, T], F32, kind="ExternalOutput").ap()
        dbg_vf = nc.dram_tensor("dbg_vf", [128, 4, 1024], F32, kind="ExternalOutput").ap()
        dbg_mv = nc.dram_tensor("dbg_mv", [128, 4, 4], F32, kind="ExternalOutput").ap()
    w_in_b = dint("w_in_b", [D, 2 * D])
    w_out_b = dint("w_out_b", [D, D])
    w_gu_b = dint("w_gu_b", [D, 2 * DFF])
    w_dn_b = dint("w_dn_b", [DFF, D])
    w_kv_b = dint("w_kv_b", [D, 2 * D])

    with contextlib.ExitStack() as es:
        wr = alloc_common(nc, es, c, sc, nwslab=3)
        vecs = es.enter_context(nc.sbuf_tensor("s_vecs", [128, 4, 8], F32))
        wspT_f = c.tmp[:].rearrange("p a (b c) -> p (a b) c", c=128)
        tril = es.enter_context(nc.sbuf_tensor("s_tril", [128, 128], F32))
        wcT = es.enter_context(nc.sbuf_tensor("s_wcT", [128, 8, 128], BF16))
        bsp = es.enter_context(nc.sbuf_tensor("s_bsp", [128, 8, 128], F32))
        vstat = es.enter_context(nc.sbuf_tensor("s_vstat", [128, 2, 6], F32))
        vmv = es.enter_context(nc.sbuf_tensor("s_vmv", [128, 4], F32))
        nh1 = c.nh1
        r_vf = Res()
        r_vstat = Res()
        r_vmv = Res()
        r_w = {n: Res() for n in ("in", "out", "gu", "dn", "kv")}

        rc = c.r_const
        r_c2, r_c3, r_c4 = Res(), Res(), Res()
        sc.dma("sp", lambda e: e.dma_start(out=vecs[:], in_=vecs_d[:, :, :]), writes=[rc])
        sc.dma("sp", lambda e: e.dma_start(out=wspT_f, in_=w_spT_d[:, :, :]), writes=[rc] + c.r_tmp)
        sc.dma("sp", lambda e: e.dma_start(out=tril[:], in_=tril_d[:, :]), writes=[r_c2])
        sc.dma("sp", lambda e: e.dma_start(out=bsp[:], in_=bsp_d[:, :, :]), writes=[r_c3])
        for g in range(8):
            sc.op("dve", lambda e, g=g: e.tensor_tensor(out=wcT[:, g, :], in0=wspT_f[:, g, :], in1=tril[:], op=ALU.mult),
                  reads=[r_c2] + c.r_tmp, writes=[r_c4])
        xTv = xT_d.rearrange("(k p) t -> p k t", p=128)
        for t in range(NT):
            sc.dma("sp", lambda e, t=t: e.dma_start(out=c.xT[:, :, t * T:(t + 1) * T], in_=xTv[:, :, t * T:(t + 1) * T]),
                   writes=[c.r_x[k][t] for k in range(8)])
        cast_weight(sc, w_in_b, w_in_d, r_w["in"])
        cast_weight(sc, w_out_b, w_out_d, r_w["out"])
        cast_weight(sc, w_gu_b, w_gu_d, r_w["gu"])
        cast_weight(sc, w_dn_b, w_dn_d, r_w["dn"])
        cast_weight(sc, w_kv_b, w_kv_d, r_w["kv"])

        x1Tv = x1T_o.rearrange("(k p) t -> p k t", p=128)
        kTv = kT_o.rearrange("(k p) t -> p k t", p=128)
        vov = v_o.rearrange("(c p) f -> p c f", p=128)
        vf = c.arena[:, 16:20, :].rearrange("p a b -> p (a b)").bitcast(F32)
        r_vfslots = c.r_ar[16:20]

        def emit_gmlp(t, ts):
            emit_norm(sc, c, t, vecs[:, 0, :])

            def evac_u(n, bank):
                sc.op("act", lambda e, n=n, bank=bank: e.activation(out=c.arena[:, n, :], in_=c.ps[:, bank, :], func=AF.Gelu_apprx_tanh),
                      reads=[c.r_ps[bank]], writes=[c.r_ar[n]])
            emit_proj_fm(sc, c, wr, w_in_b, r_w["in"], 0, D, evac_u)
            if STAGES["dbg"] and t == 0:
                sc.dma("pool", lambda e: e.dma_start(out=dbg_u[:, :, :], in_=c.arena[:, 0:8, :]), reads=c.r_ar[0:8])

            def evac_v(ch, b0):
                for hf in range(2):
                    sc.op("act", lambda e, hf=hf, b0=b0: e.activation(out=vf[:, hf * 512:(hf + 1) * 512], in_=c.ps[:, b0 + hf, :], func=AF.Gelu_apprx_tanh),
                          reads=[c.r_ps[b0 + hf]], writes=r_vfslots[2 * hf:2 * hf + 2])
                for hf in range(2):
                    sc.op("dve", lambda e, hf=hf: e.bn_stats(out=vstat[:, hf, :], in_=vf[:, hf * 512:(hf + 1) * 512]),
                          reads=r_vfslots[2 * hf:2 * hf + 2], writes=[r_vstat])
                sc.op("dve", lambda e: e.bn_aggr(out=vmv[:, 0:2], in_=vstat[:].rearrange("p a b -> p (a b)")), reads=[r_vstat], writes=[r_vmv])
                sc.op("dve", lambda e: e.tensor_scalar(out=vmv[:, 2:3], in0=vmv[:, 1:2], scalar1=EPS, scalar2=None, op0=ALU.add),
                      reads=[r_vmv], writes=[r_vmv])
                sc.op("pool", lambda e: e.tensor_tensor(out=vmv[:, 3:4], in0=vmv[:, 2:3], in1=nh1[:], op=ALU.pow),
                      reads=[r_vmv, rc], writes=[r_vmv])
                if STAGES["dbg"] and t == 0:
                    sc.dma("pool", lambda e, ch=ch: e.dma_start(out=dbg_vf[:, ch, :], in_=vf), reads=r_vfslots)
                    sc.dma("pool", lambda e, ch=ch: e.dma_start(out=dbg_mv[:, ch, :], in_=vmv[:]), reads=[r_vmv])
                vn = c.arena[:, 8 + 2 * ch:10 + 2 * ch, :].rearrange("p a b -> p (a b)")
                sc.op("dve", lambda e, vn=vn: e.tensor_scalar(out=vn, in0=vf, scalar1=vmv[:, 0:1], scalar2=vmv[:, 3:4], op0=ALU.subtract, op1=ALU.mult),
                      reads=[r_vmv] + r_vfslots, writes=c.r_ar[8 + 2 * ch:10 + 2 * ch])
            emit_proj_tm(sc, c, wr, w_in_b, r_w["in"], D, evac_v)

            if STAGES["dbg"] and t == 0:
                sc.dma("pool", lambda e: e.dma_start(out=dbg_vn[:, :, :], in_=c.arena[:, 8:16, :]), reads=c.r_ar[8:16])
            for g in range(8):
                bank = c.mm_rot % 4
                c.mm_rot += 1
                for ch in range(4):
                    vn = c.arena[:, 8 + 2 * ch:10 + 2 * ch, :].rearrange("p a b -> p (a b)")
                    sc.op("pe", lambda e, g=g, ch=ch, bank=bank, vn=vn: e.matmul(c.ps[:, bank, ch * 128:(ch + 1) * 128], lhsT=vn[:, g * 128:(g + 1) * 128], rhs=wcT[:, g, :],
                                                                                  start=True, stop=True),
                          reads=c.r_ar[8 + 2 * ch:10 + 2 * ch] + [r_c4], writes=[c.r_ps[bank]], inc=(ch == 3))
                sb = g % 2
                sc.op("dve", lambda e, g=g, bank=bank, sb=sb: e.scalar_tensor_tensor(
                    out=c.tmp[:, sb, :].rearrange("p (a b) -> p a b", a=4), in0=c.ps[:, bank, :].rearrange("p (a b) -> p a b", a=4),
                    scalar=vecs[:, 1, g:g + 1], in1=bsp[:, g, :].unsqueeze(1).broadcast_to([128, 4, 128]), op0=ALU.mult, op1=ALU.add),
                    reads=[c.r_ps[bank], rc, r_c3], writes=[c.r_tmp[sb]])
                if STAGES["dbg"] and t == 0:
                    sc.dma("pool", lambda e, g=g, sb=sb: e.dma_start(out=dbg_z[:, g, :], in_=c.tmp[:, sb, :]), reads=[c.r_tmp[sb]])
                sc.op("dve", lambda e, g=g, sb=sb: e.tensor_tensor(out=c.arena[:, g, :], in0=c.tmp[:, sb, :], in1=c.arena[:, g, :], op=ALU.mult),
                      reads=[c.r_tmp[sb], c.r_ar[g]], writes=[c.r_ar[g]])

            if STAGES["dbg"] and t == 0:
                sc.dma("pool", lambda e: e.dma_start(out=dbg_uz[:, :, :], in_=c.arena[:, 0:8, :]), reads=c.r_ar[0:8])

            def evac_res(n, bank):
                sc.op("dve", lambda e, n=n, bank=bank: e.tensor_tensor(out=c.xT[:, n, ts], in0=c.xT[:, n, ts], in1=c.ps[:, bank, :], op=ALU.add),
                      reads=[c.r_ps[bank], c.r_x[n][t]], writes=[c.r_x[n][t]])
            emit_proj_fm(sc, c, wr, w_out_b, r_w["out"], 0, D, evac_res, src=c.arena, r_src=c.r_ar)

        def emit_tail(t, ts):
            sc.dma("pool", lambda e, ts=ts: e.dma_start(out=x1Tv[:, :, ts], in_=c.xT[:, :, ts]), reads=[c.r_x[k][t] for k in range(8)])

            emit_norm(sc, c, t, vecs[:, 3, :])

            def evac_k(n, bank):
                sc.op("act", lambda e, n=n, bank=bank: e.copy(out=c.arena[:, n, :], in_=c.ps[:, bank, :]),
                      reads=[c.r_ps[bank]], writes=[c.r_ar[n]])
            emit_proj_fm(sc, c, wr, w_kv_b, r_w["kv"], 0, D, evac_k)
            sc.dma("pool", lambda e, ts=ts: e.dma_start(out=kTv[:, :, ts], in_=c.arena[:, 0:8, :]), reads=c.r_ar[0:8])

            def evac_vv(ch, b0):
                vst = c.arena[:, 8 + 2 * ch:10 + 2 * ch, :].rearrange("p a b -> p (a b)")
                sc.op("act", lambda e, vst=vst, b0=b0: e.copy(out=vst[:, 0:512], in_=c.ps[:, b0, :]),
                      reads=[c.r_ps[b0]], writes=[c.r_ar[8 + 2 * ch]])
                sc.op("dve", lambda e, vst=vst, b0=b0: e.tensor_copy(out=vst[:, 512:1024], in_=c.ps[:, b0 + 1, :]),
                      reads=[c.r_ps[b0 + 1]], writes=[c.r_ar[9 + 2 * ch]])
            emit_proj_tm(sc, c, wr, w_kv_b, r_w["kv"], D, evac_vv)
            sc.dma("pool", lambda e, t=t: e.dma_start(out=vov[:, 4 * t:4 * t + 4, :],
                                                      in_=c.arena[:, 8:16, :].rearrange("p (c a) b -> p c (a b)", a=2)),
                   reads=c.r_ar[8:16])

        for t in range(1 if STAGES["dbg"] else NT):
            ts = slice(t * T, (t + 1) * T)
            if STAGES["gmlp"]:
                emit_gmlp(t, ts)
            if STAGES["ffn"]:
                emit_norm(sc, c, t, vecs[:, 2, :])
                emit_ffn(sc, c, t, wr, w_gu_b, w_dn_b, r_w["gu"], r_w["dn"])
            emit_tail(t, ts)

        sc.finish("sp")
        sc.replay(nc)
    return nc, sc


def prep_A(inputs, core):
    b, role = core // 2, core % 2
    f = np.float32
    x = np.asarray(inputs["x"])
    toks = np.concatenate([np.arange(tile_index(role, j) * T, (tile_index(role, j) + 1) * T) for j in range(NT)])
    xT = np.ascontiguousarray(x[b][toks].T)
    col = lambda v: np.ascontiguousarray(np.asarray(v, f).reshape(8, 128).T)
    vecs = np.stack([col(inputs["a_norm"][0]), col(inputs["a_v_norm"][0]), col(inputs["ffn_norm"][0]), col(inputs["kv_norm"])], axis=1)
    w_spT = np.ascontiguousarray(np.transpose(np.asarray(inputs["a_w_sp"][0], f), (2, 0, 1)))
    trilT = np.triu(np.ones((128, 128), f))
    bsp = np.ascontiguousarray(np.broadcast_to(np.asarray(inputs["a_b_sp"][0], f)[None], (128, 8, 128)))
    return {
        "xT": xT, "vecs": np.ascontiguousarray(vecs), "w_in": np.asarray(inputs["a_w_in"][0], f),
        "w_spT": w_spT, "trilT": trilT, "bsp": bsp, "w_out": np.asarray(inputs["a_w_out"][0], f),
        "w_gu": np.asarray(inputs["ffn_w_gu"][0], f), "w_dn": np.asarray(inputs["ffn_w_down"][0], f),
        "w_kv": np.asarray(inputs["kv_w"], f),
    }


_CACHE = {}


def run_A(inputs):
    if "A" not in _CACHE:
        _CACHE["A"] = build_A()[0]
    nc = _CACHE["A"]
    in_maps = [prep_A(inputs, cidx) for cidx in range(8)]
    res = run_bass_kernel_spmd(nc, in_maps, core_ids=list(range(8)))
    return res.results


def build_B():
    import contextlib
    nc = bass.Bass("TRN2", target_bir_lowering=False)
    sc = Sched()
    c = Ctx()
    din = lambda name, shape, dt=F32: nc.dram_tensor(name, shape, dt, kind="ExternalInput").ap()
    x1T_d = din("x1T", [D, TOK])
    vecs_d = din("vecs", [128, 3, 8])
    w_q_d = din("w_q", [D, D])
    w_o_d = din("w_o", [D, D])
    w_gu_d = din("w_gu", [D, 2 * DFF])
    w_dn_d = din("w_dn", [DFF, D])
    kd_d = [din("kd_e", [NH, KROWS, 2, S], BF16), din("kd_o", [NH, KROWS, 2, S], BF16)]
    vd_d = [din("vd_e", [NH, 128, 64, 128], BF16), din("vd_o", [NH, 128, 64, 128], BF16)]
    qaug_d = din("qaug", [NT, 4, NH, T], BF16)
    masks_d = din("masks", [128, 4, T], BF16)
    ident_d = din("ident", [128, 128], BF16)
    lamb_d = din("lamb", [128, 256])
    subln_d = din("subln", [128, 1])
    outT_o = nc.dram_tensor("outT", [D, TOK], F32, kind="ExternalOutput").ap()
    if STAGES["dbgB"]:
        dbg_on = nc.dram_tensor("dbg_on", [128, 8, T], BF16, kind="ExternalOutput").ap()
        dbg_q = nc.dram_tensor("dbg_q", [128, 16, T], BF16, kind="ExternalOutput").ap()
        dbg_sm = nc.dram_tensor("dbg_sm", [128, 8], F32, kind="ExternalOutput").ap()
    dint = lambda name, shape: nc.dram_tensor(name, shape, BF16, kind="Internal").ap()
    w_q_b = dint("w_q_b", [D, D])
    w_o_b = dint("w_o_b", [D, D])
    w_gu_b = dint("w_gu_b", [D, 2 * DFF])
    w_dn_b = dint("w_dn_b", [DFF, D])

    with contextlib.ExitStack() as es:
        wr = alloc_common(nc, es, c, sc, nwslab=2)
        vecs = es.enter_context(nc.sbuf_tensor("s_vecs", [128, 3, 8], F32))
        kring = es.enter_context(nc.sbuf_tensor("s_kring", [KROWS, 2, 2, 1024], BF16))
        vring = es.enter_context(nc.sbuf_tensor("s_vring", [128, 2, 8, 128], BF16))
        masks = es.enter_context(nc.sbuf_tensor("s_masks", [128, 4, T], BF16))
        ident = es.enter_context(nc.sbuf_tensor("s_ident", [128, 128], BF16))
        lamb = es.enter_context(nc.sbuf_tensor("s_lamb", [128, 256], F32))
        sm = es.enter_context(nc.sbuf_tensor("s_sm", [128, 8], F32))
        r_kv = [Res(), Res()]
        r_qaug = Res()
        r_pt = c.r_ar[16:20]
        r_sm = Res()
        r_w = {n: Res() for n in ("q", "o", "gu", "dn")}
        rc = c.r_const
        r_c2, r_c3 = Res(), Res()
        Qt = c.arena[:, 0:16, :].rearrange("p (h i) t -> p h i t", i=2)
        onT, r_on = c.sq, c.r_sq

        sc.dma("sp", lambda e: e.dma_start(out=vecs[:], in_=vecs_d[:, :, :]), writes=[rc])
        sc.dma("sp", lambda e: e.dma_start(out=masks[:], in_=masks_d[:, :, :]), writes=[r_c2])
        r_c5 = Res()
        sc.dma("sp", lambda e: e.dma_start(out=ident[:], in_=ident_d[:, :]), writes=[r_c5])
        sc.dma("sp", lambda e: e.dma_start(out=lamb[:], in_=lamb_d[:, :]), writes=[r_c3])
        r_c4 = Res()
        sc.dma("sp", lambda e: e.dma_start(out=sm[:, 7:8], in_=subln_d[:, :]), writes=[r_c4])
        x1Tv = x1T_d.rearrange("(k p) t -> p k t", p=128)
        for t in range(NT):
            sc.dma("sp", lambda e, t=t: e.dma_start(out=c.xT[:, :, t * T:(t + 1) * T], in_=x1Tv[:, :, t * T:(t + 1) * T]),
                   writes=[c.r_x[k][t] for k in range(8)])
        cast_weight(sc, w_q_b, w_q_d, r_w["q"])
        cast_weight(sc, w_o_b, w_o_d, r_w["o"])
        cast_weight(sc, w_gu_b, w_gu_d, r_w["gu"])
        cast_weight(sc, w_dn_b, w_dn_d, r_w["dn"])
        scr = c.tmp[:, 0, 0:64]
        sc.op("dve", lambda e: e.scalar_tensor_tensor(out=scr, in0=lamb[:, 0:64], scalar=1.0, in1=lamb[:, 64:128], op0=ALU.mult, op1=ALU.mult, accum_out=sm[:, 0:1]),
              reads=[r_c3], writes=[r_sm, c.r_tmp[0]])
        sc.op("dve", lambda e: e.scalar_tensor_tensor(out=scr, in0=lamb[:, 128:192], scalar=1.0, in1=lamb[:, 192:256], op0=ALU.mult, op1=ALU.mult, accum_out=sm[:, 1:2]),
              reads=[r_c3, r_sm], writes=[r_sm, c.r_tmp[0]])
        sc.op("act", lambda e: e.activation(out=sm[:, 2:4], in_=sm[:, 0:2], func=AF.Exp), reads=[r_sm], writes=[r_sm])
        sc.op("dve", lambda e: e.tensor_tensor(out=sm[:, 4:5], in0=sm[:, 3:4], in1=sm[:, 2:3], op=ALU.subtract), reads=[r_sm], writes=[r_sm])
        sc.op("dve", lambda e: e.tensor_scalar(out=sm[:, 4:5], in0=sm[:, 4:5], scalar1=-LAMBDA_INIT, scalar2=None, op0=ALU.add), reads=[r_sm], writes=[r_sm])
        sc.op("dve", lambda e: e.tensor_scalar(out=sm[:, 5:6], in0=sm[:, 7:8], scalar1=1.0 - LAMBDA_INIT, scalar2=None, op0=ALU.mult), reads=[r_sm, r_c4], writes=[r_sm])

        outTv = outT_o.rearrange("(k p) t -> p k t", p=128)
        strot = [0]
        ptrot = [0]
        kvn = [0]

        def load_kv(var, h, cp):
            b = kvn[0] % 2
            kvn[0] += 1
            r = r_kv[b]
            sc.dma("sp", lambda e, b=b: e.dma_start(out=kring[:, b], in_=kd_d[var][h, :, :, cp * 1024:(cp + 1) * 1024]), writes=[r])
            sc.dma("sp", lambda e, b=b: e.dma_start(out=vring[:, b], in_=vd_d[var][h, :, cp * 8:(cp + 1) * 8, :]), writes=[r])
            return b, r

        def emit_attention(j, ts):
            var = j % 2
            chunks = [(h, ci) for h in range(NH) for ci in range(j + 1)]
            loaded = {}

            def ensure(idx):
                if idx < len(chunks) and idx not in loaded:
                    h, ci = chunks[idx]
                    loaded[idx] = load_kv(var, h, j - ci)
            steps = []
            for idx, (h, ci) in enumerate(chunks):
                for o in range(8):
                    for i in range(2):
                        steps.append((idx, h, ci, o, i))
            nsteps_h = (j + 1) * 16
            pend = []
            LAG = 2

            def emit_av(item):
                (idx, h, ci, o, i, pt, first, last) = item
                b, rkv = loaded[idx]
                sc.op("pe", lambda e, b=b, o=o, i=i, pt=pt, first=first, last=last: e.matmul(c.ps[:, 4 + i, :], lhsT=vring[:, b, o, :], rhs=c.arena[:, 16 + pt, :], start=first, stop=last),
                      reads=[rkv, r_pt[pt]], writes=[c.r_ps[4 + i]], inc=False)
                sc.op("pe", lambda e, i=i, pt=pt, first=first, last=last: e.matmul(c.ps[:, 6 + i, :], lhsT=c.ones[:], rhs=c.arena[:, 16 + pt, :], start=first, stop=last),
                      reads=[r_pt[pt], rc], writes=[c.r_ps[6 + i]], inc=True)
                if last and i == 1:
                    emit_head_post(h)

            def emit_head_post(h):
                a_, b_ = c.tmp[:, 0, :], c.tmp[:, 1, :]
                sc.op("dve", lambda e: e.reciprocal(out=c.ms[:], in_=c.ps[:, 6, :]), reads=[c.r_ps[6]], writes=[c.r_ms])
                sc.op("dve", lambda e: e.tensor_tensor(out=a_, in0=c.ps[:, 4, :], in1=c.ms[:], op=ALU.mult), reads=[c.r_ps[4], c.r_ms], writes=[c.r_tmp[0]])
                sc.op("dve", lambda e: e.reciprocal(out=c.rstd[:], in_=c.ps[:, 7, :]), reads=[c.r_ps[7]], writes=[c.r_rstd])
                sc.op("dve", lambda e: e.tensor_tensor(out=b_, in0=c.ps[:, 5, :], in1=c.rstd[:], op=ALU.mult), reads=[c.r_ps[5], c.r_rstd], writes=[c.r_tmp[1]])
                sc.op("dve", lambda e: e.scalar_tensor_tensor(out=a_, in0=b_, scalar=sm[:, 4:5], in1=a_, op0=ALU.mult, op1=ALU.add),
                      reads=[c.r_tmp[1], c.r_tmp[0], r_sm], writes=[c.r_tmp[0]])
                sc.op("dve", lambda e: e.tensor_tensor(out=c.arena[:, 20, :], in0=a_, in1=a_, op=ALU.mult), reads=[c.r_tmp[0]], writes=[c.r_ar[20]])
                bank = strot[0] % 4
                strot[0] += 1
                sc.op("pe", lambda e, bank=bank: e.matmul(c.ps[:, bank, :], lhsT=c.ones[:], rhs=c.arena[:, 20, :], start=True, stop=True),
                      reads=[c.r_ar[20], rc], writes=[c.r_ps[bank]])
                sc.op("act", lambda e, bank=bank: e.activation(out=c.ms[:], in_=c.ps[:, bank, :], func=AF.Sqrt, scale=1.0 / 128, bias=EPS),
                      reads=[c.r_ps[bank]], writes=[c.r_ms])
                sc.op("dve", lambda e: e.reciprocal(out=c.rstd[:], in_=c.ms[:]),
                      reads=[c.r_ms], writes=[c.r_rstd])
                sc.op("dve", lambda e, h=h: e.scalar_tensor_tensor(out=onT[:, h, :], in0=a_, scalar=sm[:, 5:6], in1=c.rstd[:], op0=ALU.mult, op1=ALU.mult),
                      reads=[c.r_tmp[0], c.r_rstd, r_sm], writes=[r_on[h]])

            for sidx, (idx, h, ci, o, i) in enumerate(steps):
                if o == 0 and i == 0:
                    ensure(idx)
                if o == 1 and i == 0:
                    ensure(idx + 1)
                b, rkv = loaded[idx]
                bank = strot[0] % 4
                strot[0] += 1
                pt = ptrot[0] % 4
                ptrot[0] += 1
                diag = (ci == 0 and o >= 4)
                sc.op("pe", lambda e, b=b, h=h, o=o, i=i, bank=bank, diag=diag: e.matmul(c.ps[:, bank, :], lhsT=kring[0:68, b, i, o * 128:(o + 1) * 128], rhs=Qt[0:68, h, i, :], start=True, stop=(not diag)),
                      reads=[rkv, c.r_ar[2 * h + i], r_qaug], writes=[c.r_ps[bank]], inc=(not diag))
                if diag:
                    dd = o - 4
                    sc.op("pe", lambda e, bank=bank, dd=dd: e.matmul(c.ps[:, bank, :], lhsT=ident[:], rhs=masks[:, dd, :], start=False, stop=True),
                          reads=[r_c2, r_c5], writes=[c.r_ps[bank]])
                sc.op("act", lambda e, bank=bank, pt=pt: e.activation(out=c.arena[:, 16 + pt, :], in_=c.ps[:, bank, :], func=AF.Exp, scale=0.125),
                      reads=[c.r_ps[bank]], writes=[r_pt[pt]])
                hs = sidx - h * nsteps_h
                first = hs < 2
                last = hs >= nsteps_h - 2
                pend.append((idx, h, ci, o, i, pt, first, last))
                if len(pend) > LAG:
                    emit_av(pend.pop(0))
            while pend:
                emit_av(pend.pop(0))

        def emit_tile(t):
            ts = slice(t * T, (t + 1) * T)
            emit_norm(sc, c, t, vecs[:, 0, :])
            for i in range(2):
                sc.dma("sp", lambda e, t=t, i=i: e.dma_start(out=Qt[64:68, :, i, :], in_=qaug_d[t, :, :, :]), writes=[r_qaug] + c.r_ar[0:16])

            def evac_q(h, bank):
                sc.op("act", lambda e, h=h, bank=bank: e.copy(out=Qt[0:64, h, 0, :], in_=c.ps[0:64, bank, :]), reads=[c.r_ps[bank]], writes=[c.r_ar[2 * h]])
                sc.op("act", lambda e, h=h, bank=bank: e.copy(out=Qt[0:64, h, 1, :], in_=c.ps[64:128, bank, :]), reads=[c.r_ps[bank]], writes=[c.r_ar[2 * h + 1]])
            emit_proj_fm(sc, c, wr, w_q_b, r_w["q"], 0, D, evac_q)
            if STAGES["dbgB"] and t == 0:
                sc.dma("pool", lambda e: e.dma_start(out=dbg_q[:, :, :], in_=c.arena[:, 0:16, :]), reads=c.r_ar[0:16] + [r_qaug])
                sc.dma("pool", lambda e: e.dma_start(out=dbg_sm[:, :], in_=sm[:]), reads=[r_sm])
            if STAGES["attn"]:
                emit_attention(t, ts)
            if STAGES["dbgB"] and t == 0:
                sc.dma("pool", lambda e: e.dma_start(out=dbg_on[:, :, :], in_=onT[:]), reads=r_on)

            def evac_res(n, bank):
                sc.op("dve", lambda e, n=n, bank=bank: e.tensor_tensor(out=c.xT[:, n, ts], in0=c.xT[:, n, ts], in1=c.ps[:, bank, :], op=ALU.add),
                      reads=[c.r_ps[bank], c.r_x[n][t]], writes=[c.r_x[n][t]])
            if STAGES["attn"]:
                emit_proj_fm(sc, c, wr, w_o_b, r_w["o"], 0, D, evac_res, src=onT, r_src=r_on)
            if STAGES["ffnB"]:
                emit_norm(sc, c, t, vecs[:, 1, :])
                emit_ffn(sc, c, t, wr, w_gu_b, w_dn_b, r_w["gu"], r_w["dn"])
            for k in range(8):
                sc.op("dve", lambda e, k=k: e.tensor_tensor(out=c.sq[:, k, :], in0=c.xT[:, k, ts], in1=c.xT[:, k, ts], op=ALU.mult),
                      reads=[c.r_x[k][t]], writes=[c.r_sq[k]])
            bank = c.stat_bank
            for k in range(8):
                sc.op("pe", lambda e, k=k: e.matmul(c.ps[:, bank, :], lhsT=c.ones[:], rhs=c.sq[:, k, :], start=(k == 0), stop=(k == 7)),
                      reads=[c.r_sq[k], rc], writes=[c.r_ps[bank]], inc=(k == 7))
            sc.op("act", lambda e: e.activation(out=c.ms[:], in_=c.ps[:, bank, :], func=AF.Sqrt, scale=1.0 / D, bias=EPS),
                  reads=[c.r_ps[bank]], writes=[c.r_ms])
            sc.op("dve", lambda e: e.reciprocal(out=c.rstd[:], in_=c.ms[:]),
                  reads=[c.r_ms], writes=[c.r_rstd])
            for k in range(8):
                sb = k % 2
                sc.op("dve", lambda e, k=k, sb=sb: e.scalar_tensor_tensor(out=c.tmp[:, sb, :], in0=c.xT[:, k, ts], scalar=vecs[:, 2, k:k + 1], in1=c.rstd[:],
                                                                       op0=ALU.mult, op1=ALU.mult),
                      reads=[c.r_x[k][t], c.r_rstd, rc], writes=[c.r_tmp[sb]])
                sc.dma("pool", lambda e, k=k, sb=sb, ts=ts: e.dma_start(out=outTv[:, k, ts], in_=c.tmp[:, sb, :]), reads=[c.r_tmp[sb]])

        for t in range(1 if STAGES["dbgB"] else NT):
            emit_tile(t)

        sc.finish("sp")
        sc.replay(nc)
    return nc, sc


def alibi_slopes():
    return np.array([2.0 ** (-8.0 * (i + 1) / NH) for i in range(NH)], dtype=np.float64)


def prep_B(inputs, core, x1T, KT_full, V_full):
    b, role = core // 2, core % 2
    f = np.float32
    col = lambda v: np.ascontiguousarray(np.asarray(v, f).reshape(8, 128).T)
    vecs = np.stack([col(inputs["b_norm"][0]), col(inputs["ffn_norm"][1]), col(inputs["final_norm"])], axis=1)
    slopes = alibi_slopes()
    pos = np.arange(S)
    khi, klo = pos // 128, pos % 128
    out = {}
    K4 = KT_full.reshape(NH, 2, 64, S)
    V4 = V_full.reshape(64, 128, NH, 128)
    for par, name in ((0, "e"), (1, "o")):
        shifted = (tile_index(role, par) != 2 * par + 1)
        kd = np.zeros((NH, KROWS, 2, S), NPBF)
        vd = np.zeros((NH, 128, 64, 128), NPBF)
        aug = np.zeros((NH, 4, S), np.float64)
        aug[:, 0, :] = 1.0
        aug[:, 1, :] = 1.0
        if not shifted:
            kd[:, 0:64, :, :] = K4.transpose(0, 2, 1, 3)
            vd[:] = V4.transpose(2, 1, 0, 3)
            aug[:, 2, :] = slopes[:, None] * 128.0 * khi[None, :]
            aug[:, 3, :] = slopes[:, None] * klo[None, :]
        else:
            kd[:, 0:64, :, 512:] = K4.transpose(0, 2, 1, 3)[:, :, :, :S - 512]
            vd[:, :, 4:, :] = V4.transpose(2, 1, 0, 3)[:, :, :60, :]
            aug[:, 2, 512:] = slopes[:, None] * 128.0 * khi[None, :S - 512]
            aug[:, 3, 512:] = slopes[:, None] * klo[None, :S - 512]
            aug[:, 2, :512] = slopes[:, None] * 128.0 * (-200.0)
        kd[:, 64:68, 0, :] = aug.astype(NPBF)
        kd[:, 64:68, 1, :] = aug.astype(NPBF)
        out["kd_" + name] = kd
        out["vd_" + name] = vd
    qaug = np.zeros((NT, 4, NH, T), np.float64)
    for j in range(NT):
        qpos = tile_index(role, j) * T + np.arange(T)
        qhi, qlo = qpos // 128, qpos % 128
        qaug[j, 0] = -8.0 * slopes[:, None] * 128.0 * qhi[None, :]
        qaug[j, 1] = -8.0 * slopes[:, None] * qlo[None, :]
        qaug[j, 2] = 8.0
        qaug[j, 3] = 8.0
    kk = np.arange(128)[:, None, None]
    dd = np.arange(4)[None, :, None]
    qq = np.arange(T)[None, None, :]
    masks = np.where(qq - 128 * dd - kk >= 0, 0.0, -240000.0).astype(NPBF)
    out.update({
        "x1T": x1T, "vecs": np.ascontiguousarray(vecs), "w_q": np.asarray(inputs["b_w_q"][0], f), "w_o": np.asarray(inputs["b_w_o"][0], f),
        "w_gu": np.asarray(inputs["ffn_w_gu"][1], f), "w_dn": np.asarray(inputs["ffn_w_down"][1], f),
        "qaug": qaug.astype(NPBF), "masks": masks, "ident": np.eye(128, dtype=np.float32).astype(NPBF),
        "lamb": np.ascontiguousarray(np.broadcast_to(np.asarray(inputs["b_lambda"][0], f).reshape(1, 256), (128, 256))),
        "subln": np.ascontiguousarray(np.asarray(inputs["b_subln"][0], f).reshape(128, 1)),
    })
    return out


def run_B(inputs, resA):
    if "B" not in _CACHE:
        _CACHE["B"] = build_B()[0]
    nc = _CACHE["B"]
    in_maps = []
    for b in range(B):
        KT_full = np.zeros((D, S), NPBF)
        V_full = np.zeros((S, D), NPBF)
        for role in range(2):
            r = resA[2 * b + role]
            for j in range(NT):
                i = tile_index(role, j)
                KT_full[:, i * T:(i + 1) * T] = r["kT"][:, j * T:(j + 1) * T]
                V_full[i * T:(i + 1) * T, :] = r["v"][j * T:(j + 1) * T, :]
        for role in range(2):
            in_maps.append(prep_B(inputs, 2 * b + role, np.asarray(resA[2 * b + role]["x1T"]), KT_full, V_full))
    res = run_bass_kernel_spmd(nc, in_maps, core_ids=list(range(8)))
    return res.results


def kernel_unfused(**inputs):
    resA = run_A(inputs)
    resB = run_B(inputs, resA)
    out = np.zeros((B, S, D), np.float32)
    for core in range(8):
        b, role = core // 2, core % 2
        oT = np.asarray(resB[core]["outT"])
        for j in range(NT):
            i = tile_index(role, j)
            out[b, i * T:(i + 1) * T, :] = oT[:, j * T:(j + 1) * T].T
    return out


def sched_coll(sc, fn, reads=(), writes=()):
    toks = sc._deps(reads, writes)
    idx = sc.dn.setdefault("coll", 0) % 4
    sc.dn["coll"] += 1
    key = ("c", "pool", idx)
    cnt = sc.dcnt.get(key, 0)
    if cnt > 0:
        toks.append((key, cnt))
    sc._wait("pool", toks)
    sc.dcnt[key] = cnt + 1
    tok = (key, cnt + 1)
    sc.q["pool"].append(("op", fn, tok))
    sc._mark(tok, reads, writes)
    sc.n_inst += 1
    return tok


def build_F():
    import contextlib
    nc = bass.Bass("TRN2", target_bir_lowering=False)
    sc = Sched()
    c = Ctx()
    din = lambda name, shape, dt=F32: nc.dram_tensor(name, shape, dt, kind="ExternalInput").ap()
    xT_d = din("xT", [D, TOK])
    vecs_d = din("vecs", [128, 7, 8])
    w_in_d = din("w_in", [D, 2 * D])
    w_spT_d = din("w_spT", [128, 8, 128])
    tril_d = din("trilT", [128, 128])
    bsp_d = din("bsp", [128, 8, 128])
    w_out_d = din("w_out", [D, D])
    w_gu0_d = din("w_gu0", [D, 2 * DFF])
    w_dn0_d = din("w_dn0", [DFF, D])
    w_kv_d = din("w_kv", [D, 2 * D])
    w_q_d = din("w_q", [D, D])
    w_o_d = din("w_o", [D, D])
    w_gu1_d = din("w_gu1", [D, 2 * DFF])
    w_dn1_d = din("w_dn1", [DFF, D])
    kaug_d = din("kaug", [2, NH, 5, 2, S], BF16)
    qaug_d = din("qaug", [NT, 5, NH, T], BF16)
    wmask_d = din("wmask", [128, 1408], BF16)
    sel_d = din("sel", [128, 4, 128], BF16)
    lamb_d = din("lamb", [128, 256])
    subln_d = din("subln", [128, 1])
    outT_o = nc.dram_tensor("outT", [D, TOK], F32, kind="ExternalOutput").ap()
    dint = lambda name, shape: nc.dram_tensor(name, shape, BF16, kind="Internal").ap()
    wb = {}
    for name, src in (("in", w_in_d), ("out", w_out_d), ("gu0", w_gu0_d), ("dn0", w_dn0_d), ("kv", w_kv_d),
                      ("q", w_q_d), ("o", w_o_d), ("gu1", w_gu1_d), ("dn1", w_dn1_d)):
        wb[name] = (dint("wb_" + name, list(src.shape)), src)
    snd = [nc.dram_tensor("snd%d" % t, [2048, T], BF16) for t in range(NT)]
    gat = [nc.dram_tensor("gat%d" % t, [4096, T], BF16) for t in range(NT)]

    with contextlib.ExitStack() as es:
        wr = alloc_common(nc, es, c, sc, nwslab=2)
        vecs = es.enter_context(nc.sbuf_tensor("s_vecs", [128, 7, 8], F32))
        wmask = es.enter_context(nc.sbuf_tensor("s_wmask", [128, 1408], BF16))
        sel = es.enter_context(nc.sbuf_tensor("s_sel", [128, 4, 128], BF16))
        sm = es.enter_context(nc.sbuf_tensor("s_sm", [128, 8], F32))
        rc = c.r_const
        r_w = {n: Res() for n in wb}
        r_c2, r_c3, r_c4, r_c5, r_c6, r_c7 = [Res() for _ in range(6)]
        r_sm = Res()
        r_snd = [Res() for _ in range(NT)]
        r_gat = [Res() for _ in range(NT)]

        sc.dma("sp", lambda e: e.dma_start(out=vecs[:], in_=vecs_d[:, :, :]), writes=[rc])
        sc.dma("sp", lambda e: e.dma_start(out=wmask[:], in_=wmask_d[:, :]), writes=[r_c5])
        sc.dma("sp", lambda e: e.dma_start(out=sel[:], in_=sel_d[:, :, :]), writes=[r_c6])
        sc.dma("sp", lambda e: e.dma_start(out=sm[:, 7:8], in_=subln_d[:, :]), writes=[r_c7])
        lamb = c.tmp[:, 0, 0:256]
        scr = c.tmp[:, 1, 0:64]
        sc.dma("sp", lambda e: e.dma_start(out=lamb, in_=lamb_d[:, :]), writes=[c.r_tmp[0]])
        sc.op("dve", lambda e: e.scalar_tensor_tensor(out=scr, in0=lamb[:, 0:64], scalar=1.0, in1=lamb[:, 64:128], op0=ALU.mult, op1=ALU.mult, accum_out=sm[:, 0:1]),
              reads=[c.r_tmp[0]], writes=[r_sm, c.r_tmp[1]])
        sc.op("dve", lambda e: e.scalar_tensor_tensor(out=scr, in0=lamb[:, 128:192], scalar=1.0, in1=lamb[:, 192:256], op0=ALU.mult, op1=ALU.mult, accum_out=sm[:, 1:2]),
              reads=[c.r_tmp[0], r_sm], writes=[r_sm, c.r_tmp[1]])
        sc.op("act", lambda e: e.activation(out=sm[:, 2:4], in_=sm[:, 0:2], func=AF.Exp), reads=[r_sm], writes=[r_sm])
        sc.op("dve", lambda e: e.tensor_tensor(out=sm[:, 4:5], in0=sm[:, 3:4], in1=sm[:, 2:3], op=ALU.subtract), reads=[r_sm], writes=[r_sm])
        sc.op("dve", lambda e: e.tensor_scalar(out=sm[:, 4:5], in0=sm[:, 4:5], scalar1=-LAMBDA_INIT, scalar2=None, op0=ALU.add), reads=[r_sm], writes=[r_sm])
        sc.op("dve", lambda e: e.tensor_scalar(out=sm[:, 5:6], in0=sm[:, 7:8], scalar1=1.0 - LAMBDA_INIT, scalar2=None, op0=ALU.mult), reads=[r_sm, r_c7], writes=[r_sm])

        esA = es.enter_context(contextlib.ExitStack())
        tril = esA.enter_context(nc.sbuf_tensor("s_tril", [128, 128], F32))
        wcT = esA.enter_context(nc.sbuf_tensor("s_wcT", [128, 8, 128], BF16))
        bsp = esA.enter_context(nc.sbuf_tensor("s_bsp", [128, 8, 128], F32))
        vstat = esA.enter_context(nc.sbuf_tensor("s_vstat", [128, 2, 6], F32))
        vmv = esA.enter_context(nc.sbuf_tensor("s_vmv", [128, 4], F32))
        nh1 = c.nh1
        r_vstat, r_vmv = Res(), Res()
        wspT_f = c.tmp[:].rearrange("p a (b c) -> p (a b) c", c=128)
        sc.dma("sp", lambda e: e.dma_start(out=wspT_f, in_=w_spT_d[:, :, :]), writes=c.r_tmp)
        sc.dma("sp", lambda e: e.dma_start(out=tril[:], in_=tril_d[:, :]), writes=[r_c2])
        sc.dma("sp", lambda e: e.dma_start(out=bsp[:], in_=bsp_d[:, :, :]), writes=[r_c3])
        for g in range(8):
            sc.op("dve", lambda e, g=g: e.tensor_tensor(out=wcT[:, g, :], in0=wspT_f[:, g, :], in1=tril[:], op=ALU.mult),
                  reads=[r_c2] + c.r_tmp, writes=[r_c4])
        xTv = xT_d.rearrange("(k p) t -> p k t", p=128)
        def load_x(t):
            sc.dma("sp", lambda e, t=t: e.dma_start(out=c.xT[:, :, t * T:(t + 1) * T], in_=xTv[:, :, t * T:(t + 1) * T]),
                   writes=[c.r_x[k][t] for k in range(8)])
        load_x(0)
        for name in ("in", "out", "gu0", "dn0", "kv"):
            cast_weight(sc, wb[name][0], wb[name][1], r_w[name])

        vf = c.arena[:, 16:20, :].rearrange("p a b -> p (a b)").bitcast(F32)
        r_vfslots = c.r_ar[16:20]
        groups = [[0, 1], [2, 3], [4, 5], [6, 7]]
        deferred = []

        def emit_gmlp(t, ts):
            emit_norm(sc, c, t, vecs[:, 0, :])
            while deferred:
                deferred.pop(0)()

            def evac_v(ch, b0):
                for hf in range(2):
                    sc.op("act", lambda e, hf=hf, b0=b0: e.activation(out=vf[:, hf * 512:(hf + 1) * 512], in_=c.ps[:, b0 + hf, :], func=AF.Gelu_apprx_tanh),
                          reads=[c.r_ps[b0 + hf]], writes=r_vfslots[2 * hf:2 * hf + 2])
                for hf in range(2):
                    sc.op("dve", lambda e, hf=hf: e.bn_stats(out=vstat[:, hf, :], in_=vf[:, hf * 512:(hf + 1) * 512]),
                          reads=r_vfslots[2 * hf:2 * hf + 2], writes=[r_vstat])
                sc.op("dve", lambda e: e.bn_aggr(out=vmv[:, 0:2], in_=vstat[:].rearrange("p a b -> p (a b)")), reads=[r_vstat], writes=[r_vmv])
                sc.op("dve", lambda e: e.tensor_scalar(out=vmv[:, 2:3], in0=vmv[:, 1:2], scalar1=EPS, scalar2=None, op0=ALU.add),
                      reads=[r_vmv], writes=[r_vmv])
                sc.op("pool", lambda e: e.tensor_tensor(out=vmv[:, 3:4], in0=vmv[:, 2:3], in1=nh1[:], op=ALU.pow),
                      reads=[r_vmv, rc], writes=[r_vmv])
                vn = c.arena[:, 8 + 2 * ch:10 + 2 * ch, :].rearrange("p a b -> p (a b)")
                sc.op("dve", lambda e, vn=vn: e.tensor_scalar(out=vn, in0=vf, scalar1=vmv[:, 0:1], scalar2=vmv[:, 3:4], op0=ALU.subtract, op1=ALU.mult),
                      reads=[r_vmv] + r_vfslots, writes=c.r_ar[8 + 2 * ch:10 + 2 * ch])
            emit_proj_tm(sc, c, wr, wb["in"][0], r_w["in"], D, evac_v)

            def evac_u(n, bank):
                sc.op("act", lambda e, n=n, bank=bank: e.activation(out=c.arena[:, n, :], in_=c.ps[:, bank, :], func=AF.Gelu_apprx_tanh),
                      reads=[c.r_ps[bank]], writes=[c.r_ar[n]])
                g = n
                zb = 4 + (g % 4)
                for ch in range(4):
                    vn = c.arena[:, 8 + 2 * ch:10 + 2 * ch, :].rearrange("p a b -> p (a b)")
                    sc.op("pe", lambda e, g=g, ch=ch, zb=zb, vn=vn: e.matmul(c.ps[:, zb, ch * 128:(ch + 1) * 128], lhsT=vn[:, g * 128:(g + 1) * 128], rhs=wcT[:, g, :],
                                                                              start=True, stop=True),
                          reads=c.r_ar[8 + 2 * ch:10 + 2 * ch] + [r_c4], writes=[c.r_ps[zb]], inc=(ch == 3))
                sb = g % 2
                sc.op("dve", lambda e, g=g, zb=zb, sb=sb: e.scalar_tensor_tensor(
                    out=c.tmp[:, sb, :].rearrange("p (a b) -> p a b", a=4), in0=c.ps[:, zb, :].rearrange("p (a b) -> p a b", a=4),
                    scalar=vecs[:, 1, g:g + 1], in1=bsp[:, g, :].unsqueeze(1).broadcast_to([128, 4, 128]), op0=ALU.mult, op1=ALU.add),
                    reads=[c.r_ps[zb], rc, r_c3], writes=[c.r_tmp[sb]])
                sc.op("dve", lambda e, g=g, sb=sb: e.tensor_tensor(out=c.arena[:, g, :], in0=c.tmp[:, sb, :], in1=c.arena[:, g, :], op=ALU.mult),
                      reads=[c.r_tmp[sb], c.r_ar[g]], writes=[c.r_ar[g]])
            emit_proj_fm(sc, c, wr, wb["in"][0], r_w["in"], 0, D, evac_u)

            def evac_res(n, bank):
                sc.op("dve", lambda e, n=n, bank=bank: e.tensor_tensor(out=c.xT[:, n, ts], in0=c.xT[:, n, ts], in1=c.ps[:, bank, :], op=ALU.add),
                      reads=[c.r_ps[bank], c.r_x[n][t]], writes=[c.r_x[n][t]])
            def evac_res_sq(n, bank):
                evac_res(n, bank)
                emit_square(sc, c, t, n)
            emit_proj_fm(sc, c, wr, wb["out"][0], r_w["out"], 0, D, evac_res_sq, src=c.arena, r_src=c.r_ar)

        def emit_kv(t, ts):
            emit_norm(sc, c, t, vecs[:, 3, :], squares_done=True)
            sndk = snd[t][0:1024, :].rearrange("(k p) t -> p k t", p=128)
            sndv = snd[t][1024:2048, :].rearrange("(h p) (c d) -> p c h d", p=128, d=128)

            def evac_k(n, bank):
                sc.op("act", lambda e, n=n, bank=bank: e.copy(out=c.arena[:, n, :], in_=c.ps[:, bank, :]),
                      reads=[c.r_ps[bank]], writes=[c.r_ar[n]])
            emit_proj_fm(sc, c, wr, wb["kv"][0], r_w["kv"], 0, D, evac_k, kouter=True)
            sc.dma("act", lambda e: e.dma_start(out=sndk, in_=c.arena[:, 0:8, :]), reads=c.r_ar[0:8], writes=[r_snd[t]])

            def evac_vv(ch, b0):
                vst = c.arena[:, 8 + 2 * ch:10 + 2 * ch, :].rearrange("p a b -> p (a b)")
                sc.op("act", lambda e, vst=vst, b0=b0: e.copy(out=vst[:, 0:512], in_=c.ps[:, b0, :]),
                      reads=[c.r_ps[b0]], writes=[c.r_ar[8 + 2 * ch]])
                sc.op("dve", lambda e, vst=vst, b0=b0: e.tensor_copy(out=vst[:, 512:1024], in_=c.ps[:, b0 + 1, :]),
                      reads=[c.r_ps[b0 + 1]], writes=[c.r_ar[9 + 2 * ch]])
            emit_proj_tm(sc, c, wr, wb["kv"][0], r_w["kv"], D, evac_vv)
            for ch in range(4):
                sc.dma("act", lambda e, ch=ch: e.dma_start(out=sndv[:, ch], in_=c.arena[:, 8 + 2 * ch:10 + 2 * ch, :].rearrange("p a (h d) -> p (a h) d", d=128)),
                       reads=c.r_ar[8 + 2 * ch:10 + 2 * ch], writes=[r_snd[t]])

            def do_gather(t=t):
                sched_coll(sc, lambda e, t=t: e.collective_compute("AllGather", ALU.bypass, replica_groups=groups,
                                                                   ins=[snd[t].ap().opt()], outs=[gat[t].ap().opt()]),
                           reads=[r_snd[t]], writes=[r_gat[t]])
            deferred.append(do_gather)

        for t in range(NT):
            ts = slice(t * T, (t + 1) * T)
            emit_gmlp(t, ts)
            if t == NT - 1:
                snap_a = {e_: sc.cnt[e_] for e_ in COMPUTE if sc.cnt[e_] > 0}
                snap_a.update({k_: (16 * v_ if k_[0] == "d" else v_) for k_, v_ in sc.dcnt.items()})
            if t + 1 < NT:
                load_x(t + 1)
            if t == 1:
                for name in ("q", "o", "gu1", "dn1"):
                    cast_weight(sc, wb[name][0], wb[name][1], r_w[name])
            emit_norm(sc, c, t, vecs[:, 2, :], squares_done=True)
            emit_ffn(sc, c, t, wr, wb["gu0"][0], wb["dn0"][0], r_w["gu0"], r_w["dn0"], post_add=lambda nn, t=t: emit_square(sc, c, t, nn))
            emit_kv(t, ts)
        while deferred:
            deferred.pop(0)()

        esA.close()
        NKV = 4
        kring = es.enter_context(nc.sbuf_tensor("s_kring", [69, NKV, 2, 512], BF16))
        vring = es.enter_context(nc.sbuf_tensor("s_vring", [128, NKV, 4, 128], BF16))
        snapshot = snap_a
        r_kv = [Res() for _ in range(NKV)]
        for r_ in r_kv:
            r_.r = dict(snapshot)
        r_qaug = Res()
        r_pt = c.r_ar[16:20]
        Qt = c.arena[:, 0:16, :].rearrange("p (h i) t -> p h i t", i=2)
        onT, r_on = c.sq, c.r_sq
        outTv = outT_o.rearrange("(k p) t -> p k t", p=128)
        strot = [0]
        ptrot = [0]
        kvn = [0]

        def load_kv(h, cp, hf, dg):
            b = kvn[0] % NKV
            kvn[0] += 1
            r = r_kv[b]
            ranks = (0, 1) if cp % 2 == 0 else (1, 0)
            rk = ranks[hf]
            g_ = gat[cp]
            ksrc = g_[rk * 2048 + h * 128:rk * 2048 + (h + 1) * 128, :].rearrange("(i d) t -> d i t", d=64)
            sc.dma("sp", lambda e, b=b, ksrc=ksrc: e.dma_start(out=kring[0:64, b, :, :], in_=ksrc), reads=[r_gat[cp]], writes=[r])
            vsrc = g_[rk * 2048 + 1024 + h * 128:rk * 2048 + 1024 + (h + 1) * 128, :].rearrange("p (c d) -> p c d", d=128)
            sc.dma("sp", lambda e, b=b, vsrc=vsrc: e.dma_start(out=vring[:, b, :, :], in_=vsrc), reads=[r_gat[cp]], writes=[r])
            sc.dma("sp", lambda e, b=b: e.dma_start(out=kring[64:69, b, :, :], in_=kaug_d[dg, h, :, :, cp * 1024 + hf * 512:cp * 1024 + (hf + 1) * 512]), writes=[r])
            return b, r

        def emit_attention(j):
            par = j % 2
            chunks = [(h, ci, hf) for h in range(NH) for ci in range(j + 1) for hf in range(2)]
            loaded = {}

            def ensure(idx):
                if idx < len(chunks) and idx not in loaded:
                    h, ci, hf = chunks[idx]
                    loaded[idx] = load_kv(h, j - ci, hf, 1 if ci == 0 else 0)
            steps = []
            for idx, (h, ci, hf) in enumerate(chunks):
                for o4 in range(4):
                    for i in range(2):
                        steps.append((idx, h, ci, hf * 4 + o4, i))
            nsteps_h = (j + 1) * 16
            pend = []
            LAG = 2

            def emit_av(item):
                (idx, h, ci, o, i, pt, first, last) = item
                b, rkv = loaded[idx]
                sc.op("pe", lambda e, b=b, o=o, i=i, pt=pt, first=first, last=last: e.matmul(c.ps[:, 4 + i, :], lhsT=vring[:, b, o % 4, :], rhs=c.arena[:, 16 + pt, :], start=first, stop=last),
                      reads=[rkv, r_pt[pt]], writes=[c.r_ps[4 + i]], inc=True)
                if i == 0:
                    sc.op("pe", lambda e, i=i, pt=pt, first=first, last=last: e.matmul(c.ps[:, 6 + i, :], lhsT=c.ones[:], rhs=c.arena[:, 16 + pt, :], start=first, stop=last),
                          reads=[r_pt[pt], rc], writes=[c.r_ps[6 + i]], inc=True)
                elif first:
                    sc.op("dve", lambda e, i=i, pt=pt: e.tensor_copy(out=c.ps[:, 6 + i, :], in_=c.arena[:, 16 + pt, :]),
                          reads=[r_pt[pt]], writes=[c.r_ps[6 + i]])
                else:
                    sc.op("dve", lambda e, i=i, pt=pt: e.tensor_tensor(out=c.ps[:, 6 + i, :], in0=c.ps[:, 6 + i, :], in1=c.arena[:, 16 + pt, :], op=ALU.add),
                          reads=[r_pt[pt], c.r_ps[6 + i]], writes=[c.r_ps[6 + i]])
                if last and i == 1:
                    emit_head_post(h)

            def emit_head_post(h):
                a_, b_ = c.tmp[:, 0, :], c.tmp[:, 1, :]
                sc.op("act", lambda e: e.copy(out=c.arena[:, 21, :], in_=c.ps[:, 7, :]), reads=[c.r_ps[7]], writes=[c.r_ar[21]])
                sc.op("act", lambda e: e.copy(out=a_, in_=c.ps[:, 4, :]), reads=[c.r_ps[4]], writes=[c.r_tmp[0]])
                sc.op("dve", lambda e: e.tensor_copy(out=b_, in_=c.ps[:, 5, :]), reads=[c.r_ps[5]], writes=[c.r_tmp[1]])
                sc.op("dve", lambda e: e.reciprocal(out=c.ms[:], in_=c.ps[:, 6, :]), reads=[c.r_ps[6]], writes=[c.r_ms])
                bank = strot[0] % 4
                strot[0] += 1
                sc.op("pe", lambda e, bank=bank: e.matmul(c.ps[:, bank, :], lhsT=c.ones[:], rhs=c.arena[:, 21, :], start=True, stop=True),
                      reads=[c.r_ar[21], rc], writes=[c.r_ps[bank]])
                sc.op("dve", lambda e: e.tensor_tensor(out=a_, in0=a_, in1=c.ms[:], op=ALU.mult), reads=[c.r_tmp[0], c.r_ms], writes=[c.r_tmp[0]])
                sc.op("dve", lambda e, bank=bank: e.reciprocal(out=c.rstd[:], in_=c.ps[:, bank, :]), reads=[c.r_ps[bank]], writes=[c.r_rstd])
                sc.op("dve", lambda e: e.tensor_tensor(out=b_, in0=b_, in1=c.rstd[:], op=ALU.mult), reads=[c.r_tmp[1], c.r_rstd], writes=[c.r_tmp[1]])
                sc.op("dve", lambda e: e.scalar_tensor_tensor(out=a_, in0=b_, scalar=sm[:, 4:5], in1=a_, op0=ALU.mult, op1=ALU.add),
                      reads=[c.r_tmp[1], c.r_tmp[0], r_sm], writes=[c.r_tmp[0]])
                sc.op("dve", lambda e: e.tensor_tensor(out=c.arena[:, 20, :], in0=a_, in1=a_, op=ALU.mult), reads=[c.r_tmp[0]], writes=[c.r_ar[20]])
                bank = strot[0] % 4
                strot[0] += 1
                sc.op("pe", lambda e, bank=bank: e.matmul(c.ps[:, bank, :], lhsT=c.ones[:], rhs=c.arena[:, 20, :], start=True, stop=True),
                      reads=[c.r_ar[20], rc], writes=[c.r_ps[bank]])
                sc.op("act", lambda e, bank=bank: e.activation(out=c.ms[:], in_=c.ps[:, bank, :], func=AF.Ln, scale=1.0 / 128, bias=EPS),
                      reads=[c.r_ps[bank]], writes=[c.r_ms])
                sc.op("act", lambda e: e.activation(out=c.rstd[:], in_=c.ms[:], func=AF.Exp, scale=-0.5),
                      reads=[c.r_ms], writes=[c.r_rstd])
                sc.op("dve", lambda e, h=h: e.scalar_tensor_tensor(out=onT[:, h, :], in0=a_, scalar=sm[:, 5:6], in1=c.rstd[:], op0=ALU.mult, op1=ALU.mult),
                      reads=[c.r_tmp[0], c.r_rstd, r_sm], writes=[r_on[h]])

            def diag_pat(d):
                off = 896 - 128 * d
                return wmask[:, off:off + 512]
            negpat = wmask[:, 0:512]
            selA, selB = sel[:, 2 * par, :], sel[:, 2 * par + 1, :]

            for sidx, (idx, h, ci, o, i) in enumerate(steps):
                if o % 4 == 0 and i == 0:
                    ensure(idx)
                    ensure(idx + 1)
                    ensure(idx + 2)
                if o % 4 == 1 and i == 0:
                    ensure(idx + 3)
                b, rkv = loaded[idx]
                bank = strot[0] % 4
                strot[0] += 1
                pt = ptrot[0] % 4
                ptrot[0] += 1
                diag = (ci == 0)
                sc.op("pe", lambda e, b=b, h=h, o=o, i=i, bank=bank, diag=diag: e.matmul(c.ps[:, bank, :], lhsT=kring[0:69, b, i, (o % 4) * 128:(o % 4 + 1) * 128], rhs=Qt[0:69, h, i, :], start=True, stop=(not diag)),
                      reads=[rkv, c.r_ar[2 * h + i], r_qaug], writes=[c.r_ps[bank]], inc=(not diag))
                if diag:
                    if o < 4:
                        mm = [(selA, diag_pat(o))]
                    else:
                        mm = [(selB, diag_pat(o - 4))]
                    for mi, (lt, rh) in enumerate(mm):
                        lastm = (mi == len(mm) - 1)
                        sc.op("pe", lambda e, bank=bank, lt=lt, rh=rh, lastm=lastm: e.matmul(c.ps[:, bank, :], lhsT=lt, rhs=rh, start=False, stop=lastm),
                              reads=[r_c5, r_c6], writes=[c.r_ps[bank]], inc=lastm)
                sc.op("act", lambda e, bank=bank, pt=pt: e.activation(out=c.arena[:, 16 + pt, :], in_=c.ps[:, bank, :], func=AF.Exp, scale=0.125),
                      reads=[c.r_ps[bank]], writes=[r_pt[pt]])
                hs = sidx - h * nsteps_h
                first = hs < 2
                last = hs >= nsteps_h - 2
                pend.append((idx, h, ci, o, i, pt, first, last))
                if len(pend) > LAG:
                    emit_av(pend.pop(0))
            while pend:
                emit_av(pend.pop(0))

        def emit_tile_B(t):
            ts = slice(t * T, (t + 1) * T)
            emit_norm(sc, c, t, vecs[:, 4, :])
            for i in range(2):
                sc.dma("sp", lambda e, t=t, i=i: e.dma_start(out=Qt[64:69, :, i, :], in_=qaug_d[t, :, :, :]), writes=[r_qaug] + c.r_ar[0:16])

            def evac_q(h, bank):
                sc.op("act", lambda e, h=h, bank=bank: e.copy(out=Qt[0:64, h, 0, :], in_=c.ps[0:64, bank, :]), reads=[c.r_ps[bank]], writes=[c.r_ar[2 * h]])
                sc.op("act", lambda e, h=h, bank=bank: e.copy(out=Qt[0:64, h, 1, :], in_=c.ps[64:128, bank, :]), reads=[c.r_ps[bank]], writes=[c.r_ar[2 * h + 1]])
            emit_proj_fm(sc, c, wr, wb["q"][0], r_w["q"], 0, D, evac_q, kouter=True)
            emit_attention(t)

            def evac_res(n, bank):
                sc.op("dve", lambda e, n=n, bank=bank: e.tensor_tensor(out=c.xT[:, n, ts], in0=c.xT[:, n, ts], in1=c.ps[:, bank, :], op=ALU.add),
                      reads=[c.r_ps[bank], c.r_x[n][t]], writes=[c.r_x[n][t]])
            emit_proj_fm(sc, c, wr, wb["o"][0], r_w["o"], 0, D, evac_res, src=onT, r_src=r_on)
            emit_norm(sc, c, t, vecs[:, 5, :])
            emit_ffn(sc, c, t, wr, wb["gu1"][0], wb["dn1"][0], r_w["gu1"], r_w["dn1"], post_add=lambda nn, t=t: emit_square(sc, c, t, nn))
            bank = c.stat_bank
            for k in range(8):
                sc.op("pe", lambda e, k=k: e.matmul(c.ps[:, bank, :], lhsT=c.ones[:], rhs=c.sq[:, k, :], start=(k == 0), stop=(k == 7)),
                      reads=[c.r_sq[k], rc], writes=[c.r_ps[bank]], inc=(k == 7))
            sc.op("act", lambda e: e.activation(out=c.ms[:], in_=c.ps[:, bank, :], func=AF.Sqrt, scale=1.0 / D, bias=EPS),
                  reads=[c.r_ps[bank]], writes=[c.r_ms])
            sc.op("dve", lambda e: e.reciprocal(out=c.rstd[:], in_=c.ms[:]),
                  reads=[c.r_ms], writes=[c.r_rstd])
            for k in range(8):
                sb = k % 2
                sc.op("dve", lambda e, k=k, sb=sb: e.scalar_tensor_tensor(out=c.tmp[:, sb, :], in0=c.xT[:, k, ts], scalar=vecs[:, 6, k:k + 1], in1=c.rstd[:],
                                                                       op0=ALU.mult, op1=ALU.mult),
                      reads=[c.r_x[k][t], c.r_rstd, rc], writes=[c.r_tmp[sb]])
                sc.dma("act", lambda e, k=k, sb=sb: e.dma_start(out=outTv[:, k, ts], in_=c.tmp[:, sb, :]), reads=[c.r_tmp[sb]])

        for t in range(NT):
            emit_tile_B(t)

        sc.finish("sp")
        sc.replay(nc)
    return nc, sc


def prep_F(inputs, core):
    b, role = core // 2, core % 2
    f = np.float32
    a = prep_A(inputs, core)
    col = lambda v: np.ascontiguousarray(np.asarray(v, f).reshape(8, 128).T)
    vecs = np.stack([col(inputs["a_norm"][0]), col(inputs["a_v_norm"][0]), col(inputs["ffn_norm"][0]), col(inputs["kv_norm"]),
                     col(inputs["b_norm"][0]), col(inputs["ffn_norm"][1]), col(inputs["final_norm"])], axis=1)
    slopes = alibi_slopes()
    pos = np.arange(S)
    khi, klo = pos // 128, pos % 128
    kaug = np.zeros((2, NH, 5, 2, S), np.float64)
    kaug[:, :, 0] = 1.0
    kaug[:, :, 1] = 1.0
    kaug[:, :, 2] = (slopes[:, None] * 128.0 * khi[None, :])[None, :, None, :]
    kaug[:, :, 3] = (slopes[:, None] * klo[None, :])[None, :, None, :]
    kaug[1, :, 4] = ((pos % 1024) >= 512).astype(np.float64)[None, None, :]
    qaug = np.zeros((NT, 5, NH, T), np.float64)
    for j in range(NT):
        qpos = tile_index(role, j) * T + np.arange(T)
        qhi, qlo = qpos // 128, qpos % 128
        qaug[j, 0] = -8.0 * slopes[:, None] * 128.0 * qhi[None, :]
        qaug[j, 1] = -8.0 * slopes[:, None] * qlo[None, :]
        qaug[j, 2] = 8.0
        qaug[j, 3] = 8.0
        qaug[j, 4] = -240000.0 if tile_index(role, j) == 2 * j else 0.0
    kk = np.arange(128)[:, None]
    cc = np.arange(1408)[None, :]
    wmask = np.where(cc - 896 - kk >= 0, 0.0, -240000.0).astype(NPBF)
    sel = np.zeros((128, 4, 128), np.float32)
    eye = np.eye(128, dtype=np.float32)
    for par in range(2):
        case_a = (tile_index(role, par) == 2 * par)
        sel[:, 2 * par, :] = eye if case_a else 0.0
        sel[:, 2 * par + 1, :] = 0.0 if case_a else eye
    return {
        "xT": a["xT"], "vecs": np.ascontiguousarray(vecs), "w_in": a["w_in"], "w_spT": a["w_spT"], "trilT": a["trilT"], "bsp": a["bsp"],
        "w_out": a["w_out"], "w_gu0": a["w_gu"], "w_dn0": a["w_dn"], "w_kv": a["w_kv"],
        "w_q": np.asarray(inputs["b_w_q"][0], f), "w_o": np.asarray(inputs["b_w_o"][0], f),
        "w_gu1": np.asarray(inputs["ffn_w_gu"][1], f), "w_dn1": np.asarray(inputs["ffn_w_down"][1], f),
        "kaug": kaug.astype(NPBF), "qaug": qaug.astype(NPBF), "wmask": wmask, "sel": sel.astype(NPBF),
        "lamb": np.ascontiguousarray(np.broadcast_to(np.asarray(inputs["b_lambda"][0], f).reshape(1, 256), (128, 256))),
        "subln": np.ascontiguousarray(np.asarray(inputs["b_subln"][0], f).reshape(128, 1)),
    }


def kernel(**inputs):
    if "F" not in _CACHE:
        _CACHE["F"] = build_F()[0]
    nc = _CACHE["F"]
    in_maps = [prep_F(inputs, cidx) for cidx in range(8)]
    res = run_bass_kernel_spmd(nc, in_maps, core_ids=list(range(8)))
    out = np.zeros((B, S, D), np.float32)
    for core in range(8):
        b, role = core // 2, core % 2
        oT = np.asarray(res.results[core]["outT"])
        for j in range(NT):
            i = tile_index(role, j)
            out[b, i * T:(i + 1) * T, :] = oT[:, j * T:(j + 1) * T].T
    return out
```

```python
import math
import numpy as np
import ml_dtypes
import concourse.bass as bass
import concourse.mybir as mybir
from concourse.bass_utils import run_bass_kernel_spmd

F32 = mybir.dt.float32
BF16 = mybir.dt.bfloat16
AF = mybir.ActivationFunctionType
ALU = mybir.AluOpType
NPBF = ml_dtypes.bfloat16

D = 1024
S = 8192
B = 4
DFF = 2816
NJ = DFF // 128
T = 512
NT = 8
TOK = NT * T
EPS = 1e-6
NH = 8
KROWS = 80
LAMBDA_INIT = 0.8 - 0.6 * math.exp(-0.3 * 1)
STAGES = {"gmlp": True, "ffn": True, "dbg": False, "attn": True, "ffnB": True, "dbgB": False}


def tile_index(role, j):
    if role == 0:
        return 2 * j if j % 2 == 0 else 2 * j + 1
    return 2 * j + 1 if j % 2 == 0 else 2 * j


class Res:
    __slots__ = ("w", "r")

    def __init__(self):
        self.w = None
        self.r = {}


COMPUTE = ("pe", "act", "dve", "pool")


class Sched:
    ND = 12

    def __init__(self):
        self.q = {e: [] for e in COMPUTE + ("sp",)}
        self.cnt = {e: 0 for e in COMPUTE}
        self.waited = {e: {} for e in COMPUTE + ("sp",)}
        self.pending = {e: [] for e in COMPUTE}
        self.dn = {"sp": 0, "pool": 0, "act": 0}
        self.dcnt = {}
        self.n_inst = 0

    def _deps(self, reads, writes):
        toks = []
        for r in reads:
            if r.w is not None:
                toks.append(r.w)
        for w in writes:
            if w.w is not None:
                toks.append(w.w)
            toks.extend(w.r.items())
        return toks

    def _wait(self, eng, toks):
        wd = self.waited[eng]
        for key, val in toks:
            if eng == "pe" and key == "pe":
                continue
            if wd.get(key, 0) < val:
                wd[key] = val
                self.q[eng].append(("wait", key, val))

    @staticmethod
    def _mark(tok, reads, writes):
        key, val = tok
        for r in reads:
            if r.r.get(key, 0) < val:
                r.r[key] = val
        for w in writes:
            w.w = tok
            w.r = {}

    def op(self, eng, fn, reads=(), writes=(), inc=True):
        self._wait(eng, self._deps(reads, writes))
        self.n_inst += 1
        if not inc:
            self._mark((eng, self.cnt[eng] + 1), reads, writes)
            self.q[eng].append(("op", fn, None))
            return None
        self.cnt[eng] += 1
        tok = (eng, self.cnt[eng])
        self.q[eng].append(("op", fn, tok))
        self._mark(tok, reads, writes)
        return tok

    def dma(self, queue, fn, reads=(), writes=()):
        toks = self._deps(reads, writes)
        nd = 6 if queue == "pool" else self.ND
        idx = self.dn[queue] % nd
        self.dn[queue] += 1
        key = ("d", queue, idx)
        c = self.dcnt.get(key, 0)
        if c > 0:
            toks.append((key, 16 * c))
        self._wait(queue, toks)
        self.dcnt[key] = c + 1
        tok = (key, 16 * (c + 1))
        self.q[queue].append(("op", fn, tok))
        self._mark(tok, reads, writes)
        self.n_inst += 1
        return tok

    def finish(self, eng="sp"):
        toks = [(e, self.cnt[e]) for e in COMPUTE if self.cnt[e] > 0]
        toks += [(k, (16 * c if k[0] == "d" else c)) for k, c in self.dcnt.items()]
        for e in COMPUTE + ("sp",):
            self._wait(e, [t for t in toks if t[0] != e])

    def replay(self, nc):
        sem_names = list(COMPUTE) + [k for k in self.dcnt]
        import contextlib
        with contextlib.ExitStack() as es:
            sems = {}
            for i, k in enumerate(sem_names):
                sems[k] = es.enter_context(nc.semaphore("s%d" % i))
            block = es.enter_context(nc.Block())
            engmap = {"pe": block.tensor, "act": block.scalar, "dve": block.vector,
                      "pool": block.gpsimd, "sp": block.sync}
            for ename, starter in engmap.items():
                items = self.q[ename]

                def body(eng, items=items):
                    for it in items:
                        if it[0] == "wait":
                            eng.wait_ge(sems[it[1]], it[2])
                        else:
                            inst = it[1](eng)
                            tok = it[2]
                            if tok is not None:
                                key = tok[0]
                                if isinstance(key, tuple) and key[0] == "c":
                                    inst.then_inc(sems[key])
                                else:
                                    inst.then_inc(sems[key], 16 if isinstance(key, tuple) else 1)
                starter(body)


class Ctx:
    pass


def emit_square(sc, c, t, k):
    ts = slice(t * T, (t + 1) * T)
    sc.op("dve", lambda e, k=k: e.tensor_tensor(out=c.sq[:, k, :], in0=c.xT[:, k, ts], in1=c.xT[:, k, ts], op=ALU.mult),
          reads=[c.r_x[k][t]], writes=[c.r_sq[k]])


def emit_norm(sc, c, t, gcol, squares_done=False):
    ts = slice(t * T, (t + 1) * T)
    for k in range(8):
        if not squares_done:
            emit_square(sc, c, t, k)
    bank = c.stat_bank
    for k in range(8):
        sc.op("pe", lambda e, k=k: e.matmul(c.ps[:, bank, :], lhsT=c.ones[:], rhs=c.sq[:, k, :], start=(k == 0), stop=(k == 7)),
              reads=[c.r_sq[k], c.r_const], writes=[c.r_ps[bank]], inc=(k == 7))
    sc.op("act", lambda e: e.activation(out=c.ms[:], in_=c.ps[:, bank, :], func=AF.Sqrt, scale=1.0 / D, bias=EPS),
          reads=[c.r_ps[bank]], writes=[c.r_ms])
    sc.op("dve", lambda e: e.reciprocal(out=c.rstd[:], in_=c.ms[:]),
          reads=[c.r_ms], writes=[c.r_rstd])
    for k in range(8):
        sc.op("dve", lambda e, k=k: e.scalar_tensor_tensor(out=c.hT[:, k, :], in0=c.xT[:, k, ts], scalar=gcol[:, k:k + 1], in1=c.rstd[:],
                                                          op0=ALU.mult, op1=ALU.mult),
              reads=[c.r_x[k][t], c.r_rstd, c.r_const], writes=[c.r_h[k]])


class WRing:
    def __init__(self, sc, tens, n):
        self.sc = sc
        self.t = tens
        self.n = n
        self.res = [Res() for _ in range(n)]
        self.i = 0

    def load(self, parts, src_res):
        b = self.i % self.n
        self.i += 1
        slab = self.t[:, b]
        r = self.res[b]
        for dst_fn, src in parts:
            self.sc.dma("sp", lambda e, dst_fn=dst_fn, src=src, slab=slab: e.dma_start(out=dst_fn(slab), in_=src),
                        reads=[src_res], writes=[r])
        return slab, r


def wview(w):
    return w.rearrange("(k p) n -> p k n", p=128)


def emit_ffn(sc, c, t, wr, wgu, wdn, r_wgu, r_wdn, post_add=None):
    ts = slice(t * T, (t + 1) * T)
    wguv = wview(wgu)
    wdnv = wview(wdn)
    for j2 in range(NJ // 2):
        slab, rs = wr.load([(lambda s: s[:, :, 0:256], wguv[:, :, j2 * 256:(j2 + 1) * 256]),
                            (lambda s: s[:, :, 256:512], wguv[:, :, DFF + j2 * 256:DFF + (j2 + 1) * 256])], r_wgu)
        for jj in range(2):
            j = 2 * j2 + jj
            bg = 2 * (j % 2)
            bu = bg + 1
            for k in range(8):
                sc.op("pe", lambda e, k=k, jj=jj, bg=bg, slab=slab: e.matmul(c.ps[:, bg, :], lhsT=slab[:, k, jj * 128:(jj + 1) * 128], rhs=c.hT[:, k, :],
                                                                              start=(k == 0), stop=(k == 7)),
                      reads=[rs, c.r_h[k]], writes=[c.r_ps[bg]], inc=(k == 7))
            for k in range(8):
                sc.op("pe", lambda e, k=k, jj=jj, bu=bu, slab=slab: e.matmul(c.ps[:, bu, :], lhsT=slab[:, k, 256 + jj * 128:256 + (jj + 1) * 128], rhs=c.hT[:, k, :],
                                                                              start=(k == 0), stop=(k == 7)),
                      reads=[rs, c.r_h[k]], writes=[c.r_ps[bu]], inc=(k == 7))
            sb = j % 2
            sc.op("act", lambda e, bg=bg, sb=sb: e.activation(out=c.tmp[:, sb, :], in_=c.ps[:, bg, :], func=AF.Silu),
                  reads=[c.r_ps[bg]], writes=[c.r_tmp[sb]])
            sc.op("dve", lambda e, j=j, bu=bu, sb=sb: e.tensor_tensor(out=c.arena[:, j, :], in0=c.tmp[:, sb, :], in1=c.ps[:, bu, :], op=ALU.mult),
                  reads=[c.r_tmp[sb], c.r_ps[bu]], writes=[c.r_ar[j]])
    kgroups = [(0, 8), (8, 8), (16, 6)]
    for hf in range(2):
        for (k0, nk) in kgroups:
            slab, rs = wr.load([(lambda s, nk=nk: s[:, 0:nk, :], wdnv[:, k0:k0 + nk, hf * 512:(hf + 1) * 512])], r_wdn)
            for kk in range(nk):
                k = k0 + kk
                for n in range(4):
                    sc.op("pe", lambda e, kk=kk, k=k, n=n, slab=slab: e.matmul(c.ps[:, 4 + n, :], lhsT=slab[:, kk, n * 128:(n + 1) * 128], rhs=c.arena[:, k, :],
                                                                                start=(k == 0), stop=(k == NJ - 1)),
                          reads=[rs, c.r_ar[k]], writes=[c.r_ps[4 + n]], inc=(k == NJ - 1 or (kk == nk - 1 and n == 3)))
        for n in range(4):
            nn = hf * 4 + n
            sc.op("dve", lambda e, n=n, nn=nn: e.tensor_tensor(out=c.xT[:, nn, ts], in0=c.xT[:, nn, ts], in1=c.ps[:, 4 + n, :], op=ALU.add),
                  reads=[c.r_ps[4 + n], c.r_x[nn][t]], writes=[c.r_x[nn][t]])
            if post_add is not None:
                post_add(nn)


def emit_proj_fm(sc, c, wr, w, r_w, col0, ncols, evac, src=None, r_src=None, kouter=False):
    src = c.hT if src is None else src
    r_src = c.r_h if r_src is None else r_src
    wv = wview(w)
    nchunks = ncols // 128
    for s0 in range(0, nchunks, 4):
        slab, rs = wr.load([(lambda s: s, wv[:, :, col0 + s0 * 128:col0 + s0 * 128 + 512])], r_w)
        if s0 == 0 and kouter:
            banks = [(c.mm_rot + q) % 4 for q in range(4)]
            c.mm_rot += 4
            for k in range(8):
                for nn in range(4):
                    sc.op("pe", lambda e, k=k, nn=nn, bank=banks[nn], slab=slab: e.matmul(c.ps[:, bank, :], lhsT=slab[:, k, nn * 128:(nn + 1) * 128], rhs=src[:, k, :],
                                                                                         start=(k == 0), stop=(k == 7)),
                          reads=[rs, r_src[k]], writes=[c.r_ps[banks[nn]]], inc=(k == 7))
            for nn in range(4):
                evac(nn, banks[nn])
            continue
        for nn in range(4):
            n = s0 + nn
            bank = c.mm_rot % 4
            c.mm_rot += 1
            for k in range(8):
                sc.op("pe", lambda e, k=k, nn=nn, bank=bank, slab=slab: e.matmul(c.ps[:, bank, :], lhsT=slab[:, k, nn * 128:(nn + 1) * 128], rhs=src[:, k, :],
                                                                                  start=(k == 0), stop=(k == 7)),
                      reads=[rs, r_src[k]], writes=[c.r_ps[bank]], inc=(k == 7))
            evac(n, bank)


def emit_proj_tm(sc, c, wr, w, r_w, col0, evac):
    wv = wview(w)
    slabs = []
    for hf in range(2):
        slabs.append(wr.load([(lambda s: s, wv[:, :, col0 + hf * 512:col0 + (hf + 1) * 512])], r_w))
    for ch in range(4):
        b0 = 4 + 2 * (ch % 2)
        for hf in range(2):
            slab, rs = slabs[hf]
            for k in range(8):
                sc.op("pe", lambda e, k=k, ch=ch, hf=hf, b0=b0, slab=slab: e.matmul(c.ps[:, b0 + hf, :], lhsT=c.hT[:, k, ch * 128:(ch + 1) * 128], rhs=slab[:, k, :],
                                                                                     start=(k == 0), stop=(k == 7)),
                      reads=[rs, c.r_h[k]], writes=[c.r_ps[b0 + hf]], inc=(k == 7))
        evac(ch, b0)


def alloc_common(nc, es, c, sc, nwslab=3):
    c.xT = es.enter_context(nc.sbuf_tensor("s_xT", [128, 8, TOK], F32))
    c.hT = es.enter_context(nc.sbuf_tensor("s_hT", [128, 8, T], BF16))
    c.sq = es.enter_context(nc.sbuf_tensor("s_sq", [128, 8, T], BF16))
    c.arena = es.enter_context(nc.sbuf_tensor("s_arena", [128, NJ, T], BF16))
    c.ms = es.enter_context(nc.sbuf_tensor("s_ms", [128, T], F32))
    c.rstd = es.enter_context(nc.sbuf_tensor("s_rstd", [128, T], F32))
    c.tmp = es.enter_context(nc.sbuf_tensor("s_tmp", [128, 2, T], F32))
    c.nh1 = es.enter_context(nc.sbuf_tensor("s_nh1", [128, 1], F32))
    c.ones = es.enter_context(nc.sbuf_tensor("s_ones", [128, 128], BF16))
    c.wslab = es.enter_context(nc.sbuf_tensor("s_wslab", [128, nwslab, 8, 512], BF16))
    c.ps = es.enter_context(nc.psum_tensor("p_ps", [128, 8, 512], F32))
    c.r_x = [[Res() for _ in range(NT)] for _ in range(8)]
    c.r_h = [Res() for _ in range(8)]
    c.r_sq = [Res() for _ in range(8)]
    c.r_ar = [Res() for _ in range(NJ)]
    c.r_ms = Res()
    c.r_rstd = Res()
    c.r_tmp = [Res(), Res()]
    c.r_const = Res()
    c.r_ps = [Res() for _ in range(8)]
    c.stat_bank = 3
    c.mm_rot = 0
    sc.op("pool", lambda e: e.memset(c.nh1[:], -0.5), writes=[c.r_const])
    sc.op("pool", lambda e: e.memset(c.ones[:], 1.0), writes=[c.r_const])
    return WRing(sc, c.wslab, nwslab)


def cast_weight(sc, dst, src, r_dst, rows_per_dma=1024):
    K, N = src.shape
    b = max(d for d in range(1, 1025) if N % d == 0)
    a = N // b
    sv = src.rearrange("r (a b) -> (r a) b", b=b)
    dv = dst.rearrange("r (a b) -> (r a) b", b=b)
    rows = K * a
    for r0 in range(0, rows, rows_per_dma):
        r1 = min(rows, r0 + rows_per_dma)
        sc.dma("pool", lambda e, r0=r0, r1=r1: e.dma_start(out=dv[r0:r1, :], in_=sv[r0:r1, :]), writes=[r_dst])


def build_A():
    import contextlib
    nc = bass.Bass("TRN2", target_bir_lowering=False)
    sc = Sched()
    c = Ctx()
    din = lambda name, shape, dt=F32: nc.dram_tensor(name, shape, dt, kind="ExternalInput").ap()
    xT_d = din("xT", [D, TOK])
    vecs_d = din("vecs", [128, 4, 8])
    w_in_d = din("w_in", [D, 2 * D])
    w_spT_d = din("w_spT", [128, 8, 128])
    tril_d = din("trilT", [128, 128])
    bsp_d = din("bsp", [128, 8, 128])
    w_out_d = din("w_out", [D, D])
    w_gu_d = din("w_gu", [D, 2 * DFF])
    w_dn_d = din("w_dn", [DFF, D])
    w_kv_d = din("w_kv", [D, 2 * D])
    x1T_o = nc.dram_tensor("x1T", [D, TOK], F32, kind="ExternalOutput").ap()
    kT_o = nc.dram_tensor("kT", [D, TOK], BF16, kind="ExternalOutput").ap()
    v_o = nc.dram_tensor("v", [TOK, D], BF16, kind="ExternalOutput").ap()
    dint = lambda name, shape: nc.dram_tensor(name, shape, BF16, kind="Internal").ap()
    if STAGES["dbg"]:
        dbg_u = nc.dram_tensor("dbg_u", [128, 8, T], BF16, kind="ExternalOutput").ap()
        dbg_vn = nc.dram_tensor("dbg_vn", [128, 8, T], BF16, kind="ExternalOutput").ap()
        dbg_uz = nc.dram_tensor("dbg_uz", [128, 8, T], BF16, kind="ExternalOutput").ap()
        dbg_z = nc.dram_tensor("dbg_z", [128, 8, T], F32, kind="ExternalOutput").ap()
        dbg_vf = nc.dram_tensor("dbg_vf", [128, 4, 1024], F32, kind="ExternalOutput").ap()
        dbg_mv = nc.dram_tensor("dbg_mv", [128, 4, 4], F32, kind="ExternalOutput").ap()
    w_in_b = dint("w_in_b", [D, 2 * D])
    w_out_b = dint("w_out_b", [D, D])
    w_gu_b = dint("w_gu_b", [D, 2 * DFF])
    w_dn_b = dint("w_dn_b", [DFF, D])
    w_kv_b = dint("w_kv_b", [D, 2 * D])

    with contextlib.ExitStack() as es:
        wr = alloc_common(nc, es, c, sc, nwslab=3)
        vecs = es.enter_context(nc.sbuf_tensor("s_vecs", [128, 4, 8], F32))
        wspT_f = c.tmp[:].rearrange("p a (b c) -> p (a b) c", c=128)
        tril = es.enter_context(nc.sbuf_tensor("s_tril", [128, 128], F32))
        wcT = es.enter_context(nc.sbuf_tensor("s_wcT", [128, 8, 128], BF16))
        bsp = es.enter_context(nc.sbuf_tensor("s_bsp", [128, 8, 128], F32))
        vstat = es.enter_context(nc.sbuf_tensor("s_vstat", [128, 2, 6], F32))
        vmv = es.enter_context(nc.sbuf_tensor("s_vmv", [128, 4], F32))
        nh1 = c.nh1
        r_vf = Res()
        r_vstat = Res()
        r_vmv = Res()
        r_w = {n: Res() for n in ("in", "out", "gu", "dn", "kv")}

        rc = c.r_const
        r_c2, r_c3, r_c4 = Res(), Res(), Res()
        sc.dma("sp", lambda e: e.dma_start(out=vecs[:], in_=vecs_d[:, :, :]), writes=[rc])
        sc.dma("sp", lambda e: e.dma_start(out=wspT_f, in_=w_spT_d[:, :, :]), writes=[rc] + c.r_tmp)
        sc.dma("sp", lambda e: e.dma_start(out=tril[:], in_=tril_d[:, :]), writes=[r_c2])
        sc.dma("sp", lambda e: e.dma_start(out=bsp[:], in_=bsp_d[:, :, :]), writes=[r_c3])
        for g in range(8):
            sc.op("dve", lambda e, g=g: e.tensor_tensor(out=wcT[:, g, :], in0=wspT_f[:, g, :], in1=tril[:], op=ALU.mult),
                  reads=[r_c2] + c.r_tmp, writes=[r_c4])
        xTv = xT_d.rearrange("(k p) t -> p k t", p=128)
        for t in range(NT):
            sc.dma("sp", lambda e, t=t: e.dma_start(out=c.xT[:, :, t * T:(t + 1) * T], in_=xTv[:, :, t * T:(t + 1) * T]),
                   writes=[c.r_x[k][t] for k in range(8)])
        cast_weight(sc, w_in_b, w_in_d, r_w["in"])
        cast_weight(sc, w_out_b, w_out_d, r_w["out"])
        cast_weight(sc, w_gu_b, w_gu_d, r_w["gu"])
        cast_weight(sc, w_dn_b, w_dn_d, r_w["dn"])
        cast_weight(sc, w_kv_b, w_kv_d, r_w["kv"])

        x1Tv = x1T_o.rearrange("(k p) t -> p k t", p=128)
        kTv = kT_o.rearrange("(k p) t -> p k t", p=128)
        vov = v_o.rearrange("(c p) f -> p c f", p=128)
        vf = c.arena[:, 16:20, :].rearrange("p a b -> p (a b)").bitcast(F32)
        r_vfslots = c.r_ar[16:20]

        def emit_gmlp(t, ts):
            emit_norm(sc, c, t, vecs[:, 0, :])

            def evac_u(n, bank):
                sc.op("act", lambda e, n=n, bank=bank: e.activation(out=c.arena[:, n, :], in_=c.ps[:, bank, :], func=AF.Gelu_apprx_tanh),
                      reads=[c.r_ps[bank]], writes=[c.r_ar[n]])
            emit_proj_fm(sc, c, wr, w_in_b, r_w["in"], 0, D, evac_u)
            if STAGES["dbg"] and t == 0:
                sc.dma("pool", lambda e: e.dma_start(out=dbg_u[:, :, :], in_=c.arena[:, 0:8, :]), reads=c.r_ar[0:8])

            def evac_v(ch, b0):
                for hf in range(2):
                    sc.op("act", lambda e, hf=hf, b0=b0: e.activation(out=vf[:, hf * 512:(hf + 1) * 512], in_=c.ps[:, b0 + hf, :], func=AF.Gelu_apprx_tanh),
                          reads=[c.r_ps[b0 + hf]], writes=r_vfslots[2 * hf:2 * hf + 2])
                for hf in range(2):
                    sc.op("dve", lambda e, hf=hf: e.bn_stats(out=vstat[:, hf, :], in_=vf[:, hf * 512:(hf + 1) * 512]),
                          reads=r_vfslots[2 * hf:2 * hf + 2], writes=[r_vstat])
                sc.op("dve", lambda e: e.bn_aggr(out=vmv[:, 0:2], in_=vstat[:].rearrange("p a b -> p (a b)")), reads=[r_vstat], writes=[r_vmv])
                sc.op("dve", lambda e: e.tensor_scalar(out=vmv[:, 2:3], in0=vmv[:, 1:2], scalar1=EPS, scalar2=None, op0=ALU.add),
                      reads=[r_vmv], writes=[r_vmv])
                sc.op("pool", lambda e: e.tensor_tensor(out=vmv[:, 3:4], in0=vmv[:, 2:3], in1=nh1[:], op=ALU.pow),
                      reads=[r_vmv, rc], writes=[r_vmv])
                if STAGES["dbg"] and t == 0:
                    sc.dma("pool", lambda e, ch=ch: e.dma_start(out=dbg_vf[:, ch, :], in_=vf), reads=r_vfslots)
                    sc.dma("pool", lambda e, ch=ch: e.dma_start(out=dbg_mv[:, ch, :], in_=vmv[:]), reads=[r_vmv])
                vn = c.arena[:, 8 + 2 * ch:10 + 2 * ch, :].rearrange("p a b -> p (a b)")
                sc.op("dve", lambda e, vn=vn: e.tensor_scalar(out=vn, in0=vf, scalar1=vmv[:, 0:1], scalar2=vmv[:, 3:4], op0=ALU.subtract, op1=ALU.mult),
                      reads=[r_vmv] + r_vfslots, writes=c.r_ar[8 + 2 * ch:10 + 2 * ch])
            emit_proj_tm(sc, c, wr, w_in_b, r_w["in"], D, evac_v)

            if STAGES["dbg"] and t == 0:
                sc.dma("pool", lambda e: e.dma_start(out=dbg_vn[:, :, :], in_=c.arena[:, 8:16, :]), reads=c.r_ar[8:16])
            for g in range(8):
                bank = c.mm_rot % 4
                c.mm_rot += 1
                for ch in range(4):
                    vn = c.arena[:, 8 + 2 * ch:10 + 2 * ch, :].rearrange("p a b -> p (a b)")
                    sc.op("pe", lambda e, g=g, ch=ch, bank=bank, vn=vn: e.matmul(c.ps[:, bank, ch * 128:(ch + 1) * 128], lhsT=vn[:, g * 128:(g + 1) * 128], rhs=wcT[:, g, :],
                                                                                  start=True, stop=True),
                          reads=c.r_ar[8 + 2 * ch:10 + 2 * ch] + [r_c4], writes=[c.r_ps[bank]], inc=(ch == 3))
                sb = g % 2
                sc.op("dve", lambda e, g=g, bank=bank, sb=sb: e.scalar_tensor_tensor(
                    out=c.tmp[:, sb, :].rearrange("p (a b) -> p a b", a=4), in0=c.ps[:, bank, :].rearrange("p (a b) -> p a b", a=4),
                    scalar=vecs[:, 1, g:g + 1], in1=bsp[:, g, :].unsqueeze(1).broadcast_to([128, 4, 128]), op0=ALU.mult, op1=ALU.add),
                    reads=[c.r_ps[bank], rc, r_c3], writes=[c.r_tmp[sb]])
                if STAGES["dbg"] and t == 0:
                    sc.dma("pool", lambda e, g=g, sb=sb: e.dma_start(out=dbg_z[:, g, :], in_=c.tmp[:, sb, :]), reads=[c.r_tmp[sb]])
                sc.op("dve", lambda e, g=g, sb=sb: e.tensor_tensor(out=c.arena[:, g, :], in0=c.tmp[:, sb, :], in1=c.arena[:, g, :], op=ALU.mult),
                      reads=[c.r_tmp[sb], c.r_ar[g]], writes=[c.r_ar[g]])

            if STAGES["dbg"] and t == 0:
                sc.dma("pool", lambda e: e.dma_start(out=dbg_uz[:, :, :], in_=c.arena[:, 0:8, :]), reads=c.r_ar[0:8])

            def evac_res(n, bank):
                sc.op("dve", lambda e, n=n, bank=bank: e.tensor_tensor(out=c.xT[:, n, ts], in0=c.xT[:, n, ts], in1=c.ps[:, bank, :], op=ALU.add),
                      reads=[c.r_ps[bank], c.r_x[n][t]], writes=[c.r_x[n][t]])
            emit_proj_fm(sc, c, wr, w_out_b, r_w["out"], 0, D, evac_res, src=c.arena, r_src=c.r_ar)

        def emit_tail(t, ts):
            sc.dma("pool", lambda e, ts=ts: e.dma_start(out=x1Tv[:, :, ts], in_=c.xT[:, :, ts]), reads=[c.r_x[k][t] for k in range(8)])

            emit_norm(sc, c, t, vecs[:, 3, :])

            def evac_k(n, bank):
                sc.op("act", lambda e, n=n, bank=bank: e.copy(out=c.arena[:, n, :], in_=c.ps[:, bank, :]),
                      reads=[c.r_ps[bank]], writes=[c.r_ar[n]])
            emit_proj_fm(sc, c, wr, w_kv_b, r_w["kv"], 0, D, evac_k)
            sc.dma("pool", lambda e, ts=ts: e.dma_start(out=kTv[:, :, ts], in_=c.arena[:, 0:8, :]), reads=c.r_ar[0:8])

            def evac_vv(ch, b0):
                vst = c.arena[:, 8 + 2 * ch:10 + 2 * ch, :].rearrange("p a b -> p (a b)")
                sc.op("act", lambda e, vst=vst, b0=b0: e.copy(out=vst[:, 0:512], in_=c.ps[:, b0, :]),
                      reads=[c.r_ps[b0]], writes=[c.r_ar[8 + 2 * ch]])
                sc.op("dve", lambda e, vst=vst, b0=b0: e.tensor_copy(out=vst[:, 512:1024], in_=c.ps[:, b0 + 1, :]),
                      reads=[c.r_ps[b0 + 1]], writes=[c.r_ar[9 + 2 * ch]])
            emit_proj_tm(sc, c, wr, w_kv_b, r_w["kv"], D, evac_vv)
            sc.dma("pool", lambda e, t=t: e.dma_start(out=vov[:, 4 * t:4 * t + 4, :],
                                                      in_=c.arena[:, 8:16, :].rearrange("p (c a) b -> p c (a b)", a=2)),
                   reads=c.r_ar[8:16])

        for t in range(1 if STAGES["dbg"] else NT):
            ts = slice(t * T, (t + 1) * T)
            if STAGES["gmlp"]:
                emit_gmlp(t, ts)
            if STAGES["ffn"]:
                emit_norm(sc, c, t, vecs[:, 2, :])
                emit_ffn(sc, c, t, wr, w_gu_b, w_dn_b, r_w["gu"], r_w["dn"])
            emit_tail(t, ts)

        sc.finish("sp")
        sc.replay(nc)
    return nc, sc


def prep_A(inputs, core):
    b, role = core // 2, core % 2
    f = np.float32
    x = np.asarray(inputs["x"])
    toks = np.concatenate([np.arange(tile_index(role, j) * T, (tile_index(role, j) + 1) * T) for j in range(NT)])
    xT = np.ascontiguousarray(x[b][toks].T)
    col = lambda v: np.ascontiguousarray(np.asarray(v, f).reshape(8, 128).T)
    vecs = np.stack([col(inputs["a_norm"][0]), col(inputs["a_v_norm"][0]), col(inputs["ffn_norm"][0]), col(inputs["kv_norm"])], axis=1)
    w_spT = np.ascontiguousarray(np.transpose(np.asarray(inputs["a_w_sp"][0], f), (2, 0, 1)))
    trilT = np.triu(np.ones((128, 128), f))
    bsp = np.ascontiguousarray(np.broadcast_to(np.asarray(inputs["a_b_sp"][0], f)[None], (128, 8, 128)))
    return {
        "xT": xT, "vecs": np.ascontiguousarray(vecs), "w_in": np.asarray(inputs["a_w_in"][0], f),
        "w_spT": w_spT, "trilT": trilT, "bsp": bsp, "w_out": np.asarray(inputs["a_w_out"][0], f),
        "w_gu": np.asarray(inputs["ffn_w_gu"][0], f), "w_dn": np.asarray(inputs["ffn_w_down"][0], f),
        "w_kv": np.asarray(inputs["kv_w"], f),
    }


_CACHE = {}


def run_A(inputs):
    if "A" not in _CACHE:
        _CACHE["A"] = build_A()[0]
    nc = _CACHE["A"]
    in_maps = [prep_A(inputs, cidx) for cidx in range(8)]
    res = run_bass_kernel_spmd(nc, in_maps, core_ids=list(range(8)))
    return res.results


def build_B():
    import contextlib
    nc = bass.Bass("TRN2", target_bir_lowering=False)
    sc = Sched()
    c = Ctx()
    din = lambda name, shape, dt=F32: nc.dram_tensor(name, shape, dt, kind="ExternalInput").ap()
    x1T_d = din("x1T", [D, TOK])
    vecs_d = din("vecs", [128, 3, 8])
    w_q_d = din("w_q", [D, D])
    w_o_d = din("w_o", [D, D])
    w_gu_d = din("w_gu", [D, 2 * DFF])
    w_dn_d = din("w_dn", [DFF, D])
    kd_d = [din("kd_e", [NH, KROWS, 2, S], BF16), din("kd_o", [NH, KROWS, 2, S], BF16)]
    vd_d = [din("vd_e", [NH, 128, 64, 128], BF16), din("vd_o", [NH, 128, 64, 128], BF16)]
    qaug_d = din("qaug", [NT, 4, NH, T], BF16)
    masks_d = din("masks", [128, 4, T], BF16)
    ident_d = din("ident", [128, 128], BF16)
    lamb_d = din("lamb", [128, 256])
    subln_d = din("subln", [128, 1])
    outT_o = nc.dram_tensor("outT", [D, TOK], F32, kind="ExternalOutput").ap()
    if STAGES["dbgB"]:
        dbg_on = nc.dram_tensor("dbg_on", [128, 8, T], BF16, kind="ExternalOutput").ap()
        dbg_q = nc.dram_tensor("dbg_q", [128, 16, T], BF16, kind="ExternalOutput").ap()
        dbg_sm = nc.dram_tensor("dbg_sm", [128, 8], F32, kind="ExternalOutput").ap()
    dint = lambda name, shape: nc.dram_tensor(name, shape, BF16, kind="Internal").ap()
    w_q_b = dint("w_q_b", [D, D])
    w_o_b = dint("w_o_b", [D, D])
    w_gu_b = dint("w_gu_b", [D, 2 * DFF])
    w_dn_b = dint("w_dn_b", [DFF, D])

    with contextlib.ExitStack() as es:
        wr = alloc_common(nc, es, c, sc, nwslab=2)
        vecs = es.enter_context(nc.sbuf_tensor("s_vecs", [128, 3, 8], F32))
        kring = es.enter_context(nc.sbuf_tensor("s_kring", [KROWS, 2, 2, 1024], BF16))
        vring = es.enter_context(nc.sbuf_tensor("s_vring", [128, 2, 8, 128], BF16))
        masks = es.enter_context(nc.sbuf_tensor("s_masks", [128, 4, T], BF16))
        ident = es.enter_context(nc.sbuf_tensor("s_ident", [128, 128], BF16))
        lamb = es.enter_context(nc.sbuf_tensor("s_lamb", [128, 256], F32))
        sm = es.enter_context(nc.sbuf_tensor("s_sm", [128, 8], F32))
        r_kv = [Res(), Res()]
        r_qaug = Res()
        r_pt = c.r_ar[16:20]
        r_sm = Res()
        r_w = {n: Res() for n in ("q", "o", "gu", "dn")}
        rc = c.r_const
        r_c2, r_c3 = Res(), Res()
        Qt = c.arena[:, 0:16, :].rearrange("p (h i) t -> p h i t", i=2)
        onT, r_on = c.sq, c.r_sq

        sc.dma("sp", lambda e: e.dma_start(out=vecs[:], in_=vecs_d[:, :, :]), writes=[rc])
        sc.dma("sp", lambda e: e.dma_start(out=masks[:], in_=masks_d[:, :, :]), writes=[r_c2])
        r_c5 = Res()
        sc.dma("sp", lambda e: e.dma_start(out=ident[:], in_=ident_d[:, :]), writes=[r_c5])
        sc.dma("sp", lambda e: e.dma_start(out=lamb[:], in_=lamb_d[:, :]), writes=[r_c3])
        r_c4 = Res()
        sc.dma("sp", lambda e: e.dma_start(out=sm[:, 7:8], in_=subln_d[:, :]), writes=[r_c4])
        x1Tv = x1T_d.rearrange("(k p) t -> p k t", p=128)
        for t in range(NT):
            sc.dma("sp", lambda e, t=t: e.dma_start(out=c.xT[:, :, t * T:(t + 1) * T], in_=x1Tv[:, :, t * T:(t + 1) * T]),
                   writes=[c.r_x[k][t] for k in range(8)])
        cast_weight(sc, w_q_b, w_q_d, r_w["q"])
        cast_weight(sc, w_o_b, w_o_d, r_w["o"])
        cast_weight(sc, w_gu_b, w_gu_d, r_w["gu"])
        cast_weight(sc, w_dn_b, w_dn_d, r_w["dn"])
        scr = c.tmp[:, 0, 0:64]
        sc.op("dve", lambda e: e.scalar_tensor_tensor(out=scr, in0=lamb[:, 0:64], scalar=1.0, in1=lamb[:, 64:128], op0=ALU.mult, op1=ALU.mult, accum_out=sm[:, 0:1]),
              reads=[r_c3], writes=[r_sm, c.r_tmp[0]])
        sc.op("dve", lambda e: e.scalar_tensor_tensor(out=scr, in0=lamb[:, 128:192], scalar=1.0, in1=lamb[:, 192:256], op0=ALU.mult, op1=ALU.mult, accum_out=sm[:, 1:2]),
              reads=[r_c3, r_sm], writes=[r_sm, c.r_tmp[0]])
        sc.op("act", lambda e: e.activation(out=sm[:, 2:4], in_=sm[:, 0:2], func=AF.Exp), reads=[r_sm], writes=[r_sm])
        sc.op("dve", lambda e: e.tensor_tensor(out=sm[:, 4:5], in0=sm[:, 3:4], in1=sm[:, 2:3], op=ALU.subtract), reads=[r_sm], writes=[r_sm])
        sc.op("dve", lambda e: e.tensor_scalar(out=sm[:, 4:5], in0=sm[:, 4:5], scalar1=-LAMBDA_INIT, scalar2=None, op0=ALU.add), reads=[r_sm], writes=[r_sm])
        sc.op("dve", lambda e: e.tensor_scalar(out=sm[:, 5:6], in0=sm[:, 7:8], scalar1=1.0 - LAMBDA_INIT, scalar2=None, op0=ALU.mult), reads=[r_sm, r_c4], writes=[r_sm])

        outTv = outT_o.rearrange("(k p) t -> p k t", p=128)
        strot = [0]
        ptrot = [0]
        kvn = [0]

        def load_kv(var, h, cp):
            b = kvn[0] % 2
            kvn[0] += 1
            r = r_kv[b]
            sc.dma("sp", lambda e, b=b: e.dma_start(out=kring[:, b], in_=kd_d[var][h, :, :, cp * 1024:(cp + 1) * 1024]), writes=[r])
            sc.dma("sp", lambda e, b=b: e.dma_start(out=vring[:, b], in_=vd_d[var][h, :, cp * 8:(cp + 1) * 8, :]), writes=[r])
            return b, r

        def emit_attention(j, ts):
            var = j % 2
            chunks = [(h, ci) for h in range(NH) for ci in range(j + 1)]
            loaded = {}

            def ensure(idx):
                if idx < len(chunks) and idx not in loaded:
                    h, ci = chunks[idx]
                    loaded[idx] = load_kv(var, h, j - ci)
            steps = []
            for idx, (h, ci) in enumerate(chunks):
                for o in range(8):
                    for i in range(2):
                        steps.append((idx, h, ci, o, i))
            nsteps_h = (j + 1) * 16
            pend = []
            LAG = 2

            def emit_av(item):
                (idx, h, ci, o, i, pt, first, last) = item
                b, rkv = loaded[idx]
                sc.op("pe", lambda e, b=b, o=o, i=i, pt=pt, first=first, last=last: e.matmul(c.ps[:, 4 + i, :], lhsT=vring[:, b, o, :], rhs=c.arena[:, 16 + pt, :], start=first, stop=last),
                      reads=[rkv, r_pt[pt]], writes=[c.r_ps[4 + i]], inc=False)
                sc.op("pe", lambda e, i=i, pt=pt, first=first, last=last: e.matmul(c.ps[:, 6 + i, :], lhsT=c.ones[:], rhs=c.arena[:, 16 + pt, :], start=first, stop=last),
                      reads=[r_pt[pt], rc], writes=[c.r_ps[6 + i]], inc=True)
                if last and i == 1:
                    emit_head_post(h)

            def emit_head_post(h):
                a_, b_ = c.tmp[:, 0, :], c.tmp[:, 1, :]
                sc.op("dve", lambda e: e.reciprocal(out=c.ms[:], in_=c.ps[:, 6, :]), reads=[c.r_ps[6]], writes=[c.r_ms])
                sc.op("dve", lambda e: e.tensor_tensor(out=a_, in0=c.ps[:, 4, :], in1=c.ms[:], op=ALU.mult), reads=[c.r_ps[4], c.r_ms], writes=[c.r_tmp[0]])
                sc.op("dve", lambda e: e.reciprocal(out=c.rstd[:], in_=c.ps[:, 7, :]), reads=[c.r_ps[7]], writes=[c.r_rstd])
                sc.op("dve", lambda e: e.tensor_tensor(out=b_, in0=c.ps[:, 5, :], in1=c.rstd[:], op=ALU.mult), reads=[c.r_ps[5], c.r_rstd], writes=[c.r_tmp[1]])
                sc.op("dve", lambda e: e.scalar_tensor_tensor(out=a_, in0=b_, scalar=sm[:, 4:5], in1=a_, op0=ALU.mult, op1=ALU.add),
                      reads=[c.r_tmp[1], c.r_tmp[0], r_sm], writes=[c.r_tmp[0]])
                sc.op("dve", lambda e: e.tensor_tensor(out=c.arena[:, 20, :], in0=a_, in1=a_, op=ALU.mult), reads=[c.r_tmp[0]], writes=[c.r_ar[20]])
                bank = strot[0] % 4
                strot[0] += 1
                sc.op("pe", lambda e, bank=bank: e.matmul(c.ps[:, bank, :], lhsT=c.ones[:], rhs=c.arena[:, 20, :], start=True, stop=True),
                      reads=[c.r_ar[20], rc], writes=[c.r_ps[bank]])
                sc.op("act", lambda e, bank=bank: e.activation(out=c.ms[:], in_=c.ps[:, bank, :], func=AF.Sqrt, scale=1.0 / 128, bias=EPS),
                      reads=[c.r_ps[bank]], writes=[c.r_ms])
                sc.op("dve", lambda e: e.reciprocal(out=c.rstd[:], in_=c.ms[:]),
                      reads=[c.r_ms], writes=[c.r_rstd])
                sc.op("dve", lambda e, h=h: e.scalar_tensor_tensor(out=onT[:, h, :], in0=a_, scalar=sm[:, 5:6], in1=c.rstd[:], op0=ALU.mult, op1=ALU.mult),
                      reads=[c.r_tmp[0], c.r_rstd, r_sm], writes=[r_on[h]])

            for sidx, (idx, h, ci, o, i) in enumerate(steps):
                if o == 0 and i == 0:
                    ensure(idx)
                if o == 1 and i == 0:
                    ensure(idx + 1)
                b, rkv = loaded[idx]
                bank = strot[0] % 4
                strot[0] += 1
                pt = ptrot[0] % 4
                ptrot[0] += 1
                diag = (ci == 0 and o >= 4)
                sc.op("pe", lambda e, b=b, h=h, o=o, i=i, bank=bank, diag=diag: e.matmul(c.ps[:, bank, :], lhsT=kring[0:68, b, i, o * 128:(o + 1) * 128], rhs=Qt[0:68, h, i, :], start=True, stop=(not diag)),
                      reads=[rkv, c.r_ar[2 * h + i], r_qaug], writes=[c.r_ps[bank]], inc=(not diag))
                if diag:
                    dd = o - 4
                    sc.op("pe", lambda e, bank=bank, dd=dd: e.matmul(c.ps[:, bank, :], lhsT=ident[:], rhs=masks[:, dd, :], start=False, stop=True),
                          reads=[r_c2, r_c5], writes=[c.r_ps[bank]])
                sc.op("act", lambda e, bank=bank, pt=pt: e.activation(out=c.arena[:, 16 + pt, :], in_=c.ps[:, bank, :], func=AF.Exp, scale=0.125),
                      reads=[c.r_ps[bank]], writes=[r_pt[pt]])
                hs = sidx - h * nsteps_h
                first = hs < 2
                last = hs >= nsteps_h - 2
                pend.append((idx, h, ci, o, i, pt, first, last))
                if len(pend) > LAG:
                    emit_av(pend.pop(0))
            while pend:
                emit_av(pend.pop(0))

        def emit_tile(t):
            ts = slice(t * T, (t + 1) * T)
            emit_norm(sc, c, t, vecs[:, 0, :])
            for i in range(2):
                sc.dma("sp", lambda e, t=t, i=i: e.dma_start(out=Qt[64:68, :, i, :], in_=qaug_d[t, :, :, :]), writes=[r_qaug] + c.r_ar[0:16])

            def evac_q(h, bank):
                sc.op("act", lambda e, h=h, bank=bank: e.copy(out=Qt[0:64, h, 0, :], in_=c.ps[0:64, bank, :]), reads=[c.r_ps[bank]], writes=[c.r_ar[2 * h]])
                sc.op("act", lambda e, h=h, bank=bank: e.copy(out=Qt[0:64, h, 1, :], in_=c.ps[64:128, bank, :]), reads=[c.r_ps[bank]], writes=[c.r_ar[2 * h + 1]])
            emit_proj_fm(sc, c, wr, w_q_b, r_w["q"], 0, D, evac_q)
            if STAGES["dbgB"] and t == 0:
                sc.dma("pool", lambda e: e.dma_start(out=dbg_q[:, :, :], in_=c.arena[:, 0:16, :]), reads=c.r_ar[0:16] + [r_qaug])
                sc.dma("pool", lambda e: e.dma_start(out=dbg_sm[:, :], in_=sm[:]), reads=[r_sm])
            if STAGES["attn"]:
                emit_attention(t, ts)
            if STAGES["dbgB"] and t == 0:
                sc.dma("pool", lambda e: e.dma_start(out=dbg_on[:, :, :], in_=onT[:]), reads=r_on)

            def evac_res(n, bank):
                sc.op("dve", lambda e, n=n, bank=bank: e.tensor_tensor(out=c.xT[:, n, ts], in0=c.xT[:, n, ts], in1=c.ps[:, bank, :], op=ALU.add),
                      reads=[c.r_ps[bank], c.r_x[n][t]], writes=[c.r_x[n][t]])
            if STAGES["attn"]:
                emit_proj_fm(sc, c, wr, w_o_b, r_w["o"], 0, D, evac_res, src=onT, r_src=r_on)
            if STAGES["ffnB"]:
                emit_norm(sc, c, t, vecs[:, 1, :])
                emit_ffn(sc, c, t, wr, w_gu_b, w_dn_b, r_w["gu"], r_w["dn"])
            for k in range(8):
                sc.op("dve", lambda e, k=k: e.tensor_tensor(out=c.sq[:, k, :], in0=c.xT[:, k, ts], in1=c.xT[:, k, ts], op=ALU.mult),
                      reads=[c.r_x[k][t]], writes=[c.r_sq[k]])
            bank = c.stat_bank
            for k in range(8):
                sc.op("pe", lambda e, k=k: e.matmul(c.ps[:, bank, :], lhsT=c.ones[:], rhs=c.sq[:, k, :], start=(k == 0), stop=(k == 7)),
                      reads=[c.r_sq[k], rc], writes=[c.r_ps[bank]], inc=(k == 7))
            sc.op("act", lambda e: e.activation(out=c.ms[:], in_=c.ps[:, bank, :], func=AF.Sqrt, scale=1.0 / D, bias=EPS),
                  reads=[c.r_ps[bank]], writes=[c.r_ms])
            sc.op("dve", lambda e: e.reciprocal(out=c.rstd[:], in_=c.ms[:]),
                  reads=[c.r_ms], writes=[c.r_rstd])
            for k in range(8):
                sb = k % 2
                sc.op("dve", lambda e, k=k, sb=sb: e.scalar_tensor_tensor(out=c.tmp[:, sb, :], in0=c.xT[:, k, ts], scalar=vecs[:, 2, k:k + 1], in1=c.rstd[:],
                                                                       op0=ALU.mult, op1=ALU.mult),
                      reads=[c.r_x[k][t], c.r_rstd, rc], writes=[c.r_tmp[sb]])
                sc.dma("pool", lambda e, k=k, sb=sb, ts=ts: e.dma_start(out=outTv[:, k, ts], in_=c.tmp[:, sb, :]), reads=[c.r_tmp[sb]])

        for t in range(1 if STAGES["dbgB"] else NT):
            emit_tile(t)

        sc.finish("sp")
        sc.replay(nc)
    return nc, sc


def alibi_slopes():
    return np.array([2.0 ** (-8.0 * (i + 1) / NH) for i in range(NH)], dtype=np.float64)


def prep_B(inputs, core, x1T, KT_full, V_full):
    b, role = core // 2, core % 2
    f = np.float32
    col = lambda v: np.ascontiguousarray(np.asarray(v, f).reshape(8, 128).T)
    vecs = np.stack([col(inputs["b_norm"][0]), col(inputs["ffn_norm"][1]), col(inputs["final_norm"])], axis=1)
    slopes = alibi_slopes()
    pos = np.arange(S)
    khi, klo = pos // 128, pos % 128
    out = {}
    K4 = KT_full.reshape(NH, 2, 64, S)
    V4 = V_full.reshape(64, 128, NH, 128)
    for par, name in ((0, "e"), (1, "o")):
        shifted = (tile_index(role, par) != 2 * par + 1)
        kd = np.zeros((NH, KROWS, 2, S), NPBF)
        vd = np.zeros((NH, 128, 64, 128), NPBF)
        aug = np.zeros((NH, 4, S), np.float64)
        aug[:, 0, :] = 1.0
        aug[:, 1, :] = 1.0
        if not shifted:
            kd[:, 0:64, :, :] = K4.transpose(0, 2, 1, 3)
            vd[:] = V4.transpose(2, 1, 0, 3)
            aug[:, 2, :] = slopes[:, None] * 128.0 * khi[None, :]
            aug[:, 3, :] = slopes[:, None] * klo[None, :]
        else:
            kd[:, 0:64, :, 512:] = K4.transpose(0, 2, 1, 3)[:, :, :, :S - 512]
            vd[:, :, 4:, :] = V4.transpose(2, 1, 0, 3)[:, :, :60, :]
            aug[:, 2, 512:] = slopes[:, None] * 128.0 * khi[None, :S - 512]
            aug[:, 3, 512:] = slopes[:, None] * klo[None, :S - 512]
            aug[:, 2, :512] = slopes[:, None] * 128.0 * (-200.0)
        kd[:, 64:68, 0, :] = aug.astype(NPBF)
        kd[:, 64:68, 1, :] = aug.astype(NPBF)
        out["kd_" + name] = kd
        out["vd_" + name] = vd
    qaug = np.zeros((NT, 4, NH, T), np.float64)
    for j in range(NT):
        qpos = tile_index(role, j) * T + np.arange(T)
        qhi, qlo = qpos // 128, qpos % 128
        qaug[j, 0] = -8.0 * slopes[:, None] * 128.0 * qhi[None, :]
        qaug[j, 1] = -8.0 * slopes[:, None] * qlo[None, :]
        qaug[j, 2] = 8.0
        qaug[j, 3] = 8.0
    kk = np.arange(128)[:, None, None]
    dd = np.arange(4)[None, :, None]
    qq = np.arange(T)[None, None, :]
    masks = np.where(qq - 128 * dd - kk >= 0, 0.0, -240000.0).astype(NPBF)
    out.update({
        "x1T": x1T, "vecs": np.ascontiguousarray(vecs), "w_q": np.asarray(inputs["b_w_q"][0], f), "w_o": np.asarray(inputs["b_w_o"][0], f),
        "w_gu": np.asarray(inputs["ffn_w_gu"][1], f), "w_dn": np.asarray(inputs["ffn_w_down"][1], f),
        "qaug": qaug.astype(NPBF), "masks": masks, "ident": np.eye(128, dtype=np.float32).astype(NPBF),
        "lamb": np.ascontiguousarray(np.broadcast_to(np.asarray(inputs["b_lambda"][0], f).reshape(1, 256), (128, 256))),
        "subln": np.ascontiguousarray(np.asarray(inputs["b_subln"][0], f).reshape(128, 1)),
    })
    return out


def run_B(inputs, resA):
    if "B" not in _CACHE:
        _CACHE["B"] = build_B()[0]
    nc = _CACHE["B"]
    in_maps = []
    for b in range(B):
        KT_full = np.zeros((D, S), NPBF)
        V_full = np.zeros((S, D), NPBF)
        for role in range(2):
            r = resA[2 * b + role]
            for j in range(NT):
                i = tile_index(role, j)
                KT_full[:, i * T:(i + 1) * T] = r["kT"][:, j * T:(j + 1) * T]
                V_full[i * T:(i + 1) * T, :] = r["v"][j * T:(j + 1) * T, :]
        for role in range(2):
            in_maps.append(prep_B(inputs, 2 * b + role, np.asarray(resA[2 * b + role]["x1T"]), KT_full, V_full))
    res = run_bass_kernel_spmd(nc, in_maps, core_ids=list(range(8)))
    return res.results


def kernel_unfused(**inputs):
    resA = run_A(inputs)
    resB = run_B(inputs, resA)
    out = np.zeros((B, S, D), np.float32)
    for core in range(8):
        b, role = core // 2, core % 2
        oT = np.asarray(resB[core]["outT"])
        for j in range(NT):
            i = tile_index(role, j)
            out[b, i * T:(i + 1) * T, :] = oT[:, j * T:(j + 1) * T].T
    return out


def sched_coll(sc, fn, reads=(), writes=()):
    toks = sc._deps(reads, writes)
    idx = sc.dn.setdefault("coll", 0) % 4
    sc.dn["coll"] += 1
    key = ("c", "pool", idx)
    cnt = sc.dcnt.get(key, 0)
    if cnt > 0:
        toks.append((key, cnt))
    sc._wait("pool", toks)
    sc.dcnt[key] = cnt + 1
    tok = (key, cnt + 1)
    sc.q["pool"].append(("op", fn, tok))
    sc._mark(tok, reads, writes)
    sc.n_inst += 1
    return tok


def build_F():
    import contextlib
    nc = bass.Bass("TRN2", target_bir_lowering=False)
    sc = Sched()
    c = Ctx()
    din = lambda name, shape, dt=F32: nc.dram_tensor(name, shape, dt, kind="ExternalInput").ap()
    xT_d = din("xT", [D, TOK])
    vecs_d = din("vecs", [128, 7, 8])
    w_in_d = din("w_in", [D, 2 * D])
    w_spT_d = din("w_spT", [128, 8, 128])
    tril_d = din("trilT", [128, 128])
    bsp_d = din("bsp", [128, 8, 128])
    w_out_d = din("w_out", [D, D])
    w_gu0_d = din("w_gu0", [D, 2 * DFF])
    w_dn0_d = din("w_dn0", [DFF, D])
    w_kv_d = din("w_kv", [D, 2 * D])
    w_q_d = din("w_q", [D, D])
    w_o_d = din("w_o", [D, D])
    w_gu1_d = din("w_gu1", [D, 2 * DFF])
    w_dn1_d = din("w_dn1", [DFF, D])
    kaug_d = din("kaug", [2, NH, 5, 2, S], BF16)
    qaug_d = din("qaug", [NT, 5, NH, T], BF16)
    wmask_d = din("wmask", [128, 1408], BF16)
    sel_d = din("sel", [128, 4, 128], BF16)
    lamb_d = din("lamb", [128, 256])
    subln_d = din("subln", [128, 1])
    outT_o = nc.dram_tensor("outT", [D, TOK], F32, kind="ExternalOutput").ap()
    dint = lambda name, shape: nc.dram_tensor(name, shape, BF16, kind="Internal").ap()
    wb = {}
    for name, src in (("in", w_in_d), ("out", w_out_d), ("gu0", w_gu0_d), ("dn0", w_dn0_d), ("kv", w_kv_d),
                      ("q", w_q_d), ("o", w_o_d), ("gu1", w_gu1_d), ("dn1", w_dn1_d)):
        wb[name] = (dint("wb_" + name, list(src.shape)), src)
    snd = [nc.dram_tensor("snd%d" % t, [2048, T], BF16) for t in range(NT)]
    gat = [nc.dram_tensor("gat%d" % t, [4096, T], BF16) for t in range(NT)]

    with contextlib.ExitStack() as es:
        wr = alloc_common(nc, es, c, sc, nwslab=2)
        vecs = es.enter_context(nc.sbuf_tensor("s_vecs", [128, 7, 8], F32))
        wmask = es.enter_context(nc.sbuf_tensor("s_wmask", [128, 1408], BF16))
        sel = es.enter_context(nc.sbuf_tensor("s_sel", [128, 4, 128], BF16))
        sm = es.enter_context(nc.sbuf_tensor("s_sm", [128, 8], F32))
        rc = c.r_const
        r_w = {n: Res() for n in wb}
        r_c2, r_c3, r_c4, r_c5, r_c6, r_c7 = [Res() for _ in range(6)]
        r_sm = Res()
        r_snd = [Res() for _ in range(NT)]
        r_gat = [Res() for _ in range(NT)]

        sc.dma("sp", lambda e: e.dma_start(out=vecs[:], in_=vecs_d[:, :, :]), writes=[rc])
        sc.dma("sp", lambda e: e.dma_start(out=wmask[:], in_=wmask_d[:, :]), writes=[r_c5])
        sc.dma("sp", lambda e: e.dma_start(out=sel[:], in_=sel_d[:, :, :]), writes=[r_c6])
        sc.dma("sp", lambda e: e.dma_start(out=sm[:, 7:8], in_=subln_d[:, :]), writes=[r_c7])
        lamb = c.tmp[:, 0, 0:256]
        scr = c.tmp[:, 1, 0:64]
        sc.dma("sp", lambda e: e.dma_start(out=lamb, in_=lamb_d[:, :]), writes=[c.r_tmp[0]])
        sc.op("dve", lambda e: e.scalar_tensor_tensor(out=scr, in0=lamb[:, 0:64], scalar=1.0, in1=lamb[:, 64:128], op0=ALU.mult, op1=ALU.mult, accum_out=sm[:, 0:1]),
              reads=[c.r_tmp[0]], writes=[r_sm, c.r_tmp[1]])
        sc.op("dve", lambda e: e.scalar_tensor_tensor(out=scr, in0=lamb[:, 128:192], scalar=1.0, in1=lamb[:, 192:256], op0=ALU.mult, op1=ALU.mult, accum_out=sm[:, 1:2]),
              reads=[c.r_tmp[0], r_sm], writes=[r_sm, c.r_tmp[1]])
        sc.op("act", lambda e: e.activation(out=sm[:, 2:4], in_=sm[:, 0:2], func=AF.Exp), reads=[r_sm], writes=[r_sm])
        sc.op("dve", lambda e: e.tensor_tensor(out=sm[:, 4:5], in0=sm[:, 3:4], in1=sm[:, 2:3], op=ALU.subtract), reads=[r_sm], writes=[r_sm])
        sc.op("dve", lambda e: e.tensor_scalar(out=sm[:, 4:5], in0=sm[:, 4:5], scalar1=-LAMBDA_INIT, scalar2=None, op0=ALU.add), reads=[r_sm], writes=[r_sm])
        sc.op("dve", lambda e: e.tensor_scalar(out=sm[:, 5:6], in0=sm[:, 7:8], scalar1=1.0 - LAMBDA_INIT, scalar2=None, op0=ALU.mult), reads=[r_sm, r_c7], writes=[r_sm])

        esA = es.enter_context(contextlib.ExitStack())
        tril = esA.enter_context(nc.sbuf_tensor("s_tril", [128, 128], F32))
        wcT = esA.enter_context(nc.sbuf_tensor("s_wcT", [128, 8, 128], BF16))
        bsp = esA.enter_context(nc.sbuf_tensor("s_bsp", [128, 8, 128], F32))
        vstat = esA.enter_context(nc.sbuf_tensor("s_vstat", [128, 2, 6], F32))
        vmv = esA.enter_context(nc.sbuf_tensor("s_vmv", [128, 4], F32))
        nh1 = c.nh1
        r_vstat, r_vmv = Res(), Res()
        wspT_f = c.tmp[:].rearrange("p a (b c) -> p (a b) c", c=128)
        sc.dma("sp", lambda e: e.dma_start(out=wspT_f, in_=w_spT_d[:, :, :]), writes=c.r_tmp)
        sc.dma("sp", lambda e: e.dma_start(out=tril[:], in_=tril_d[:, :]), writes=[r_c2])
        sc.dma("sp", lambda e: e.dma_start(out=bsp[:], in_=bsp_d[:, :, :]), writes=[r_c3])
        for g in range(8):
            sc.op("dve", lambda e, g=g: e.tensor_tensor(out=wcT[:, g, :], in0=wspT_f[:, g, :], in1=tril[:], op=ALU.mult),
                  reads=[r_c2] + c.r_tmp, writes=[r_c4])
        xTv = xT_d.rearrange("(k p) t -> p k t", p=128)
        def load_x(t):
            sc.dma("sp", lambda e, t=t: e.dma_start(out=c.xT[:, :, t * T:(t + 1) * T], in_=xTv[:, :, t * T:(t + 1) * T]),
                   writes=[c.r_x[k][t] for k in range(8)])
        load_x(0)
        for name in ("in", "out", "gu0", "dn0", "kv"):
            cast_weight(sc, wb[name][0], wb[name][1], r_w[name])

        vf = c.arena[:, 16:20, :].rearrange("p a b -> p (a b)").bitcast(F32)
        r_vfslots = c.r_ar[16:20]
        groups = [[0, 1], [2, 3], [4, 5], [6, 7]]
        deferred = []

        def emit_gmlp(t, ts):
            emit_norm(sc, c, t, vecs[:, 0, :])
            while deferred:
                deferred.pop(0)()

            def evac_v(ch, b0):
                for hf in range(2):
                    sc.op("act", lambda e, hf=hf, b0=b0: e.activation(out=vf[:, hf * 512:(hf + 1) * 512], in_=c.ps[:, b0 + hf, :], func=AF.Gelu_apprx_tanh),
                          reads=[c.r_ps[b0 + hf]], writes=r_vfslots[2 * hf:2 * hf + 2])
                for hf in range(2):
                    sc.op("dve", lambda e, hf=hf: e.bn_stats(out=vstat[:, hf, :], in_=vf[:, hf * 512:(hf + 1) * 512]),
                          reads=r_vfslots[2 * hf:2 * hf + 2], writes=[r_vstat])
                sc.op("dve", lambda e: e.bn_aggr(out=vmv[:, 0:2], in_=vstat[:].rearrange("p a b -> p (a b)")), reads=[r_vstat], writes=[r_vmv])
                sc.op("dve", lambda e: e.tensor_scalar(out=vmv[:, 2:3], in0=vmv[:, 1:2], scalar1=EPS, scalar2=None, op0=ALU.add),
                      reads=[r_vmv], writes=[r_vmv])
                sc.op("pool", lambda e: e.tensor_tensor(out=vmv[:, 3:4], in0=vmv[:, 2:3], in1=nh1[:], op=ALU.pow),
                      reads=[r_vmv, rc], writes=[r_vmv])
                vn = c.arena[:, 8 + 2 * ch:10 + 2 * ch, :].rearrange("p a b -> p (a b)")
                sc.op("dve", lambda e, vn=vn: e.tensor_scalar(out=vn, in0=vf, scalar1=vmv[:, 0:1], scalar2=vmv[:, 3:4], op0=ALU.subtract, op1=ALU.mult),
                      reads=[r_vmv] + r_vfslots, writes=c.r_ar[8 + 2 * ch:10 + 2 * ch])
            emit_proj_tm(sc, c, wr, wb["in"][0], r_w["in"], D, evac_v)

            def evac_u(n, bank):
                sc.op("act", lambda e, n=n, bank=bank: e.activation(out=c.arena[:, n, :], in_=c.ps[:, bank, :], func=AF.Gelu_apprx_tanh),
                      reads=[c.r_ps[bank]], writes=[c.r_ar[n]])
                g = n
                zb = 4 + (g % 4)
                for ch in range(4):
                    vn = c.arena[:, 8 + 2 * ch:10 + 2 * ch, :].rearrange("p a b -> p (a b)")
                    sc.op("pe", lambda e, g=g, ch=ch, zb=zb, vn=vn: e.matmul(c.ps[:, zb, ch * 128:(ch + 1) * 128], lhsT=vn[:, g * 128:(g + 1) * 128], rhs=wcT[:, g, :],
                                                                              start=True, stop=True),
                          reads=c.r_ar[8 + 2 * ch:10 + 2 * ch] + [r_c4], writes=[c.r_ps[zb]], inc=(ch == 3))
                sb = g % 2
                sc.op("dve", lambda e, g=g, zb=zb, sb=sb: e.scalar_tensor_tensor(
                    out=c.tmp[:, sb, :].rearrange("p (a b) -> p a b", a=4), in0=c.ps[:, zb, :].rearrange("p (a b) -> p a b", a=4),
                    scalar=vecs[:, 1, g:g + 1], in1=bsp[:, g, :].unsqueeze(1).broadcast_to([128, 4, 128]), op0=ALU.mult, op1=ALU.add),
                    reads=[c.r_ps[zb], rc, r_c3], writes=[c.r_tmp[sb]])
                sc.op("dve", lambda e, g=g, sb=sb: e.tensor_tensor(out=c.arena[:, g, :], in0=c.tmp[:, sb, :], in1=c.arena[:, g, :], op=ALU.mult),
                      reads=[c.r_tmp[sb], c.r_ar[g]], writes=[c.r_ar[g]])
            emit_proj_fm(sc, c, wr, wb["in"][0], r_w["in"], 0, D, evac_u)

            def evac_res(n, bank):
                sc.op("dve", lambda e, n=n, bank=bank: e.tensor_tensor(out=c.xT[:, n, ts], in0=c.xT[:, n, ts], in1=c.ps[:, bank, :], op=ALU.add),
                      reads=[c.r_ps[bank], c.r_x[n][t]], writes=[c.r_x[n][t]])
            def evac_res_sq(n, bank):
                evac_res(n, bank)
                emit_square(sc, c, t, n)
            emit_proj_fm(sc, c, wr, wb["out"][0], r_w["out"], 0, D, evac_res_sq, src=c.arena, r_src=c.r_ar)

        def emit_kv(t, ts):
            emit_norm(sc, c, t, vecs[:, 3, :], squares_done=True)
            sndk = snd[t][0:1024, :].rearrange("(k p) t -> p k t", p=128)
            sndv = snd[t][1024:2048, :].rearrange("(h p) (c d) -> p c h d", p=128, d=128)

            def evac_k(n, bank):
                sc.op("act", lambda e, n=n, bank=bank: e.copy(out=c.arena[:, n, :], in_=c.ps[:, bank, :]),
                      reads=[c.r_ps[bank]], writes=[c.r_ar[n]])
            emit_proj_fm(sc, c, wr, wb["kv"][0], r_w["kv"], 0, D, evac_k, kouter=True)
            sc.dma("act", lambda e: e.dma_start(out=sndk, in_=c.arena[:, 0:8, :]), reads=c.r_ar[0:8], writes=[r_snd[t]])

            def evac_vv(ch, b0):
                vst = c.arena[:, 8 + 2 * ch:10 + 2 * ch, :].rearrange("p a b -> p (a b)")
                sc.op("act", lambda e, vst=vst, b0=b0: e.copy(out=vst[:, 0:512], in_=c.ps[:, b0, :]),
                      reads=[c.r_ps[b0]], writes=[c.r_ar[8 + 2 * ch]])
                sc.op("dve", lambda e, vst=vst, b0=b0: e.tensor_copy(out=vst[:, 512:1024], in_=c.ps[:, b0 + 1, :]),
                      reads=[c.r_ps[b0 + 1]], writes=[c.r_ar[9 + 2 * ch]])
            emit_proj_tm(sc, c, wr, wb["kv"][0], r_w["kv"], D, evac_vv)
            for ch in range(4):
                sc.dma("act", lambda e, ch=ch: e.dma_start(out=sndv[:, ch], in_=c.arena[:, 8 + 2 * ch:10 + 2 * ch, :].rearrange("p a (h d) -> p (a h) d", d=128)),
                       reads=c.r_ar[8 + 2 * ch:10 + 2 * ch], writes=[r_snd[t]])

            def do_gather(t=t):
                sched_coll(sc, lambda e, t=t: e.collective_compute("AllGather", ALU.bypass, replica_groups=groups,
                                                                   ins=[snd[t].ap().opt()], outs=[gat[t].ap().opt()]),
                           reads=[r_snd[t]], writes=[r_gat[t]])
            deferred.append(do_gather)

        for t in range(NT):
            ts = slice(t * T, (t + 1) * T)
            emit_gmlp(t, ts)
            if t == NT - 1:
                snap_a = {e_: sc.cnt[e_] for e_ in COMPUTE if sc.cnt[e_] > 0}
                snap_a.update({k_: (16 * v_ if k_[0] == "d" else v_) for k_, v_ in sc.dcnt.items()})
            if t + 1 < NT:
                load_x(t + 1)
            if t == 1:
                for name in ("q", "o", "gu1", "dn1"):
                    cast_weight(sc, wb[name][0], wb[name][1], r_w[name])
            emit_norm(sc, c, t, vecs[:, 2, :], squares_done=True)
            emit_ffn(sc, c, t, wr, wb["gu0"][0], wb["dn0"][0], r_w["gu0"], r_w["dn0"], post_add=lambda nn, t=t: emit_square(sc, c, t, nn))
            emit_kv(t, ts)
        while deferred:
            deferred.pop(0)()

        esA.close()
        NKV = 4
        kring = es.enter_context(nc.sbuf_tensor("s_kring", [69, NKV, 2, 512], BF16))
        vring = es.enter_context(nc.sbuf_tensor("s_vring", [128, NKV, 4, 128], BF16))
        snapshot = snap_a
        r_kv = [Res() for _ in range(NKV)]
        for r_ in r_kv:
            r_.r = dict(snapshot)
        r_qaug = Res()
        r_pt = c.r_ar[16:20]
        Qt = c.arena[:, 0:16, :].rearrange("p (h i) t -> p h i t", i=2)
        onT, r_on = c.sq, c.r_sq
        outTv = outT_o.rearrange("(k p) t -> p k t", p=128)
        strot = [0]
        ptrot = [0]
        kvn = [0]

        def load_kv(h, cp, hf, dg):
            b = kvn[0] % NKV
            kvn[0] += 1
            r = r_kv[b]
            ranks = (0, 1) if cp % 2 == 0 else (1, 0)
            rk = ranks[hf]
            g_ = gat[cp]
            ksrc = g_[rk * 2048 + h * 128:rk * 2048 + (h + 1) * 128, :].rearrange("(i d) t -> d i t", d=64)
            sc.dma("sp", lambda e, b=b, ksrc=ksrc: e.dma_start(out=kring[0:64, b, :, :], in_=ksrc), reads=[r_gat[cp]], writes=[r])
            vsrc = g_[rk * 2048 + 1024 + h * 128:rk * 2048 + 1024 + (h + 1) * 128, :].rearrange("p (c d) -> p c d", d=128)
            sc.dma("sp", lambda e, b=b, vsrc=vsrc: e.dma_start(out=vring[:, b, :, :], in_=vsrc), reads=[r_gat[cp]], writes=[r])
            sc.dma("sp", lambda e, b=b: e.dma_start(out=kring[64:69, b, :, :], in_=kaug_d[dg, h, :, :, cp * 1024 + hf * 512:cp * 1024 + (hf + 1) * 512]), writes=[r])
            return b, r

        def emit_attention(j):
            par = j % 2
            chunks = [(h, ci, hf) for h in range(NH) for ci in range(j + 1) for hf in range(2)]
            loaded = {}

            def ensure(idx):
                if idx < len(chunks) and idx not in loaded:
                    h, ci, hf = chunks[idx]
                    loaded[idx] = load_kv(h, j - ci, hf, 1 if ci == 0 else 0)
            steps = []
            for idx, (h, ci, hf) in enumerate(chunks):
                for o4 in range(4):
                    for i in range(2):
                        steps.append((idx, h, ci, hf * 4 + o4, i))
            nsteps_h = (j + 1) * 16
            pend = []
            LAG = 3

            def emit_av(item):
                (idx, h, ci, o, i, pt, first, last) = item
                b, rkv = loaded[idx]
                sc.op("pe", lambda e, b=b, o=o, i=i, pt=pt, first=first, last=last: e.matmul(c.ps[:, 4 + i, :], lhsT=vring[:, b, o % 4, :], rhs=c.arena[:, 16 + pt, :], start=first, stop=last),
                      reads=[rkv, r_pt[pt]], writes=[c.r_ps[4 + i]], inc=True)
                if i == 0:
                    sc.op("pe", lambda e, i=i, pt=pt, first=first, last=last: e.matmul(c.ps[:, 6 + i, :], lhsT=c.ones[:], rhs=c.arena[:, 16 + pt, :], start=first, stop=last),
                          reads=[r_pt[pt], rc], writes=[c.r_ps[6 + i]], inc=True)
                elif first:
                    sc.op("dve", lambda e, i=i, pt=pt: e.tensor_copy(out=c.ps[:, 6 + i, :], in_=c.arena[:, 16 + pt, :]),
                          reads=[r_pt[pt]], writes=[c.r_ps[6 + i]])
                else:
                    sc.op("dve", lambda e, i=i, pt=pt: e.tensor_tensor(out=c.ps[:, 6 + i, :], in0=c.ps[:, 6 + i, :], in1=c.arena[:, 16 + pt, :], op=ALU.add),
                          reads=[r_pt[pt], c.r_ps[6 + i]], writes=[c.r_ps[6 + i]])
                if last and i == 1:
                    emit_head_post(h)

            def emit_head_post(h):
                a_, b_ = c.tmp[:, 0, :], c.tmp[:, 1, :]
                sc.op("act", lambda e: e.copy(out=c.arena[:, 21, :], in_=c.ps[:, 7, :]), reads=[c.r_ps[7]], writes=[c.r_ar[21]])
                sc.op("act", lambda e: e.copy(out=a_, in_=c.ps[:, 4, :]), reads=[c.r_ps[4]], writes=[c.r_tmp[0]])
                sc.op("dve", lambda e: e.tensor_copy(out=b_, in_=c.ps[:, 5, :]), reads=[c.r_ps[5]], writes=[c.r_tmp[1]])
                sc.op("dve", lambda e: e.reciprocal(out=c.ms[:], in_=c.ps[:, 6, :]), reads=[c.r_ps[6]], writes=[c.r_ms])
                bank = strot[0] % 4
                strot[0] += 1
                sc.op("pe", lambda e, bank=bank: e.matmul(c.ps[:, bank, :], lhsT=c.ones[:], rhs=c.arena[:, 21, :], start=True, stop=True),
                      reads=[c.r_ar[21], rc], writes=[c.r_ps[bank]])
                sc.op("dve", lambda e: e.tensor_tensor(out=a_, in0=a_, in1=c.ms[:], op=ALU.mult), reads=[c.r_tmp[0], c.r_ms], writes=[c.r_tmp[0]])
                sc.op("dve", lambda e, bank=bank: e.reciprocal(out=c.rstd[:], in_=c.ps[:, bank, :]), reads=[c.r_ps[bank]], writes=[c.r_rstd])
                sc.op("dve", lambda e: e.tensor_tensor(out=b_, in0=b_, in1=c.rstd[:], op=ALU.mult), reads=[c.r_tmp[1], c.r_rstd], writes=[c.r_tmp[1]])
                sc.op("dve", lambda e: e.scalar_tensor_tensor(out=a_, in0=b_, scalar=sm[:, 4:5], in1=a_, op0=ALU.mult, op1=ALU.add),
                      reads=[c.r_tmp[1], c.r_tmp[0], r_sm], writes=[c.r_tmp[0]])
                sc.op("dve", lambda e: e.tensor_tensor(out=c.arena[:, 20, :], in0=a_, in1=a_, op=ALU.mult), reads=[c.r_tmp[0]], writes=[c.r_ar[20]])
                bank = strot[0] % 4
                strot[0] += 1
                sc.op("pe", lambda e, bank=bank: e.matmul(c.ps[:, bank, :], lhsT=c.ones[:], rhs=c.arena[:, 20, :], start=True, stop=True),
                      reads=[c.r_ar[20], rc], writes=[c.r_ps[bank]])
                sc.op("act", lambda e, bank=bank: e.activation(out=c.ms[:], in_=c.ps[:, bank, :], func=AF.Ln, scale=1.0 / 128, bias=EPS),
                      reads=[c.r_ps[bank]], writes=[c.r_ms])
                sc.op("act", lambda e: e.activation(out=c.rstd[:], in_=c.ms[:], func=AF.Exp, scale=-0.5),
                      reads=[c.r_ms], writes=[c.r_rstd])
                sc.op("dve", lambda e, h=h: e.scalar_tensor_tensor(out=onT[:, h, :], in0=a_, scalar=sm[:, 5:6], in1=c.rstd[:], op0=ALU.mult, op1=ALU.mult),
                      reads=[c.r_tmp[0], c.r_rstd, r_sm], writes=[r_on[h]])

            def diag_pat(d):
                off = 896 - 128 * d
                return wmask[:, off:off + 512]
            negpat = wmask[:, 0:512]
            selA, selB = sel[:, 2 * par, :], sel[:, 2 * par + 1, :]

            for sidx, (idx, h, ci, o, i) in enumerate(steps):
                if o % 4 == 0 and i == 0:
                    ensure(idx)
                    ensure(idx + 1)
                    ensure(idx + 2)
                if o % 4 == 2 and i == 0:
                    ensure(idx + 3)
                b, rkv = loaded[idx]
                bank = strot[0] % 4
                strot[0] += 1
                pt = ptrot[0] % 4
                ptrot[0] += 1
                diag = (ci == 0)
                sc.op("pe", lambda e, b=b, h=h, o=o, i=i, bank=bank, diag=diag: e.matmul(c.ps[:, bank, :], lhsT=kring[0:69, b, i, (o % 4) * 128:(o % 4 + 1) * 128], rhs=Qt[0:69, h, i, :], start=True, stop=(not diag)),
                      reads=[rkv, c.r_ar[2 * h + i], r_qaug], writes=[c.r_ps[bank]], inc=(not diag))
                if diag:
                    if o < 4:
                        mm = [(selA, diag_pat(o))]
                    else:
                        mm = [(selB, diag_pat(o - 4))]
                    for mi, (lt, rh) in enumerate(mm):
                        lastm = (mi == len(mm) - 1)
                        sc.op("pe", lambda e, bank=bank, lt=lt, rh=rh, lastm=lastm: e.matmul(c.ps[:, bank, :], lhsT=lt, rhs=rh, start=False, stop=lastm),
                              reads=[r_c5, r_c6], writes=[c.r_ps[bank]], inc=lastm)
                sc.op("act", lambda e, bank=bank, pt=pt: e.activation(out=c.arena[:, 16 + pt, :], in_=c.ps[:, bank, :], func=AF.Exp, scale=0.125),
                      reads=[c.r_ps[bank]], writes=[r_pt[pt]])
                hs = sidx - h * nsteps_h
                first = hs < 2
                last = hs >= nsteps_h - 2
                pend.append((idx, h, ci, o, i, pt, first, last))
                if len(pend) > LAG:
                    emit_av(pend.pop(0))
            while pend:
                emit_av(pend.pop(0))

        def emit_tile_B(t):
            ts = slice(t * T, (t + 1) * T)
            emit_norm(sc, c, t, vecs[:, 4, :])
            for i in range(2):
                sc.dma("sp", lambda e, t=t, i=i: e.dma_start(out=Qt[64:69, :, i, :], in_=qaug_d[t, :, :, :]), writes=[r_qaug] + c.r_ar[0:16])

            def evac_q(h, bank):
                sc.op("act", lambda e, h=h, bank=bank: e.copy(out=Qt[0:64, h, 0, :], in_=c.ps[0:64, bank, :]), reads=[c.r_ps[bank]], writes=[c.r_ar[2 * h]])
                sc.op("act", lambda e, h=h, bank=bank: e.copy(out=Qt[0:64, h, 1, :], in_=c.ps[64:128, bank, :]), reads=[c.r_ps[bank]], writes=[c.r_ar[2 * h + 1]])
            emit_proj_fm(sc, c, wr, wb["q"][0], r_w["q"], 0, D, evac_q, kouter=True)
            emit_attention(t)

            def evac_res(n, bank):
                sc.op("dve", lambda e, n=n, bank=bank: e.tensor_tensor(out=c.xT[:, n, ts], in0=c.xT[:, n, ts], in1=c.ps[:, bank, :], op=ALU.add),
                      reads=[c.r_ps[bank], c.r_x[n][t]], writes=[c.r_x[n][t]])
            emit_proj_fm(sc, c, wr, wb["o"][0], r_w["o"], 0, D, evac_res, src=onT, r_src=r_on)
            emit_norm(sc, c, t, vecs[:, 5, :])
            emit_ffn(sc, c, t, wr, wb["gu1"][0], wb["dn1"][0], r_w["gu1"], r_w["dn1"], post_add=lambda nn, t=t: emit_square(sc, c, t, nn))
            bank = c.stat_bank
            for k in range(8):
                sc.op("pe", lambda e, k=k: e.matmul(c.ps[:, bank, :], lhsT=c.ones[:], rhs=c.sq[:, k, :], start=(k == 0), stop=(k == 7)),
                      reads=[c.r_sq[k], rc], writes=[c.r_ps[bank]], inc=(k == 7))
            sc.op("act", lambda e: e.activation(out=c.ms[:], in_=c.ps[:, bank, :], func=AF.Sqrt, scale=1.0 / D, bias=EPS),
                  reads=[c.r_ps[bank]], writes=[c.r_ms])
            sc.op("dve", lambda e: e.reciprocal(out=c.rstd[:], in_=c.ms[:]),
                  reads=[c.r_ms], writes=[c.r_rstd])
            for k in range(8):
                sb = k % 2
                sc.op("dve", lambda e, k=k, sb=sb: e.scalar_tensor_tensor(out=c.tmp[:, sb, :], in0=c.xT[:, k, ts], scalar=vecs[:, 6, k:k + 1], in1=c.rstd[:],
                                                                       op0=ALU.mult, op1=ALU.mult),
                      reads=[c.r_x[k][t], c.r_rstd, rc], writes=[c.r_tmp[sb]])
                sc.dma("act", lambda e, k=k, sb=sb: e.dma_start(out=outTv[:, k, ts], in_=c.tmp[:, sb, :]), reads=[c.r_tmp[sb]])

        for t in range(NT):
            emit_tile_B(t)

        sc.finish("sp")
        sc.replay(nc)
    return nc, sc


def prep_F(inputs, core):
    b, role = core // 2, core % 2
    f = np.float32
    a = prep_A(inputs, core)
    col = lambda v: np.ascontiguousarray(np.asarray(v, f).reshape(8, 128).T)
    vecs = np.stack([col(inputs["a_norm"][0]), col(inputs["a_v_norm"][0]), col(inputs["ffn_norm"][0]), col(inputs["kv_norm"]),
                     col(inputs["b_norm"][0]), col(inputs["ffn_norm"][1]), col(inputs["final_norm"])], axis=1)
    slopes = alibi_slopes()
    pos = np.arange(S)
    khi, klo = pos // 128, pos % 128
    kaug = np.zeros((2, NH, 5, 2, S), np.float64)
    kaug[:, :, 0] = 1.0
    kaug[:, :, 1] = 1.0
    kaug[:, :, 2] = (slopes[:, None] * 128.0 * khi[None, :])[None, :, None, :]
    kaug[:, :, 3] = (slopes[:, None] * klo[None, :])[None, :, None, :]
    kaug[1, :, 4] = ((pos % 1024) >= 512).astype(np.float64)[None, None, :]
    qaug = np.zeros((NT, 5, NH, T), np.float64)
    for j in range(NT):
        qpos = tile_index(role, j) * T + np.arange(T)
        qhi, qlo = qpos // 128, qpos % 128
        qaug[j, 0] = -8.0 * slopes[:, None] * 128.0 * qhi[None, :]
        qaug[j, 1] = -8.0 * slopes[:, None] * qlo[None, :]
        qaug[j, 2] = 8.0
        qaug[j, 3] = 8.0
        qaug[j, 4] = -240000.0 if tile_index(role, j) == 2 * j else 0.0
    kk = np.arange(128)[:, None]
    cc = np.arange(1408)[None, :]
    wmask = np.where(cc - 896 - kk >= 0, 0.0, -240000.0).astype(NPBF)
    sel = np.zeros((128, 4, 128), np.float32)
    eye = np.eye(128, dtype=np.float32)
    for par in range(2):
        case_a = (tile_index(role, par) == 2 * par)
        sel[:, 2 * par, :] = eye if case_a else 0.0
        sel[:, 2 * par + 1, :] = 0.0 if case_a else eye
    return {
        "xT": a["xT"], "vecs": np.ascontiguousarray(vecs), "w_in": a["w_in"], "w_spT": a["w_spT"], "trilT": a["trilT"], "bsp": a["bsp"],
        "w_out": a["w_out"], "w_gu0": a["w_gu"], "w_dn0": a["w_dn"], "w_kv": a["w_kv"],
        "w_q": np.asarray(inputs["b_w_q"][0], f), "w_o": np.asarray(inputs["b_w_o"][0], f),
        "w_gu1": np.asarray(inputs["ffn_w_gu"][1], f), "w_dn1": np.asarray(inputs["ffn_w_down"][1], f),
        "kaug": kaug.astype(NPBF), "qaug": qaug.astype(NPBF), "wmask": wmask, "sel": sel.astype(NPBF),
        "lamb": np.ascontiguousarray(np.broadcast_to(np.asarray(inputs["b_lambda"][0], f).reshape(1, 256), (128, 256))),
        "subln": np.ascontiguousarray(np.asarray(inputs["b_subln"][0], f).reshape(128, 1)),
    }


def kernel(**inputs):
    if "F" not in _CACHE:
        _CACHE["F"] = build_F()[0]
    nc = _CACHE["F"]
    in_maps = [prep_F(inputs, cidx) for cidx in range(8)]
    res = run_bass_kernel_spmd(nc, in_maps, core_ids=list(range(8)))
    out = np.zeros((B, S, D), np.float32)
    for core in range(8):
        b, role = core // 2, core % 2
        oT = np.asarray(res.results[core]["outT"])
        for j in range(NT):
            i = tile_index(role, j)
            out[b, i * T:(i + 1) * T, :] = oT[:, j * T:(j + 1) * T].T
    return out
```

```python
import math
import numpy as np
import ml_dtypes
import concourse.bass as bass
import concourse.mybir as mybir
from concourse.bass_utils import run_bass_kernel_spmd

F32 = mybir.dt.float32
BF16 = mybir.dt.bfloat16
AF = mybir.ActivationFunctionType
ALU = mybir.AluOpType
NPBF = ml_dtypes.bfloat16

D = 1024
S = 8192
B = 4
DFF = 2816
NJ = DFF // 128
T = 512
NT = 8
TOK = NT * T
EPS = 1e-6
NH = 8
KROWS = 80
LAMBDA_INIT = 0.8 - 0.6 * math.exp(-0.3 * 1)
STAGES = {"gmlp": True, "ffn": True, "dbg": False, "attn": True, "ffnB": True, "dbgB": False}


def tile_index(role, j):
    if role == 0:
        return 2 * j if j % 2 == 0 else 2 * j + 1
    return 2 * j + 1 if j % 2 == 0 else 2 * j


class Res:
    __slots__ = ("w", "r")

    def __init__(self):
        self.w = None
        self.r = {}


COMPUTE = ("pe", "act", "dve", "pool")


class Sched:
    ND = 12

    def __init__(self):
        self.q = {e: [] for e in COMPUTE + ("sp",)}
        self.cnt = {e: 0 for e in COMPUTE}
        self.waited = {e: {} for e in COMPUTE + ("sp",)}
        self.pending = {e: [] for e in COMPUTE}
        self.dn = {"sp": 0, "pool": 0, "act": 0}
        self.dcnt = {}
        self.n_inst = 0

    def _deps(self, reads, writes):
        toks = []
        for r in reads:
            if r.w is not None:
                toks.append(r.w)
        for w in writes:
            if w.w is not None:
                toks.append(w.w)
            toks.extend(w.r.items())
        return toks

    def _wait(self, eng, toks):
        wd = self.waited[eng]
        for key, val in toks:
            if eng == "pe" and key == "pe":
                continue
            if wd.get(key, 0) < val:
                wd[key] = val
                self.q[eng].append(("wait", key, val))

    @staticmethod
    def _mark(tok, reads, writes):
        key, val = tok
        for r in reads:
            if r.r.get(key, 0) < val:
                r.r[key] = val
        for w in writes:
            w.w = tok
            w.r = {}

    def op(self, eng, fn, reads=(), writes=(), inc=True):
        self._wait(eng, self._deps(reads, writes))
        self.n_inst += 1
        if not inc:
            self._mark((eng, self.cnt[eng] + 1), reads, writes)
            self.q[eng].append(("op", fn, None))
            return None
        self.cnt[eng] += 1
        tok = (eng, self.cnt[eng])
        self.q[eng].append(("op", fn, tok))
        self._mark(tok, reads, writes)
        return tok

    def dma(self, queue, fn, reads=(), writes=()):
        toks = self._deps(reads, writes)
        nd = 6 if queue == "pool" else self.ND
        idx = self.dn[queue] % nd
        self.dn[queue] += 1
        key = ("d", queue, idx)
        c = self.dcnt.get(key, 0)
        if c > 0:
            toks.append((key, 16 * c))
        self._wait(queue, toks)
        self.dcnt[key] = c + 1
        tok = (key, 16 * (c + 1))
        self.q[queue].append(("op", fn, tok))
        self._mark(tok, reads, writes)
        self.n_inst += 1
        return tok

    def finish(self, eng="sp"):
        toks = [(e, self.cnt[e]) for e in COMPUTE if self.cnt[e] > 0]
        toks += [(k, (16 * c if k[0] == "d" else c)) for k, c in self.dcnt.items()]
        for e in COMPUTE + ("sp",):
            self._wait(e, [t for t in toks if t[0] != e])

    def replay(self, nc):
        sem_names = list(COMPUTE) + [k for k in self.dcnt]
        import contextlib
        with contextlib.ExitStack() as es:
            sems = {}
            for i, k in enumerate(sem_names):
                sems[k] = es.enter_context(nc.semaphore("s%d" % i))
            block = es.enter_context(nc.Block())
            engmap = {"pe": block.tensor, "act": block.scalar, "dve": block.vector,
                      "pool": block.gpsimd, "sp": block.sync}
            for ename, starter in engmap.items():
                items = self.q[ename]

                def body(eng, items=items):
                    for it in items:
                        if it[0] == "wait":
                            eng.wait_ge(sems[it[1]], it[2])
                        else:
                            inst = it[1](eng)
                            tok = it[2]
                            if tok is not None:
                                key = tok[0]
                                if isinstance(key, tuple) and key[0] == "c":
                                    inst.then_inc(sems[key])
                                else:
                                    inst.then_inc(sems[key], 16 if isinstance(key, tuple) else 1)
                starter(body)


class Ctx:
    pass


def emit_square(sc, c, t, k):
    ts = slice(t * T, (t + 1) * T)
    sc.op("dve", lambda e, k=k: e.tensor_tensor(out=c.sq[:, k, :], in0=c.xT[:, k, ts], in1=c.xT[:, k, ts], op=ALU.mult),
          reads=[c.r_x[k][t]], writes=[c.r_sq[k]])


def emit_norm(sc, c, t, gcol, squares_done=False, lnexp=False):
    ts = slice(t * T, (t + 1) * T)
    for k in range(8):
        if not squares_done:
            emit_square(sc, c, t, k)
    bank = c.stat_bank
    for k in range(8):
        sc.op("pe", lambda e, k=k: e.matmul(c.ps[:, bank, :], lhsT=c.ones[:], rhs=c.sq[:, k, :], start=(k == 0), stop=(k == 7)),
              reads=[c.r_sq[k], c.r_const], writes=[c.r_ps[bank]], inc=(k == 7))
    if lnexp:
        sc.op("act", lambda e: e.activation(out=c.ms[:], in_=c.ps[:, bank, :], func=AF.Ln, scale=1.0 / D, bias=EPS),
              reads=[c.r_ps[bank]], writes=[c.r_ms])
        sc.op("act", lambda e: e.activation(out=c.rstd[:], in_=c.ms[:], func=AF.Exp, scale=-0.5),
              reads=[c.r_ms], writes=[c.r_rstd])
    else:
        sc.op("act", lambda e: e.activation(out=c.ms[:], in_=c.ps[:, bank, :], func=AF.Sqrt, scale=1.0 / D, bias=EPS),
              reads=[c.r_ps[bank]], writes=[c.r_ms])
        sc.op("dve", lambda e: e.reciprocal(out=c.rstd[:], in_=c.ms[:]),
              reads=[c.r_ms], writes=[c.r_rstd])
    for k in range(8):
        sc.op("dve", lambda e, k=k: e.scalar_tensor_tensor(out=c.hT[:, k, :], in0=c.xT[:, k, ts], scalar=gcol[:, k:k + 1], in1=c.rstd[:],
                                                          op0=ALU.mult, op1=ALU.mult),
              reads=[c.r_x[k][t], c.r_rstd, c.r_const], writes=[c.r_h[k]])


class WRing:
    def __init__(self, sc, tens, n):
        self.sc = sc
        self.t = tens
        self.n = n
        self.res = [Res() for _ in range(n)]
        self.i = 0

    def load(self, parts, src_res):
        b = self.i % self.n
        self.i += 1
        slab = self.t[:, b]
        r = self.res[b]
        for dst_fn, src in parts:
            self.sc.dma("sp", lambda e, dst_fn=dst_fn, src=src, slab=slab: e.dma_start(out=dst_fn(slab), in_=src),
                        reads=[src_res], writes=[r])
        return slab, r


def wview(w):
    return w.rearrange("(k p) n -> p k n", p=128)


def emit_ffn(sc, c, t, wr, wgu, wdn, r_wgu, r_wdn, post_add=None):
    ts = slice(t * T, (t + 1) * T)
    wguv = wview(wgu)
    wdnv = wview(wdn)
    for j2 in range(NJ // 2):
        slab, rs = wr.load([(lambda s: s[:, :, 0:256], wguv[:, :, j2 * 256:(j2 + 1) * 256]),
                            (lambda s: s[:, :, 256:512], wguv[:, :, DFF + j2 * 256:DFF + (j2 + 1) * 256])], r_wgu)
        for jj in range(2):
            j = 2 * j2 + jj
            bg = 2 * (j % 2)
            bu = bg + 1
            for k in range(8):
                sc.op("pe", lambda e, k=k, jj=jj, bg=bg, slab=slab: e.matmul(c.ps[:, bg, :], lhsT=slab[:, k, jj * 128:(jj + 1) * 128], rhs=c.hT[:, k, :],
                                                                              start=(k == 0), stop=(k == 7)),
                      reads=[rs, c.r_h[k]], writes=[c.r_ps[bg]], inc=(k == 7))
            for k in range(8):
                sc.op("pe", lambda e, k=k, jj=jj, bu=bu, slab=slab: e.matmul(c.ps[:, bu, :], lhsT=slab[:, k, 256 + jj * 128:256 + (jj + 1) * 128], rhs=c.hT[:, k, :],
                                                                              start=(k == 0), stop=(k == 7)),
                      reads=[rs, c.r_h[k]], writes=[c.r_ps[bu]], inc=(k == 7))
            sb = j % 2
            sc.op("act", lambda e, bg=bg, sb=sb: e.activation(out=c.tmp[:, sb, :], in_=c.ps[:, bg, :], func=AF.Silu),
                  reads=[c.r_ps[bg]], writes=[c.r_tmp[sb]])
            sc.op("dve", lambda e, j=j, bu=bu, sb=sb: e.tensor_tensor(out=c.arena[:, j, :], in0=c.tmp[:, sb, :], in1=c.ps[:, bu, :], op=ALU.mult),
                  reads=[c.r_tmp[sb], c.r_ps[bu]], writes=[c.r_ar[j]])
    kgroups = [(0, 8), (8, 8), (16, 6)]
    for hf in range(2):
        for (k0, nk) in kgroups:
            slab, rs = wr.load([(lambda s, nk=nk: s[:, 0:nk, :], wdnv[:, k0:k0 + nk, hf * 512:(hf + 1) * 512])], r_wdn)
            for kk in range(nk):
                k = k0 + kk
                for n in range(4):
                    sc.op("pe", lambda e, kk=kk, k=k, n=n, slab=slab: e.matmul(c.ps[:, 4 + n, :], lhsT=slab[:, kk, n * 128:(n + 1) * 128], rhs=c.arena[:, k, :],
                                                                                start=(k == 0), stop=(k == NJ - 1)),
                          reads=[rs, c.r_ar[k]], writes=[c.r_ps[4 + n]], inc=(k == NJ - 1 or (kk == nk - 1 and n == 3)))
        for n in range(4):
            nn = hf * 4 + n
            sc.op("dve", lambda e, n=n, nn=nn: e.tensor_tensor(out=c.xT[:, nn, ts], in0=c.xT[:, nn, ts], in1=c.ps[:, 4 + n, :], op=ALU.add),
                  reads=[c.r_ps[4 + n], c.r_x[nn][t]], writes=[c.r_x[nn][t]])
            if post_add is not None:
                post_add(nn)


def emit_proj_fm(sc, c, wr, w, r_w, col0, ncols, evac, src=None, r_src=None, kouter=False):
    src = c.hT if src is None else src
    r_src = c.r_h if r_src is None else r_src
    wv = wview(w)
    nchunks = ncols // 128
    for s0 in range(0, nchunks, 4):
        slab, rs = wr.load([(lambda s: s, wv[:, :, col0 + s0 * 128:col0 + s0 * 128 + 512])], r_w)
        if s0 == 0 and kouter:
            banks = [(c.mm_rot + q) % 4 for q in range(4)]
            c.mm_rot += 4
            for k in range(8):
                for nn in range(4):
                    sc.op("pe", lambda e, k=k, nn=nn, bank=banks[nn], slab=slab: e.matmul(c.ps[:, bank, :], lhsT=slab[:, k, nn * 128:(nn + 1) * 128], rhs=src[:, k, :],
                                                                                         start=(k == 0), stop=(k == 7)),
                          reads=[rs, r_src[k]], writes=[c.r_ps[banks[nn]]], inc=(k == 7))
            for nn in range(4):
                evac(nn, banks[nn])
            continue
        for nn in range(4):
            n = s0 + nn
            bank = c.mm_rot % 4
            c.mm_rot += 1
            for k in range(8):
                sc.op("pe", lambda e, k=k, nn=nn, bank=bank, slab=slab: e.matmul(c.ps[:, bank, :], lhsT=slab[:, k, nn * 128:(nn + 1) * 128], rhs=src[:, k, :],
                                                                                  start=(k == 0), stop=(k == 7)),
                      reads=[rs, r_src[k]], writes=[c.r_ps[bank]], inc=(k == 7))
            evac(n, bank)


def emit_proj_tm(sc, c, wr, w, r_w, col0, evac):
    wv = wview(w)
    slabs = []
    for hf in range(2):
        slabs.append(wr.load([(lambda s: s, wv[:, :, col0 + hf * 512:col0 + (hf + 1) * 512])], r_w))
    for ch in range(4):
        b0 = 4 + 2 * (ch % 2)
        for hf in range(2):
            slab, rs = slabs[hf]
            for k in range(8):
                sc.op("pe", lambda e, k=k, ch=ch, hf=hf, b0=b0, slab=slab: e.matmul(c.ps[:, b0 + hf, :], lhsT=c.hT[:, k, ch * 128:(ch + 1) * 128], rhs=slab[:, k, :],
                                                                                     start=(k == 0), stop=(k == 7)),
                      reads=[rs, c.r_h[k]], writes=[c.r_ps[b0 + hf]], inc=(k == 7))
        evac(ch, b0)


def alloc_common(nc, es, c, sc, nwslab=3):
    c.xT = es.enter_context(nc.sbuf_tensor("s_xT", [128, 8, TOK], F32))
    c.hT = es.enter_context(nc.sbuf_tensor("s_hT", [128, 8, T], BF16))
    c.sq = es.enter_context(nc.sbuf_tensor("s_sq", [128, 8, T], BF16))
    c.arena = es.enter_context(nc.sbuf_tensor("s_arena", [128, NJ, T], BF16))
    c.ms = es.enter_context(nc.sbuf_tensor("s_ms", [128, T], F32))
    c.rstd = es.enter_context(nc.sbuf_tensor("s_rstd", [128, T], F32))
    c.tmp = es.enter_context(nc.sbuf_tensor("s_tmp", [128, 2, T], F32))
    c.nh1 = es.enter_context(nc.sbuf_tensor("s_nh1", [128, 1], F32))
    c.ones = es.enter_context(nc.sbuf_tensor("s_ones", [128, 128], BF16))
    c.wslab = es.enter_context(nc.sbuf_tensor("s_wslab", [128, nwslab, 8, 512], BF16))
    c.ps = es.enter_context(nc.psum_tensor("p_ps", [128, 8, 512], F32))
    c.r_x = [[Res() for _ in range(NT)] for _ in range(8)]
    c.r_h = [Res() for _ in range(8)]
    c.r_sq = [Res() for _ in range(8)]
    c.r_ar = [Res() for _ in range(NJ)]
    c.r_ms = Res()
    c.r_rstd = Res()
    c.r_tmp = [Res(), Res()]
    c.r_const = Res()
    c.r_ps = [Res() for _ in range(8)]
    c.stat_bank = 3
    c.mm_rot = 0
    sc.op("pool", lambda e: e.memset(c.nh1[:], -0.5), writes=[c.r_const])
    sc.op("pool", lambda e: e.memset(c.ones[:], 1.0), writes=[c.r_const])
    return WRing(sc, c.wslab, nwslab)


def cast_weight(sc, dst, src, r_dst, rows_per_dma=1024):
    K, N = src.shape
    b = max(d for d in range(1, 1025) if N % d == 0)
    a = N // b
    sv = src.rearrange("r (a b) -> (r a) b", b=b)
    dv = dst.rearrange("r (a b) -> (r a) b", b=b)
    rows = K * a
    for r0 in range(0, rows, rows_per_dma):
        r1 = min(rows, r0 + rows_per_dma)
        sc.dma("pool", lambda e, r0=r0, r1=r1: e.dma_start(out=dv[r0:r1, :], in_=sv[r0:r1, :]), writes=[r_dst])


def build_A():
    import contextlib
    nc = bass.Bass("TRN2", target_bir_lowering=False)
    sc = Sched()
    c = Ctx()
    din = lambda name, shape, dt=F32: nc.dram_tensor(name, shape, dt, kind="ExternalInput").ap()
    xT_d = din("xT", [D, TOK])
    vecs_d = din("vecs", [128, 4, 8])
    w_in_d = din("w_in", [D, 2 * D])
    w_spT_d = din("w_spT", [128, 8, 128])
    tril_d = din("trilT", [128, 128])
    bsp_d = din("bsp", [128, 8, 128])
    w_out_d = din("w_out", [D, D])
    w_gu_d = din("w_gu", [D, 2 * DFF])
    w_dn_d = din("w_dn", [DFF, D])
    w_kv_d = din("w_kv", [D, 2 * D])
    x1T_o = nc.dram_tensor("x1T", [D, TOK], F32, kind="ExternalOutput").ap()
    kT_o = nc.dram_tensor("kT", [D, TOK], BF16, kind="ExternalOutput").ap()
    v_o = nc.dram_tensor("v", [TOK, D], BF16, kind="ExternalOutput").ap()
    dint = lambda name, shape: nc.dram_tensor(name, shape, BF16, kind="Internal").ap()
    if STAGES["dbg"]:
        dbg_u = nc.dram_tensor("dbg_u", [128, 8, T], BF16, kind="ExternalOutput").ap()
        dbg_vn = nc.dram_tensor("dbg_vn", [128, 8, T], BF16, kind="ExternalOutput").ap()
        dbg_uz = nc.dram_tensor("dbg_uz", [128, 8, T], BF16, kind="ExternalOutput").ap()
        dbg_z = nc.dram_tensor("dbg_z", [128, 8, T], F32, kind="ExternalOutput").ap()
        dbg_vf = nc.dram_tensor("dbg_vf", [128, 4, 1024], F32, kind="ExternalOutput").ap()
        dbg_mv = nc.dram_tensor("dbg_mv", [128, 4, 4], F32, kind="ExternalOutput").ap()
    w_in_b = dint("w_in_b", [D, 2 * D])
    w_out_b = dint("w_out_b", [D, D])
    w_gu_b = dint("w_gu_b", [D, 2 * DFF])
    w_dn_b = dint("w_dn_b", [DFF, D])
    w_kv_b = dint("w_kv_b", [D, 2 * D])

    with contextlib.ExitStack() as es:
        wr = alloc_common(nc, es, c, sc, nwslab=3)
        vecs = es.enter_context(nc.sbuf_tensor("s_vecs", [128, 4, 8], F32))
        wspT_f = c.tmp[:].rearrange("p a (b c) -> p (a b) c", c=128)
        tril = es.enter_context(nc.sbuf_tensor("s_tril", [128, 128], F32))
        wcT = es.enter_context(nc.sbuf_tensor("s_wcT", [128, 8, 128], BF16))
        bsp = es.enter_context(nc.sbuf_tensor("s_bsp", [128, 8, 128], F32))
        vstat = es.enter_context(nc.sbuf_tensor("s_vstat", [128, 2, 6], F32))
        vmv = es.enter_context(nc.sbuf_tensor("s_vmv", [128, 4], F32))
        nh1 = c.nh1
        r_vf = Res()
        r_vstat = Res()
        r_vmv = Res()
        r_w = {n: Res() for n in ("in", "out", "gu", "dn", "kv")}

        rc = c.r_const
        r_c2, r_c3, r_c4 = Res(), Res(), Res()
        sc.dma("sp", lambda e: e.dma_start(out=vecs[:], in_=vecs_d[:, :, :]), writes=[rc])
        sc.dma("sp", lambda e: e.dma_start(out=wspT_f, in_=w_spT_d[:, :, :]), writes=[rc] + c.r_tmp)
        sc.dma("sp", lambda e: e.dma_start(out=tril[:], in_=tril_d[:, :]), writes=[r_c2])
        sc.dma("sp", lambda e: e.dma_start(out=bsp[:], in_=bsp_d[:, :, :]), writes=[r_c3])
        for g in range(8):
            sc.op("dve", lambda e, g=g: e.tensor_tensor(out=wcT[:, g, :], in0=wspT_f[:, g, :], in1=tril[:], op=ALU.mult),
                  reads=[r_c2] + c.r_tmp, writes=[r_c4])
        xTv = xT_d.rearrange("(k p) t -> p k t", p=128)
        for t in range(NT):
            sc.dma("sp", lambda e, t=t: e.dma_start(out=c.xT[:, :, t * T:(t + 1) * T], in_=xTv[:, :, t * T:(t + 1) * T]),
                   writes=[c.r_x[k][t] for k in range(8)])
        cast_weight(sc, w_in_b, w_in_d, r_w["in"])
        cast_weight(sc, w_out_b, w_out_d, r_w["out"])
        cast_weight(sc, w_gu_b, w_gu_d, r_w["gu"])
        cast_weight(sc, w_dn_b, w_dn_d, r_w["dn"])
        cast_weight(sc, w_kv_b, w_kv_d, r_w["kv"])

        x1Tv = x1T_o.rearrange("(k p) t -> p k t", p=128)
        kTv = kT_o.rearrange("(k p) t -> p k t", p=128)
        vov = v_o.rearrange("(c p) f -> p c f", p=128)
        vf = c.arena[:, 16:20, :].rearrange("p a b -> p (a b)").bitcast(F32)
        r_vfslots = c.r_ar[16:20]

        def emit_gmlp(t, ts):
            emit_norm(sc, c, t, vecs[:, 0, :])

            def evac_u(n, bank):
                sc.op("act", lambda e, n=n, bank=bank: e.activation(out=c.arena[:, n, :], in_=c.ps[:, bank, :], func=AF.Gelu_apprx_tanh),
                      reads=[c.r_ps[bank]], writes=[c.r_ar[n]])
            emit_proj_fm(sc, c, wr, w_in_b, r_w["in"], 0, D, evac_u)
            if STAGES["dbg"] and t == 0:
                sc.dma("pool", lambda e: e.dma_start(out=dbg_u[:, :, :], in_=c.arena[:, 0:8, :]), reads=c.r_ar[0:8])

            def evac_v(ch, b0):
                for hf in range(2):
                    sc.op("act", lambda e, hf=hf, b0=b0: e.activation(out=vf[:, hf * 512:(hf + 1) * 512], in_=c.ps[:, b0 + hf, :], func=AF.Gelu_apprx_tanh),
                          reads=[c.r_ps[b0 + hf]], writes=r_vfslots[2 * hf:2 * hf + 2])
                for hf in range(2):
                    sc.op("dve", lambda e, hf=hf: e.bn_stats(out=vstat[:, hf, :], in_=vf[:, hf * 512:(hf + 1) * 512]),
                          reads=r_vfslots[2 * hf:2 * hf + 2], writes=[r_vstat])
                sc.op("dve", lambda e: e.bn_aggr(out=vmv[:, 0:2], in_=vstat[:].rearrange("p a b -> p (a b)")), reads=[r_vstat], writes=[r_vmv])
                sc.op("dve", lambda e: e.tensor_scalar(out=vmv[:, 2:3], in0=vmv[:, 1:2], scalar1=EPS, scalar2=None, op0=ALU.add),
                      reads=[r_vmv], writes=[r_vmv])
                sc.op("pool", lambda e: e.tensor_tensor(out=vmv[:, 3:4], in0=vmv[:, 2:3], in1=nh1[:], op=ALU.pow),
                      reads=[r_vmv, rc], writes=[r_vmv])
                if STAGES["dbg"] and t == 0:
                    sc.dma("pool", lambda e, ch=ch: e.dma_start(out=dbg_vf[:, ch, :], in_=vf), reads=r_vfslots)
                    sc.dma("pool", lambda e, ch=ch: e.dma_start(out=dbg_mv[:, ch, :], in_=vmv[:]), reads=[r_vmv])
                vn = c.arena[:, 8 + 2 * ch:10 + 2 * ch, :].rearrange("p a b -> p (a b)")
                sc.op("dve", lambda e, vn=vn: e.tensor_scalar(out=vn, in0=vf, scalar1=vmv[:, 0:1], scalar2=vmv[:, 3:4], op0=ALU.subtract, op1=ALU.mult),
                      reads=[r_vmv] + r_vfslots, writes=c.r_ar[8 + 2 * ch:10 + 2 * ch])
            emit_proj_tm(sc, c, wr, w_in_b, r_w["in"], D, evac_v)

            if STAGES["dbg"] and t == 0:
                sc.dma("pool", lambda e: e.dma_start(out=dbg_vn[:, :, :], in_=c.arena[:, 8:16, :]), reads=c.r_ar[8:16])
            for g in range(8):
                bank = c.mm_rot % 4
                c.mm_rot += 1
                for ch in range(4):
                    vn = c.arena[:, 8 + 2 * ch:10 + 2 * ch, :].rearrange("p a b -> p (a b)")
                    sc.op("pe", lambda e, g=g, ch=ch, bank=bank, vn=vn: e.matmul(c.ps[:, bank, ch * 128:(ch + 1) * 128], lhsT=vn[:, g * 128:(g + 1) * 128], rhs=wcT[:, g, :],
                                                                                  start=True, stop=True),
                          reads=c.r_ar[8 + 2 * ch:10 + 2 * ch] + [r_c4], writes=[c.r_ps[bank]], inc=(ch == 3))
                sb = g % 2
                sc.op("dve", lambda e, g=g, bank=bank, sb=sb: e.scalar_tensor_tensor(
                    out=c.tmp[:, sb, :].rearrange("p (a b) -> p a b", a=4), in0=c.ps[:, bank, :].rearrange("p (a b) -> p a b", a=4),
                    scalar=vecs[:, 1, g:g + 1], in1=bsp[:, g, :].unsqueeze(1).broadcast_to([128, 4, 128]), op0=ALU.mult, op1=ALU.add),
                    reads=[c.r_ps[bank], rc, r_c3], writes=[c.r_tmp[sb]])
                if STAGES["dbg"] and t == 0:
                    sc.dma("pool", lambda e, g=g, sb=sb: e.dma_start(out=dbg_z[:, g, :], in_=c.tmp[:, sb, :]), reads=[c.r_tmp[sb]])
                sc.op("dve", lambda e, g=g, sb=sb: e.tensor_tensor(out=c.arena[:, g, :], in0=c.tmp[:, sb, :], in1=c.arena[:, g, :], op=ALU.mult),
                      reads=[c.r_tmp[sb], c.r_ar[g]], writes=[c.r_ar[g]])

            if STAGES["dbg"] and t == 0:
                sc.dma("pool", lambda e: e.dma_start(out=dbg_uz[:, :, :], in_=c.arena[:, 0:8, :]), reads=c.r_ar[0:8])

            def evac_res(n, bank):
                sc.op("dve", lambda e, n=n, bank=bank: e.tensor_tensor(out=c.xT[:, n, ts], in0=c.xT[:, n, ts], in1=c.ps[:, bank, :], op=ALU.add),
                      reads=[c.r_ps[bank], c.r_x[n][t]], writes=[c.r_x[n][t]])
            emit_proj_fm(sc, c, wr, w_out_b, r_w["out"], 0, D, evac_res, src=c.arena, r_src=c.r_ar)

        def emit_tail(t, ts):
            sc.dma("pool", lambda e, ts=ts: e.dma_start(out=x1Tv[:, :, ts], in_=c.xT[:, :, ts]), reads=[c.r_x[k][t] for k in range(8)])

            emit_norm(sc, c, t, vecs[:, 3, :])

            def evac_k(n, bank):
                sc.op("act", lambda e, n=n, bank=bank: e.copy(out=c.arena[:, n, :], in_=c.ps[:, bank, :]),
                      reads=[c.r_ps[bank]], writes=[c.r_ar[n]])
            emit_proj_fm(sc, c, wr, w_kv_b, r_w["kv"], 0, D, evac_k)
            sc.dma("pool", lambda e, ts=ts: e.dma_start(out=kTv[:, :, ts], in_=c.arena[:, 0:8, :]), reads=c.r_ar[0:8])

            def evac_vv(ch, b0):
                vst = c.arena[:, 8 + 2 * ch:10 + 2 * ch, :].rearrange("p a b -> p (a b)")
                sc.op("act", lambda e, vst=vst, b0=b0: e.copy(out=vst[:, 0:512], in_=c.ps[:, b0, :]),
                      reads=[c.r_ps[b0]], writes=[c.r_ar[8 + 2 * ch]])
                sc.op("dve", lambda e, vst=vst, b0=b0: e.tensor_copy(out=vst[:, 512:1024], in_=c.ps[:, b0 + 1, :]),
                      reads=[c.r_ps[b0 + 1]], writes=[c.r_ar[9 + 2 * ch]])
            emit_proj_tm(sc, c, wr, w_kv_b, r_w["kv"], D, evac_vv)
            sc.dma("pool", lambda e, t=t: e.dma_start(out=vov[:, 4 * t:4 * t + 4, :],
                                                      in_=c.arena[:, 8:16, :].rearrange("p (c a) b -> p c (a b)", a=2)),
                   reads=c.r_ar[8:16])

        for t in range(1 if STAGES["dbg"] else NT):
            ts = slice(t * T, (t + 1) * T)
            if STAGES["gmlp"]:
                emit_gmlp(t, ts)
            if STAGES["ffn"]:
                emit_norm(sc, c, t, vecs[:, 2, :])
                emit_ffn(sc, c, t, wr, w_gu_b, w_dn_b, r_w["gu"], r_w["dn"])
            emit_tail(t, ts)

        sc.finish("sp")
        sc.replay(nc)
    return nc, sc


def prep_A(inputs, core):
    b, role = core // 2, core % 2
    f = np.float32
    x = np.asarray(inputs["x"])
    toks = np.concatenate([np.arange(tile_index(role, j) * T, (tile_index(role, j) + 1) * T) for j in range(NT)])
    xT = np.ascontiguousarray(x[b][toks].T)
    col = lambda v: np.ascontiguousarray(np.asarray(v, f).reshape(8, 128).T)
    vecs = np.stack([col(inputs["a_norm"][0]), col(inputs["a_v_norm"][0]), col(inputs["ffn_norm"][0]), col(inputs["kv_norm"])], axis=1)
    w_spT = np.ascontiguousarray(np.transpose(np.asarray(inputs["a_w_sp"][0], f), (2, 0, 1)))
    trilT = np.triu(np.ones((128, 128), f))
    bsp = np.ascontiguousarray(np.broadcast_to(np.asarray(inputs["a_b_sp"][0], f)[None], (128, 8, 128)))
    return {
        "xT": xT, "vecs": np.ascontiguousarray(vecs), "w_in": np.asarray(inputs["a_w_in"][0], f),
        "w_spT": w_spT, "trilT": trilT, "bsp": bsp, "w_out": np.asarray(inputs["a_w_out"][0], f),
        "w_gu": np.asarray(inputs["ffn_w_gu"][0], f), "w_dn": np.asarray(inputs["ffn_w_down"][0], f),
        "w_kv": np.asarray(inputs["kv_w"], f),
    }


_CACHE = {}


def run_A(inputs):
    if "A" not in _CACHE:
        _CACHE["A"] = build_A()[0]
    nc = _CACHE["A"]
    in_maps = [prep_A(inputs, cidx) for cidx in range(8)]
    res = run_bass_kernel_spmd(nc, in_maps, core_ids=list(range(8)))
    return res.results


def build_B():
    import contextlib
    nc = bass.Bass("TRN2", target_bir_lowering=False)
    sc = Sched()
    c = Ctx()
    din = lambda name, shape, dt=F32: nc.dram_tensor(name, shape, dt, kind="ExternalInput").ap()
    x1T_d = din("x1T", [D, TOK])
    vecs_d = din("vecs", [128, 3, 8])
    w_q_d = din("w_q", [D, D])
    w_o_d = din("w_o", [D, D])
    w_gu_d = din("w_gu", [D, 2 * DFF])
    w_dn_d = din("w_dn", [DFF, D])
    kd_d = [din("kd_e", [NH, KROWS, 2, S], BF16), din("kd_o", [NH, KROWS, 2, S], BF16)]
    vd_d = [din("vd_e", [NH, 128, 64, 128], BF16), din("vd_o", [NH, 128, 64, 128], BF16)]
    qaug_d = din("qaug", [NT, 4, NH, T], BF16)
    masks_d = din("masks", [128, 4, T], BF16)
    ident_d = din("ident", [128, 128], BF16)
    lamb_d = din("lamb", [128, 256])
    subln_d = din("subln", [128, 1])
    outT_o = nc.dram_tensor("outT", [D, TOK], F32, kind="ExternalOutput").ap()
    if STAGES["dbgB"]:
        dbg_on = nc.dram_tensor("dbg_on", [128, 8, T], BF16, kind="ExternalOutput").ap()
        dbg_q = nc.dram_tensor("dbg_q", [128, 16, T], BF16, kind="ExternalOutput").ap()
        dbg_sm = nc.dram_tensor("dbg_sm", [128, 8], F32, kind="ExternalOutput").ap()
    dint = lambda name, shape: nc.dram_tensor(name, shape, BF16, kind="Internal").ap()
    w_q_b = dint("w_q_b", [D, D])
    w_o_b = dint("w_o_b", [D, D])
    w_gu_b = dint("w_gu_b", [D, 2 * DFF])
    w_dn_b = dint("w_dn_b", [DFF, D])

    with contextlib.ExitStack() as es:
        wr = alloc_common(nc, es, c, sc, nwslab=2)
        vecs = es.enter_context(nc.sbuf_tensor("s_vecs", [128, 3, 8], F32))
        kring = es.enter_context(nc.sbuf_tensor("s_kring", [KROWS, 2, 2, 1024], BF16))
        vring = es.enter_context(nc.sbuf_tensor("s_vring", [128, 2, 8, 128], BF16))
        masks = es.enter_context(nc.sbuf_tensor("s_masks", [128, 4, T], BF16))
        ident = es.enter_context(nc.sbuf_tensor("s_ident", [128, 128], BF16))
        lamb = es.enter_context(nc.sbuf_tensor("s_lamb", [128, 256], F32))
        sm = es.enter_context(nc.sbuf_tensor("s_sm", [128, 8], F32))
        r_kv = [Res(), Res()]
        r_qaug = Res()
        r_pt = c.r_ar[16:20]
        r_sm = Res()
        r_w = {n: Res() for n in ("q", "o", "gu", "dn")}
        rc = c.r_const
        r_c2, r_c3 = Res(), Res()
        Qt = c.arena[:, 0:16, :].rearrange("p (h i) t -> p h i t", i=2)
        onT, r_on = c.sq, c.r_sq

        sc.dma("sp", lambda e: e.dma_start(out=vecs[:], in_=vecs_d[:, :, :]), writes=[rc])
        sc.dma("sp", lambda e: e.dma_start(out=masks[:], in_=masks_d[:, :, :]), writes=[r_c2])
        r_c5 = Res()
        sc.dma("sp", lambda e: e.dma_start(out=ident[:], in_=ident_d[:, :]), writes=[r_c5])
        sc.dma("sp", lambda e: e.dma_start(out=lamb[:], in_=lamb_d[:, :]), writes=[r_c3])
        r_c4 = Res()
        sc.dma("sp", lambda e: e.dma_start(out=sm[:, 7:8], in_=subln_d[:, :]), writes=[r_c4])
        x1Tv = x1T_d.rearrange("(k p) t -> p k t", p=128)
        for t in range(NT):
            sc.dma("sp", lambda e, t=t: e.dma_start(out=c.xT[:, :, t * T:(t + 1) * T], in_=x1Tv[:, :, t * T:(t + 1) * T]),
                   writes=[c.r_x[k][t] for k in range(8)])
        cast_weight(sc, w_q_b, w_q_d, r_w["q"])
        cast_weight(sc, w_o_b, w_o_d, r_w["o"])
        cast_weight(sc, w_gu_b, w_gu_d, r_w["gu"])
        cast_weight(sc, w_dn_b, w_dn_d, r_w["dn"])
        scr = c.tmp[:, 0, 0:64]
        sc.op("dve", lambda e: e.scalar_tensor_tensor(out=scr, in0=lamb[:, 0:64], scalar=1.0, in1=lamb[:, 64:128], op0=ALU.mult, op1=ALU.mult, accum_out=sm[:, 0:1]),
              reads=[r_c3], writes=[r_sm, c.r_tmp[0]])
        sc.op("dve", lambda e: e.scalar_tensor_tensor(out=scr, in0=lamb[:, 128:192], scalar=1.0, in1=lamb[:, 192:256], op0=ALU.mult, op1=ALU.mult, accum_out=sm[:, 1:2]),
              reads=[r_c3, r_sm], writes=[r_sm, c.r_tmp[0]])
        sc.op("act", lambda e: e.activation(out=sm[:, 2:4], in_=sm[:, 0:2], func=AF.Exp), reads=[r_sm], writes=[r_sm])
        sc.op("dve", lambda e: e.tensor_tensor(out=sm[:, 4:5], in0=sm[:, 3:4], in1=sm[:, 2:3], op=ALU.subtract), reads=[r_sm], writes=[r_sm])
        sc.op("dve", lambda e: e.tensor_scalar(out=sm[:, 4:5], in0=sm[:, 4:5], scalar1=-LAMBDA_INIT, scalar2=None, op0=ALU.add), reads=[r_sm], writes=[r_sm])
        sc.op("dve", lambda e: e.tensor_scalar(out=sm[:, 5:6], in0=sm[:, 7:8], scalar1=1.0 - LAMBDA_INIT, scalar2=None, op0=ALU.mult), reads=[r_sm, r_c4], writes=[r_sm])

        outTv = outT_o.rearrange("(k p) t -> p k t", p=128)
        strot = [0]
        ptrot = [0]
        kvn = [0]

        def load_kv(var, h, cp):
            b = kvn[0] % 2
            kvn[0] += 1
            r = r_kv[b]
            sc.dma("sp", lambda e, b=b: e.dma_start(out=kring[:, b], in_=kd_d[var][h, :, :, cp * 1024:(cp + 1) * 1024]), writes=[r])
            sc.dma("sp", lambda e, b=b: e.dma_start(out=vring[:, b], in_=vd_d[var][h, :, cp * 8:(cp + 1) * 8, :]), writes=[r])
            return b, r

        def emit_attention(j, ts):
            var = j % 2
            chunks = [(h, ci) for h in range(NH) for ci in range(j + 1)]
            loaded = {}

            def ensure(idx):
                if idx < len(chunks) and idx not in loaded:
                    h, ci = chunks[idx]
                    loaded[idx] = load_kv(var, h, j - ci)
            steps = []
            for idx, (h, ci) in enumerate(chunks):
                for o in range(8):
                    for i in range(2):
                        steps.append((idx, h, ci, o, i))
            nsteps_h = (j + 1) * 16
            pend = []
            LAG = 2

            def emit_av(item):
                (idx, h, ci, o, i, pt, first, last) = item
                b, rkv = loaded[idx]
                sc.op("pe", lambda e, b=b, o=o, i=i, pt=pt, first=first, last=last: e.matmul(c.ps[:, 4 + i, :], lhsT=vring[:, b, o, :], rhs=c.arena[:, 16 + pt, :], start=first, stop=last),
                      reads=[rkv, r_pt[pt]], writes=[c.r_ps[4 + i]], inc=False)
                sc.op("pe", lambda e, i=i, pt=pt, first=first, last=last: e.matmul(c.ps[:, 6 + i, :], lhsT=c.ones[:], rhs=c.arena[:, 16 + pt, :], start=first, stop=last),
                      reads=[r_pt[pt], rc], writes=[c.r_ps[6 + i]], inc=True)
                if last and i == 1:
                    emit_head_post(h)

            def emit_head_post(h):
                a_, b_ = c.tmp[:, 0, :], c.tmp[:, 1, :]
                sc.op("dve", lambda e: e.reciprocal(out=c.ms[:], in_=c.ps[:, 6, :]), reads=[c.r_ps[6]], writes=[c.r_ms])
                sc.op("dve", lambda e: e.tensor_tensor(out=a_, in0=c.ps[:, 4, :], in1=c.ms[:], op=ALU.mult), reads=[c.r_ps[4], c.r_ms], writes=[c.r_tmp[0]])
                sc.op("dve", lambda e: e.reciprocal(out=c.rstd[:], in_=c.ps[:, 7, :]), reads=[c.r_ps[7]], writes=[c.r_rstd])
                sc.op("dve", lambda e: e.tensor_tensor(out=b_, in0=c.ps[:, 5, :], in1=c.rstd[:], op=ALU.mult), reads=[c.r_ps[5], c.r_rstd], writes=[c.r_tmp[1]])
                sc.op("dve", lambda e: e.scalar_tensor_tensor(out=a_, in0=b_, scalar=sm[:, 4:5], in1=a_, op0=ALU.mult, op1=ALU.add),
                      reads=[c.r_tmp[1], c.r_tmp[0], r_sm], writes=[c.r_tmp[0]])
                sc.op("dve", lambda e: e.tensor_tensor(out=c.arena[:, 20, :], in0=a_, in1=a_, op=ALU.mult), reads=[c.r_tmp[0]], writes=[c.r_ar[20]])
                bank = strot[0] % 4
                strot[0] += 1
                sc.op("pe", lambda e, bank=bank: e.matmul(c.ps[:, bank, :], lhsT=c.ones[:], rhs=c.arena[:, 20, :], start=True, stop=True),
                      reads=[c.r_ar[20], rc], writes=[c.r_ps[bank]])
                sc.op("act", lambda e, bank=bank: e.activation(out=c.ms[:], in_=c.ps[:, bank, :], func=AF.Sqrt, scale=1.0 / 128, bias=EPS),
                      reads=[c.r_ps[bank]], writes=[c.r_ms])
                sc.op("dve", lambda e: e.reciprocal(out=c.rstd[:], in_=c.ms[:]),
                      reads=[c.r_ms], writes=[c.r_rstd])
                sc.op("dve", lambda e, h=h: e.scalar_tensor_tensor(out=onT[:, h, :], in0=a_, scalar=sm[:, 5:6], in1=c.rstd[:], op0=ALU.mult, op1=ALU.mult),
                      reads=[c.r_tmp[0], c.r_rstd, r_sm], writes=[r_on[h]])

            for sidx, (idx, h, ci, o, i) in enumerate(steps):
                if o == 0 and i == 0:
                    ensure(idx)
                if o == 1 and i == 0:
                    ensure(idx + 1)
                b, rkv = loaded[idx]
                bank = strot[0] % 4
                strot[0] += 1
                pt = ptrot[0] % 4
                ptrot[0] += 1
                diag = (ci == 0 and o >= 4)
                sc.op("pe", lambda e, b=b, h=h, o=o, i=i, bank=bank, diag=diag: e.matmul(c.ps[:, bank, :], lhsT=kring[0:68, b, i, o * 128:(o + 1) * 128], rhs=Qt[0:68, h, i, :], start=True, stop=(not diag)),
                      reads=[rkv, c.r_ar[2 * h + i], r_qaug], writes=[c.r_ps[bank]], inc=(not diag))
                if diag:
                    dd = o - 4
                    sc.op("pe", lambda e, bank=bank, dd=dd: e.matmul(c.ps[:, bank, :], lhsT=ident[:], rhs=masks[:, dd, :], start=False, stop=True),
                          reads=[r_c2, r_c5], writes=[c.r_ps[bank]])
                sc.op("act", lambda e, bank=bank, pt=pt: e.activation(out=c.arena[:, 16 + pt, :], in_=c.ps[:, bank, :], func=AF.Exp, scale=0.125),
                      reads=[c.r_ps[bank]], writes=[r_pt[pt]])
                hs = sidx - h * nsteps_h
                first = hs < 2
                last = hs >= nsteps_h - 2
                pend.append((idx, h, ci, o, i, pt, first, last))
                if len(pend) > LAG:
                    emit_av(pend.pop(0))
            while pend:
                emit_av(pend.pop(0))

        def emit_tile(t):
            ts = slice(t * T, (t + 1) * T)
            emit_norm(sc, c, t, vecs[:, 0, :])
            for i in range(2):
                sc.dma("sp", lambda e, t=t, i=i: e.dma_start(out=Qt[64:68, :, i, :], in_=qaug_d[t, :, :, :]), writes=[r_qaug] + c.r_ar[0:16])

            def evac_q(h, bank):
                sc.op("act", lambda e, h=h, bank=bank: e.copy(out=Qt[0:64, h, 0, :], in_=c.ps[0:64, bank, :]), reads=[c.r_ps[bank]], writes=[c.r_ar[2 * h]])
                sc.op("act", lambda e, h=h, bank=bank: e.copy(out=Qt[0:64, h, 1, :], in_=c.ps[64:128, bank, :]), reads=[c.r_ps[bank]], writes=[c.r_ar[2 * h + 1]])
            emit_proj_fm(sc, c, wr, w_q_b, r_w["q"], 0, D, evac_q)
            if STAGES["dbgB"] and t == 0:
                sc.dma("pool", lambda e: e.dma_start(out=dbg_q[:, :, :], in_=c.arena[:, 0:16, :]), reads=c.r_ar[0:16] + [r_qaug])
                sc.dma("pool", lambda e: e.dma_start(out=dbg_sm[:, :], in_=sm[:]), reads=[r_sm])
            if STAGES["attn"]:
                emit_attention(t, ts)
            if STAGES["dbgB"] and t == 0:
                sc.dma("pool", lambda e: e.dma_start(out=dbg_on[:, :, :], in_=onT[:]), reads=r_on)

            def evac_res(n, bank):
                sc.op("dve", lambda e, n=n, bank=bank: e.tensor_tensor(out=c.xT[:, n, ts], in0=c.xT[:, n, ts], in1=c.ps[:, bank, :], op=ALU.add),
                      reads=[c.r_ps[bank], c.r_x[n][t]], writes=[c.r_x[n][t]])
            if STAGES["attn"]:
                emit_proj_fm(sc, c, wr, w_o_b, r_w["o"], 0, D, evac_res, src=onT, r_src=r_on)
            if STAGES["ffnB"]:
                emit_norm(sc, c, t, vecs[:, 1, :])
                emit_ffn(sc, c, t, wr, w_gu_b, w_dn_b, r_w["gu"], r_w["dn"])
            for k in range(8):
                sc.op("dve", lambda e, k=k: e.tensor_tensor(out=c.sq[:, k, :], in0=c.xT[:, k, ts], in1=c.xT[:, k, ts], op=ALU.mult),
                      reads=[c.r_x[k][t]], writes=[c.r_sq[k]])
            bank = c.stat_bank
            for k in range(8):
                sc.op("pe", lambda e, k=k: e.matmul(c.ps[:, bank, :], lhsT=c.ones[:], rhs=c.sq[:, k, :], start=(k == 0), stop=(k == 7)),
                      reads=[c.r_sq[k], rc], writes=[c.r_ps[bank]], inc=(k == 7))
            sc.op("act", lambda e: e.activation(out=c.ms[:], in_=c.ps[:, bank, :], func=AF.Sqrt, scale=1.0 / D, bias=EPS),
                  reads=[c.r_ps[bank]], writes=[c.r_ms])
            sc.op("dve", lambda e: e.reciprocal(out=c.rstd[:], in_=c.ms[:]),
                  reads=[c.r_ms], writes=[c.r_rstd])
            for k in range(8):
                sb = k % 2
                sc.op("dve", lambda e, k=k, sb=sb: e.scalar_tensor_tensor(out=c.tmp[:, sb, :], in0=c.xT[:, k, ts], scalar=vecs[:, 2, k:k + 1], in1=c.rstd[:],
                                                                       op0=ALU.mult, op1=ALU.mult),
                      reads=[c.r_x[k][t], c.r_rstd, rc], writes=[c.r_tmp[sb]])
                sc.dma("pool", lambda e, k=k, sb=sb, ts=ts: e.dma_start(out=outTv[:, k, ts], in_=c.tmp[:, sb, :]), reads=[c.r_tmp[sb]])

        for t in range(1 if STAGES["dbgB"] else NT):
            emit_tile(t)

        sc.finish("sp")
        sc.replay(nc)
    return nc, sc


def alibi_slopes():
    return np.array([2.0 ** (-8.0 * (i + 1) / NH) for i in range(NH)], dtype=np.float64)


def prep_B(inputs, core, x1T, KT_full, V_full):
    b, role = core // 2, core % 2
    f = np.float32
    col = lambda v: np.ascontiguousarray(np.asarray(v, f).reshape(8, 128).T)
    vecs = np.stack([col(inputs["b_norm"][0]), col(inputs["ffn_norm"][1]), col(inputs["final_norm"])], axis=1)
    slopes = alibi_slopes()
    pos = np.arange(S)
    khi, klo = pos // 128, pos % 128
    out = {}
    K4 = KT_full.reshape(NH, 2, 64, S)
    V4 = V_full.reshape(64, 128, NH, 128)
    for par, name in ((0, "e"), (1, "o")):
        shifted = (tile_index(role, par) != 2 * par + 1)
        kd = np.zeros((NH, KROWS, 2, S), NPBF)
        vd = np.zeros((NH, 128, 64, 128), NPBF)
        aug = np.zeros((NH, 4, S), np.float64)
        aug[:, 0, :] = 1.0
        aug[:, 1, :] = 1.0
        if not shifted:
            kd[:, 0:64, :, :] = K4.transpose(0, 2, 1, 3)
            vd[:] = V4.transpose(2, 1, 0, 3)
            aug[:, 2, :] = slopes[:, None] * 128.0 * khi[None, :]
            aug[:, 3, :] = slopes[:, None] * klo[None, :]
        else:
            kd[:, 0:64, :, 512:] = K4.transpose(0, 2, 1, 3)[:, :, :, :S - 512]
            vd[:, :, 4:, :] = V4.transpose(2, 1, 0, 3)[:, :, :60, :]
            aug[:, 2, 512:] = slopes[:, None] * 128.0 * khi[None, :S - 512]
            aug[:, 3, 512:] = slopes[:, None] * klo[None, :S - 512]
            aug[:, 2, :512] = slopes[:, None] * 128.0 * (-200.0)
        kd[:, 64:68, 0, :] = aug.astype(NPBF)
        kd[:, 64:68, 1, :] = aug.astype(NPBF)
        out["kd_" + name] = kd
        out["vd_" + name] = vd
    qaug = np.zeros((NT, 4, NH, T), np.float64)
    for j in range(NT):
        qpos = tile_index(role, j) * T + np.arange(T)
        qhi, qlo = qpos // 128, qpos % 128
        qaug[j, 0] = -8.0 * slopes[:, None] * 128.0 * qhi[None, :]
        qaug[j, 1] = -8.0 * slopes[:, None] * qlo[None, :]
        qaug[j, 2] = 8.0
        qaug[j, 3] = 8.0
    kk = np.arange(128)[:, None, None]
    dd = np.arange(4)[None, :, None]
    qq = np.arange(T)[None, None, :]
    masks = np.where(qq - 128 * dd - kk >= 0, 0.0, -240000.0).astype(NPBF)
    out.update({
        "x1T": x1T, "vecs": np.ascontiguousarray(vecs), "w_q": np.asarray(inputs["b_w_q"][0], f), "w_o": np.asarray(inputs["b_w_o"][0], f),
        "w_gu": np.asarray(inputs["ffn_w_gu"][1], f), "w_dn": np.asarray(inputs["ffn_w_down"][1], f),
        "qaug": qaug.astype(NPBF), "masks": masks, "ident": np.eye(128, dtype=np.float32).astype(NPBF),
        "lamb": np.ascontiguousarray(np.broadcast_to(np.asarray(inputs["b_lambda"][0], f).reshape(1, 256), (128, 256))),
        "subln": np.ascontiguousarray(np.asarray(inputs["b_subln"][0], f).reshape(128, 1)),
    })
    return out


def run_B(inputs, resA):
    if "B" not in _CACHE:
        _CACHE["B"] = build_B()[0]
    nc = _CACHE["B"]
    in_maps = []
    for b in range(B):
        KT_full = np.zeros((D, S), NPBF)
        V_full = np.zeros((S, D), NPBF)
        for role in range(2):
            r = resA[2 * b + role]
            for j in range(NT):
                i = tile_index(role, j)
                KT_full[:, i * T:(i + 1) * T] = r["kT"][:, j * T:(j + 1) * T]
                V_full[i * T:(i + 1) * T, :] = r["v"][j * T:(j + 1) * T, :]
        for role in range(2):
            in_maps.append(prep_B(inputs, 2 * b + role, np.asarray(resA[2 * b + role]["x1T"]), KT_full, V_full))
    res = run_bass_kernel_spmd(nc, in_maps, core_ids=list(range(8)))
    return res.results


def kernel_unfused(**inputs):
    resA = run_A(inputs)
    resB = run_B(inputs, resA)
    out = np.zeros((B, S, D), np.float32)
    for core in range(8):
        b, role = core // 2, core % 2
        oT = np.asarray(resB[core]["outT"])
        for j in range(NT):
            i = tile_index(role, j)
            out[b, i * T:(i + 1) * T, :] = oT[:, j * T:(j + 1) * T].T
    return out


def sched_coll(sc, fn, reads=(), writes=()):
    toks = sc._deps(reads, writes)
    idx = sc.dn.setdefault("coll", 0) % 4
    sc.dn["coll"] += 1
    key = ("c", "pool", idx)
    cnt = sc.dcnt.get(key, 0)
    if cnt > 0:
        toks.append((key, cnt))
    sc._wait("pool", toks)
    sc.dcnt[key] = cnt + 1
    tok = (key, cnt + 1)
    sc.q["pool"].append(("op", fn, tok))
    sc._mark(tok, reads, writes)
    sc.n_inst += 1
    return tok


def build_F():
    import contextlib
    nc = bass.Bass("TRN2", target_bir_lowering=False)
    sc = Sched()
    c = Ctx()
    din = lambda name, shape, dt=F32: nc.dram_tensor(name, shape, dt, kind="ExternalInput").ap()
    xT_d = din("xT", [D, TOK])
    vecs_d = din("vecs", [128, 7, 8])
    w_in_d = din("w_in", [D, 2 * D])
    w_spT_d = din("w_spT", [128, 8, 128])
    tril_d = din("trilT", [128, 128])
    bsp_d = din("bsp", [128, 8, 128])
    w_out_d = din("w_out", [D, D])
    w_gu0_d = din("w_gu0", [D, 2 * DFF])
    w_dn0_d = din("w_dn0", [DFF, D])
    w_kv_d = din("w_kv", [D, 2 * D])
    w_q_d = din("w_q", [D, D])
    w_o_d = din("w_o", [D, D])
    w_gu1_d = din("w_gu1", [D, 2 * DFF])
    w_dn1_d = din("w_dn1", [DFF, D])
    kaug_d = din("kaug", [2, NH, 5, 2, S], BF16)
    qaug_d = din("qaug", [NT, 5, NH, T], BF16)
    wmask_d = din("wmask", [128, 1408], BF16)
    sel_d = din("sel", [128, 4, 128], BF16)
    lamb_d = din("lamb", [128, 256])
    subln_d = din("subln", [128, 1])
    outT_o = nc.dram_tensor("outT", [D, TOK], F32, kind="ExternalOutput").ap()
    dint = lambda name, shape: nc.dram_tensor(name, shape, BF16, kind="Internal").ap()
    wb = {}
    for name, src in (("in", w_in_d), ("out", w_out_d), ("gu0", w_gu0_d), ("dn0", w_dn0_d), ("kv", w_kv_d),
                      ("q", w_q_d), ("o", w_o_d), ("gu1", w_gu1_d), ("dn1", w_dn1_d)):
        wb[name] = (dint("wb_" + name, list(src.shape)), src)
    snd = [nc.dram_tensor("snd%d" % t, [2048, T], BF16) for t in range(NT)]
    gat = [nc.dram_tensor("gat%d" % t, [4096, T], BF16) for t in range(NT)]

    with contextlib.ExitStack() as es:
        wr = alloc_common(nc, es, c, sc, nwslab=2)
        vecs = es.enter_context(nc.sbuf_tensor("s_vecs", [128, 7, 8], F32))
        wmask = es.enter_context(nc.sbuf_tensor("s_wmask", [128, 1408], BF16))
        sel = es.enter_context(nc.sbuf_tensor("s_sel", [128, 4, 128], BF16))
        sm = es.enter_context(nc.sbuf_tensor("s_sm", [128, 8], F32))
        rc = c.r_const
        r_w = {n: Res() for n in wb}
        r_c2, r_c3, r_c4, r_c5, r_c6, r_c7 = [Res() for _ in range(6)]
        r_sm = Res()
        r_snd = [Res() for _ in range(NT)]
        r_gat = [Res() for _ in range(NT)]

        sc.dma("sp", lambda e: e.dma_start(out=vecs[:], in_=vecs_d[:, :, :]), writes=[rc])
        sc.dma("sp", lambda e: e.dma_start(out=wmask[:], in_=wmask_d[:, :]), writes=[r_c5])
        sc.dma("sp", lambda e: e.dma_start(out=sel[:], in_=sel_d[:, :, :]), writes=[r_c6])
        sc.dma("sp", lambda e: e.dma_start(out=sm[:, 7:8], in_=subln_d[:, :]), writes=[r_c7])
        lamb = c.tmp[:, 0, 0:256]
        scr = c.tmp[:, 1, 0:64]
        sc.dma("sp", lambda e: e.dma_start(out=lamb, in_=lamb_d[:, :]), writes=[c.r_tmp[0]])
        sc.op("dve", lambda e: e.scalar_tensor_tensor(out=scr, in0=lamb[:, 0:64], scalar=1.0, in1=lamb[:, 64:128], op0=ALU.mult, op1=ALU.mult, accum_out=sm[:, 0:1]),
              reads=[c.r_tmp[0]], writes=[r_sm, c.r_tmp[1]])
        sc.op("dve", lambda e: e.scalar_tensor_tensor(out=scr, in0=lamb[:, 128:192], scalar=1.0, in1=lamb[:, 192:256], op0=ALU.mult, op1=ALU.mult, accum_out=sm[:, 1:2]),
              reads=[c.r_tmp[0], r_sm], writes=[r_sm, c.r_tmp[1]])
        sc.op("act", lambda e: e.activation(out=sm[:, 2:4], in_=sm[:, 0:2], func=AF.Exp), reads=[r_sm], writes=[r_sm])
        sc.op("dve", lambda e: e.tensor_tensor(out=sm[:, 4:5], in0=sm[:, 3:4], in1=sm[:, 2:3], op=ALU.subtract), reads=[r_sm], writes=[r_sm])
        sc.op("dve", lambda e: e.tensor_scalar(out=sm[:, 4:5], in0=sm[:, 4:5], scalar1=-LAMBDA_INIT, scalar2=None, op0=ALU.add), reads=[r_sm], writes=[r_sm])
        sc.op("dve", lambda e: e.tensor_scalar(out=sm[:, 5:6], in0=sm[:, 7:8], scalar1=1.0 - LAMBDA_INIT, scalar2=None, op0=ALU.mult), reads=[r_sm, r_c7], writes=[r_sm])

        esA = es.enter_context(contextlib.ExitStack())
        tril = esA.enter_context(nc.sbuf_tensor("s_tril", [128, 128], F32))
        wcT = esA.enter_context(nc.sbuf_tensor("s_wcT", [128, 8, 128], BF16))
        bsp = esA.enter_context(nc.sbuf_tensor("s_bsp", [128, 8, 128], F32))
        vstat = esA.enter_context(nc.sbuf_tensor("s_vstat", [128, 2, 6], F32))
        vmv = esA.enter_context(nc.sbuf_tensor("s_vmv", [128, 4], F32))
        nh1 = c.nh1
        r_vstat, r_vmv = Res(), Res()
        wspT_f = c.tmp[:].rearrange("p a (b c) -> p (a b) c", c=128)
        sc.dma("sp", lambda e: e.dma_start(out=wspT_f, in_=w_spT_d[:, :, :]), writes=c.r_tmp)
        sc.dma("sp", lambda e: e.dma_start(out=tril[:], in_=tril_d[:, :]), writes=[r_c2])
        sc.dma("sp", lambda e: e.dma_start(out=bsp[:], in_=bsp_d[:, :, :]), writes=[r_c3])
        for g in range(8):
            sc.op("dve", lambda e, g=g: e.tensor_tensor(out=wcT[:, g, :], in0=wspT_f[:, g, :], in1=tril[:], op=ALU.mult),
                  reads=[r_c2] + c.r_tmp, writes=[r_c4])
        xTv = xT_d.rearrange("(k p) t -> p k t", p=128)
        def load_x(t):
            sc.dma("sp", lambda e, t=t: e.dma_start(out=c.xT[:, :, t * T:(t + 1) * T], in_=xTv[:, :, t * T:(t + 1) * T]),
                   writes=[c.r_x[k][t] for k in range(8)])
        load_x(0)
        for name in ("in", "out", "gu0", "dn0", "kv"):
            cast_weight(sc, wb[name][0], wb[name][1], r_w[name])

        vf = c.arena[:, 16:20, :].rearrange("p a b -> p (a b)").bitcast(F32)
        r_vfslots = c.r_ar[16:20]
        groups = [[0, 1], [2, 3], [4, 5], [6, 7]]
        deferred = []

        def emit_gmlp(t, ts):
            emit_norm(sc, c, t, vecs[:, 0, :])
            while deferred:
                deferred.pop(0)()

            def evac_v(ch, b0):
                for hf in range(2):
                    sc.op("act", lambda e, hf=hf, b0=b0: e.activation(out=vf[:, hf * 512:(hf + 1) * 512], in_=c.ps[:, b0 + hf, :], func=AF.Gelu_apprx_tanh),
                          reads=[c.r_ps[b0 + hf]], writes=r_vfslots[2 * hf:2 * hf + 2])
                for hf in range(2):
                    sc.op("dve", lambda e, hf=hf: e.bn_stats(out=vstat[:, hf, :], in_=vf[:, hf * 512:(hf + 1) * 512]),
                          reads=r_vfslots[2 * hf:2 * hf + 2], writes=[r_vstat])
                sc.op("dve", lambda e: e.bn_aggr(out=vmv[:, 0:2], in_=vstat[:].rearrange("p a b -> p (a b)")), reads=[r_vstat], writes=[r_vmv])
                sc.op("dve", lambda e: e.tensor_scalar(out=vmv[:, 2:3], in0=vmv[:, 1:2], scalar1=EPS, scalar2=None, op0=ALU.add),
                      reads=[r_vmv], writes=[r_vmv])
                sc.op("pool", lambda e: e.tensor_tensor(out=vmv[:, 3:4], in0=vmv[:, 2:3], in1=nh1[:], op=ALU.pow),
                      reads=[r_vmv, rc], writes=[r_vmv])
                vn = c.arena[:, 8 + 2 * ch:10 + 2 * ch, :].rearrange("p a b -> p (a b)")
                sc.op("dve", lambda e, vn=vn: e.tensor_scalar(out=vn, in0=vf, scalar1=vmv[:, 0:1], scalar2=vmv[:, 3:4], op0=ALU.subtract, op1=ALU.mult),
                      reads=[r_vmv] + r_vfslots, writes=c.r_ar[8 + 2 * ch:10 + 2 * ch])
            emit_proj_tm(sc, c, wr, wb["in"][0], r_w["in"], D, evac_v)

            def evac_u(n, bank):
                sc.op("act", lambda e, n=n, bank=bank: e.activation(out=c.arena[:, n, :], in_=c.ps[:, bank, :], func=AF.Gelu_apprx_tanh),
                      reads=[c.r_ps[bank]], writes=[c.r_ar[n]])
                g = n
                zb = 4 + (g % 4)
                for ch in range(4):
                    vn = c.arena[:, 8 + 2 * ch:10 + 2 * ch, :].rearrange("p a b -> p (a b)")
                    sc.op("pe", lambda e, g=g, ch=ch, zb=zb, vn=vn: e.matmul(c.ps[:, zb, ch * 128:(ch + 1) * 128], lhsT=vn[:, g * 128:(g + 1) * 128], rhs=wcT[:, g, :],
                                                                              start=True, stop=True),
                          reads=c.r_ar[8 + 2 * ch:10 + 2 * ch] + [r_c4], writes=[c.r_ps[zb]], inc=(ch == 3))
                sb = g % 2
                sc.op("dve", lambda e, g=g, zb=zb, sb=sb: e.scalar_tensor_tensor(
                    out=c.tmp[:, sb, :].rearrange("p (a b) -> p a b", a=4), in0=c.ps[:, zb, :].rearrange("p (a b) -> p a b", a=4),
                    scalar=vecs[:, 1, g:g + 1], in1=bsp[:, g, :].unsqueeze(1).broadcast_to([128, 4, 128]), op0=ALU.mult, op1=ALU.add),
                    reads=[c.r_ps[zb], rc, r_c3], writes=[c.r_tmp[sb]])
                sc.op("dve", lambda e, g=g, sb=sb: e.tensor_tensor(out=c.arena[:, g, :], in0=c.tmp[:, sb, :], in1=c.arena[:, g, :], op=ALU.mult),
                      reads=[c.r_tmp[sb], c.r_ar[g]], writes=[c.r_ar[g]])
            emit_proj_fm(sc, c, wr, wb["in"][0], r_w["in"], 0, D, evac_u)

            def evac_res(n, bank):
                sc.op("dve", lambda e, n=n, bank=bank: e.tensor_tensor(out=c.xT[:, n, ts], in0=c.xT[:, n, ts], in1=c.ps[:, bank, :], op=ALU.add),
                      reads=[c.r_ps[bank], c.r_x[n][t]], writes=[c.r_x[n][t]])
            def evac_res_sq(n, bank):
                evac_res(n, bank)
                emit_square(sc, c, t, n)
            emit_proj_fm(sc, c, wr, wb["out"][0], r_w["out"], 0, D, evac_res_sq, src=c.arena, r_src=c.r_ar)

        def emit_kv(t, ts):
            emit_norm(sc, c, t, vecs[:, 3, :], squares_done=True)
            sndk = snd[t][0:1024, :].rearrange("(k p) t -> p k t", p=128)
            sndv = snd[t][1024:2048, :].rearrange("(h p) (c d) -> p c h d", p=128, d=128)

            def evac_k(n, bank):
                sc.op("act", lambda e, n=n, bank=bank: e.copy(out=c.arena[:, n, :], in_=c.ps[:, bank, :]),
                      reads=[c.r_ps[bank]], writes=[c.r_ar[n]])
            emit_proj_fm(sc, c, wr, wb["kv"][0], r_w["kv"], 0, D, evac_k, kouter=True)
            sc.dma("act", lambda e: e.dma_start(out=sndk, in_=c.arena[:, 0:8, :]), reads=c.r_ar[0:8], writes=[r_snd[t]])

            def evac_vv(ch, b0):
                vst = c.arena[:, 8 + 2 * ch:10 + 2 * ch, :].rearrange("p a b -> p (a b)")
                sc.op("act", lambda e, vst=vst, b0=b0: e.copy(out=vst[:, 0:512], in_=c.ps[:, b0, :]),
                      reads=[c.r_ps[b0]], writes=[c.r_ar[8 + 2 * ch]])
                sc.op("dve", lambda e, vst=vst, b0=b0: e.tensor_copy(out=vst[:, 512:1024], in_=c.ps[:, b0 + 1, :]),
                      reads=[c.r_ps[b0 + 1]], writes=[c.r_ar[9 + 2 * ch]])
            emit_proj_tm(sc, c, wr, wb["kv"][0], r_w["kv"], D, evac_vv)
            for ch in range(4):
                sc.dma("act", lambda e, ch=ch: e.dma_start(out=sndv[:, ch], in_=c.arena[:, 8 + 2 * ch:10 + 2 * ch, :].rearrange("p a (h d) -> p (a h) d", d=128)),
                       reads=c.r_ar[8 + 2 * ch:10 + 2 * ch], writes=[r_snd[t]])

            def do_gather(t=t):
                sched_coll(sc, lambda e, t=t: e.collective_compute("AllGather", ALU.bypass, replica_groups=groups,
                                                                   ins=[snd[t].ap().opt()], outs=[gat[t].ap().opt()]),
                           reads=[r_snd[t]], writes=[r_gat[t]])
            deferred.append(do_gather)

        for t in range(NT):
            ts = slice(t * T, (t + 1) * T)
            emit_gmlp(t, ts)
            if t == NT - 1:
                snap_a = {e_: sc.cnt[e_] for e_ in COMPUTE if sc.cnt[e_] > 0}
                snap_a.update({k_: (16 * v_ if k_[0] == "d" else v_) for k_, v_ in sc.dcnt.items()})
            if t + 1 < NT:
                load_x(t + 1)
            if t == 1:
                for name in ("q", "o", "gu1", "dn1"):
                    cast_weight(sc, wb[name][0], wb[name][1], r_w[name])
            emit_norm(sc, c, t, vecs[:, 2, :], squares_done=True)
            emit_ffn(sc, c, t, wr, wb["gu0"][0], wb["dn0"][0], r_w["gu0"], r_w["dn0"], post_add=lambda nn, t=t: emit_square(sc, c, t, nn))
            emit_kv(t, ts)
        while deferred:
            deferred.pop(0)()

        esA.close()
        NKV = 4
        kring = es.enter_context(nc.sbuf_tensor("s_kring", [69, NKV, 2, 512], BF16))
        vring = es.enter_context(nc.sbuf_tensor("s_vring", [128, NKV, 4, 128], BF16))
        snapshot = snap_a
        r_kv = [Res() for _ in range(NKV)]
        for r_ in r_kv:
            r_.r = dict(snapshot)
        r_qaug = Res()
        r_pt = c.r_ar[16:20]
        Qt = c.arena[:, 0:16, :].rearrange("p (h i) t -> p h i t", i=2)
        onT, r_on = c.sq, c.r_sq
        outTv = outT_o.rearrange("(k p) t -> p k t", p=128)
        strot = [0]
        ptrot = [0]
        kvn = [0]

        def load_kv(h, cp, hf, dg):
            b = kvn[0] % NKV
            kvn[0] += 1
            r = r_kv[b]
            ranks = (0, 1) if cp % 2 == 0 else (1, 0)
            rk = ranks[hf]
            g_ = gat[cp]
            ksrc = g_[rk * 2048 + h * 128:rk * 2048 + (h + 1) * 128, :].rearrange("(i d) t -> d i t", d=64)
            sc.dma("sp", lambda e, b=b, ksrc=ksrc: e.dma_start(out=kring[0:64, b, :, :], in_=ksrc), reads=[r_gat[cp]], writes=[r])
            vsrc = g_[rk * 2048 + 1024 + h * 128:rk * 2048 + 1024 + (h + 1) * 128, :].rearrange("p (c d) -> p c d", d=128)
            sc.dma("sp", lambda e, b=b, vsrc=vsrc: e.dma_start(out=vring[:, b, :, :], in_=vsrc), reads=[r_gat[cp]], writes=[r])
            sc.dma("sp", lambda e, b=b: e.dma_start(out=kring[64:69, b, :, :], in_=kaug_d[dg, h, :, :, cp * 1024 + hf * 512:cp * 1024 + (hf + 1) * 512]), writes=[r])
            return b, r

        def emit_attention(j):
            par = j % 2
            chunks = [(h, ci, hf) for h in range(NH) for ci in range(j + 1) for hf in range(2)]
            loaded = {}

            def ensure(idx):
                if idx < len(chunks) and idx not in loaded:
                    h, ci, hf = chunks[idx]
                    loaded[idx] = load_kv(h, j - ci, hf, 1 if ci == 0 else 0)
            steps = []
            for idx, (h, ci, hf) in enumerate(chunks):
                for o4 in range(4):
                    for i in range(2):
                        steps.append((idx, h, ci, hf * 4 + o4, i))
            nsteps_h = (j + 1) * 16
            pend = []
            LAG = 2

            def emit_av(item):
                (idx, h, ci, o, i, pt, first, last) = item
                b, rkv = loaded[idx]
                sc.op("pe", lambda e, b=b, o=o, i=i, pt=pt, first=first, last=last: e.matmul(c.ps[:, 4 + i, :], lhsT=vring[:, b, o % 4, :], rhs=c.arena[:, 16 + pt, :], start=first, stop=last),
                      reads=[rkv, r_pt[pt]], writes=[c.r_ps[4 + i]], inc=True)
                if i == 0:
                    sc.op("pe", lambda e, i=i, pt=pt, first=first, last=last: e.matmul(c.ps[:, 6 + i, :], lhsT=c.ones[:], rhs=c.arena[:, 16 + pt, :], start=first, stop=last),
                          reads=[r_pt[pt], rc], writes=[c.r_ps[6 + i]], inc=True)
                elif first:
                    sc.op("dve", lambda e, i=i, pt=pt: e.tensor_copy(out=c.ps[:, 6 + i, :], in_=c.arena[:, 16 + pt, :]),
                          reads=[r_pt[pt]], writes=[c.r_ps[6 + i]])
                else:
                    sc.op("dve", lambda e, i=i, pt=pt: e.tensor_tensor(out=c.ps[:, 6 + i, :], in0=c.ps[:, 6 + i, :], in1=c.arena[:, 16 + pt, :], op=ALU.add),
                          reads=[r_pt[pt], c.r_ps[6 + i]], writes=[c.r_ps[6 + i]])
                if last and i == 1:
                    emit_head_post(h)

            def emit_head_post(h):
                a_, b_ = c.tmp[:, 0, :], c.tmp[:, 1, :]
                sc.op("act", lambda e: e.copy(out=c.arena[:, 21, :], in_=c.ps[:, 7, :]), reads=[c.r_ps[7]], writes=[c.r_ar[21]])
                sc.op("act", lambda e: e.copy(out=a_, in_=c.ps[:, 4, :]), reads=[c.r_ps[4]], writes=[c.r_tmp[0]])
                sc.op("dve", lambda e: e.tensor_copy(out=b_, in_=c.ps[:, 5, :]), reads=[c.r_ps[5]], writes=[c.r_tmp[1]])
                sc.op("dve", lambda e: e.reciprocal(out=c.ms[:], in_=c.ps[:, 6, :]), reads=[c.r_ps[6]], writes=[c.r_ms])
                bank = strot[0] % 4
                strot[0] += 1
                sc.op("pe", lambda e, bank=bank: e.matmul(c.ps[:, bank, :], lhsT=c.ones[:], rhs=c.arena[:, 21, :], start=True, stop=True),
                      reads=[c.r_ar[21], rc], writes=[c.r_ps[bank]])
                sc.op("dve", lambda e: e.tensor_tensor(out=a_, in0=a_, in1=c.ms[:], op=ALU.mult), reads=[c.r_tmp[0], c.r_ms], writes=[c.r_tmp[0]])
                sc.op("dve", lambda e, bank=bank: e.reciprocal(out=c.rstd[:], in_=c.ps[:, bank, :]), reads=[c.r_ps[bank]], writes=[c.r_rstd])
                sc.op("dve", lambda e: e.tensor_tensor(out=b_, in0=b_, in1=c.rstd[:], op=ALU.mult), reads=[c.r_tmp[1], c.r_rstd], writes=[c.r_tmp[1]])
                sc.op("dve", lambda e: e.scalar_tensor_tensor(out=a_, in0=b_, scalar=sm[:, 4:5], in1=a_, op0=ALU.mult, op1=ALU.add),
                      reads=[c.r_tmp[1], c.r_tmp[0], r_sm], writes=[c.r_tmp[0]])
                sc.op("dve", lambda e: e.tensor_tensor(out=c.arena[:, 20, :], in0=a_, in1=a_, op=ALU.mult), reads=[c.r_tmp[0]], writes=[c.r_ar[20]])
                bank = strot[0] % 4
                strot[0] += 1
                sc.op("pe", lambda e, bank=bank: e.matmul(c.ps[:, bank, :], lhsT=c.ones[:], rhs=c.arena[:, 20, :], start=True, stop=True),
                      reads=[c.r_ar[20], rc], writes=[c.r_ps[bank]])
                sc.op("act", lambda e, bank=bank: e.activation(out=c.ms[:], in_=c.ps[:, bank, :], func=AF.Ln, scale=1.0 / 128, bias=EPS),
                      reads=[c.r_ps[bank]], writes=[c.r_ms])
                sc.op("act", lambda e: e.activation(out=c.rstd[:], in_=c.ms[:], func=AF.Exp, scale=-0.5),
                      reads=[c.r_ms], writes=[c.r_rstd])
                sc.op("dve", lambda e, h=h: e.scalar_tensor_tensor(out=onT[:, h, :], in0=a_, scalar=sm[:, 5:6], in1=c.rstd[:], op0=ALU.mult, op1=ALU.mult),
                      reads=[c.r_tmp[0], c.r_rstd, r_sm], writes=[r_on[h]])

            def diag_pat(d):
                off = 896 - 128 * d
                return wmask[:, off:off + 512]
            negpat = wmask[:, 0:512]
            selA, selB = sel[:, 2 * par, :], sel[:, 2 * par + 1, :]

            for sidx, (idx, h, ci, o, i) in enumerate(steps):
                if o % 4 == 0 and i == 0:
                    ensure(idx)
                    ensure(idx + 1)
                    ensure(idx + 2)
                if o % 4 == 1 and i == 0:
                    ensure(idx + 3)
                b, rkv = loaded[idx]
                bank = strot[0] % 4
                strot[0] += 1
                pt = ptrot[0] % 4
                ptrot[0] += 1
                diag = (ci == 0)
                sc.op("pe", lambda e, b=b, h=h, o=o, i=i, bank=bank, diag=diag: e.matmul(c.ps[:, bank, :], lhsT=kring[0:69, b, i, (o % 4) * 128:(o % 4 + 1) * 128], rhs=Qt[0:69, h, i, :], start=True, stop=(not diag)),
                      reads=[rkv, c.r_ar[2 * h + i], r_qaug], writes=[c.r_ps[bank]], inc=(not diag))
                if diag:
                    if o < 4:
                        mm = [(selA, diag_pat(o))]
                    else:
                        mm = [(selB, diag_pat(o - 4))]
                    for mi, (lt, rh) in enumerate(mm):
                        lastm = (mi == len(mm) - 1)
                        sc.op("pe", lambda e, bank=bank, lt=lt, rh=rh, lastm=lastm: e.matmul(c.ps[:, bank, :], lhsT=lt, rhs=rh, start=False, stop=lastm),
                              reads=[r_c5, r_c6], writes=[c.r_ps[bank]], inc=lastm)
                sc.op("act", lambda e, bank=bank, pt=pt: e.activation(out=c.arena[:, 16 + pt, :], in_=c.ps[:, bank, :], func=AF.Exp, scale=0.125),
                      reads=[c.r_ps[bank]], writes=[r_pt[pt]])
                hs = sidx - h * nsteps_h
                first = hs < 2
                last = hs >= nsteps_h - 2
                pend.append((idx, h, ci, o, i, pt, first, last))
                if len(pend) > LAG:
                    emit_av(pend.pop(0))
            while pend:
                emit_av(pend.pop(0))

        def emit_tile_B(t):
            ts = slice(t * T, (t + 1) * T)
            emit_norm(sc, c, t, vecs[:, 4, :], lnexp=True)
            for i in range(2):
                sc.dma("sp", lambda e, t=t, i=i: e.dma_start(out=Qt[64:69, :, i, :], in_=qaug_d[t, :, :, :]), writes=[r_qaug] + c.r_ar[0:16])

            def evac_q(h, bank):
                sc.op("act", lambda e, h=h, bank=bank: e.copy(out=Qt[0:64, h, 0, :], in_=c.ps[0:64, bank, :]), reads=[c.r_ps[bank]], writes=[c.r_ar[2 * h]])
                sc.op("act", lambda e, h=h, bank=bank: e.copy(out=Qt[0:64, h, 1, :], in_=c.ps[64:128, bank, :]), reads=[c.r_ps[bank]], writes=[c.r_ar[2 * h + 1]])
            emit_proj_fm(sc, c, wr, wb["q"][0], r_w["q"], 0, D, evac_q, kouter=True)
            emit_attention(t)

            def evac_res(n, bank):
                sc.op("dve", lambda e, n=n, bank=bank: e.tensor_tensor(out=c.xT[:, n, ts], in0=c.xT[:, n, ts], in1=c.ps[:, bank, :], op=ALU.add),
                      reads=[c.r_ps[bank], c.r_x[n][t]], writes=[c.r_x[n][t]])
            emit_proj_fm(sc, c, wr, wb["o"][0], r_w["o"], 0, D, evac_res, src=onT, r_src=r_on)
            emit_norm(sc, c, t, vecs[:, 5, :], lnexp=True)
            emit_ffn(sc, c, t, wr, wb["gu1"][0], wb["dn1"][0], r_w["gu1"], r_w["dn1"], post_add=lambda nn, t=t: emit_square(sc, c, t, nn))
            bank = c.stat_bank
            for k in range(8):
                sc.op("pe", lambda e, k=k: e.matmul(c.ps[:, bank, :], lhsT=c.ones[:], rhs=c.sq[:, k, :], start=(k == 0), stop=(k == 7)),
                      reads=[c.r_sq[k], rc], writes=[c.r_ps[bank]], inc=(k == 7))
            sc.op("act", lambda e: e.activation(out=c.ms[:], in_=c.ps[:, bank, :], func=AF.Ln, scale=1.0 / D, bias=EPS),
                  reads=[c.r_ps[bank]], writes=[c.r_ms])
            sc.op("act", lambda e: e.activation(out=c.rstd[:], in_=c.ms[:], func=AF.Exp, scale=-0.5),
                  reads=[c.r_ms], writes=[c.r_rstd])
            for k in range(8):
                sb = k % 2
                sc.op("dve", lambda e, k=k, sb=sb: e.scalar_tensor_tensor(out=c.tmp[:, sb, :], in0=c.xT[:, k, ts], scalar=vecs[:, 6, k:k + 1], in1=c.rstd[:],
                                                                       op0=ALU.mult, op1=ALU.mult),
                      reads=[c.r_x[k][t], c.r_rstd, rc], writes=[c.r_tmp[sb]])
                sc.dma("act", lambda e, k=k, sb=sb: e.dma_start(out=outTv[:, k, ts], in_=c.tmp[:, sb, :]), reads=[c.r_tmp[sb]])

        for t in range(NT):
            emit_tile_B(t)

        sc.finish("sp")
        sc.replay(nc)
    return nc, sc


def prep_F(inputs, core):
    b, role = core // 2, core % 2
    f = np.float32
    a = prep_A(inputs, core)
    col = lambda v: np.ascontiguousarray(np.asarray(v, f).reshape(8, 128).T)
    vecs = np.stack([col(inputs["a_norm"][0]), col(inputs["a_v_norm"][0]), col(inputs["ffn_norm"][0]), col(inputs["kv_norm"]),
                     col(inputs["b_norm"][0]), col(inputs["ffn_norm"][1]), col(inputs["final_norm"])], axis=1)
    slopes = alibi_slopes()
    pos = np.arange(S)
    khi, klo = pos // 128, pos % 128
    kaug = np.zeros((2, NH, 5, 2, S), np.float64)
    kaug[:, :, 0] = 1.0
    kaug[:, :, 1] = 1.0
    kaug[:, :, 2] = (slopes[:, None] * 128.0 * khi[None, :])[None, :, None, :]
    kaug[:, :, 3] = (slopes[:, None] * klo[None, :])[None, :, None, :]
    kaug[1, :, 4] = ((pos % 1024) >= 512).astype(np.float64)[None, None, :]
    qaug = np.zeros((NT, 5, NH, T), np.float64)
    for j in range(NT):
        qpos = tile_index(role, j) * T + np.arange(T)
        qhi, qlo = qpos // 128, qpos % 128
        qaug[j, 0] = -8.0 * slopes[:, None] * 128.0 * qhi[None, :]
        qaug[j, 1] = -8.0 * slopes[:, None] * qlo[None, :]
        qaug[j, 2] = 8.0
        qaug[j, 3] = 8.0
        qaug[j, 4] = -240000.0 if tile_index(role, j) == 2 * j else 0.0
    kk = np.arange(128)[:, None]
    cc = np.arange(1408)[None, :]
    wmask = np.where(cc - 896 - kk >= 0, 0.0, -240000.0).astype(NPBF)
    sel = np.zeros((128, 4, 128), np.float32)
    eye = np.eye(128, dtype=np.float32)
    for par in range(2):
        case_a = (tile_index(role, par) == 2 * par)
        sel[:, 2 * par, :] = eye if case_a else 0.0
        sel[:, 2 * par + 1, :] = 0.0 if case_a else eye
    return {
        "xT": a["xT"], "vecs": np.ascontiguousarray(vecs), "w_in": a["w_in"], "w_spT": a["w_spT"], "trilT": a["trilT"], "bsp": a["bsp"],
        "w_out": a["w_out"], "w_gu0": a["w_gu"], "w_dn0": a["w_dn"], "w_kv": a["w_kv"],
        "w_q": np.asarray(inputs["b_w_q"][0], f), "w_o": np.asarray(inputs["b_w_o"][0], f),
        "w_gu1": np.asarray(inputs["ffn_w_gu"][1], f), "w_dn1": np.asarray(inputs["ffn_w_down"][1], f),
        "kaug": kaug.astype(NPBF), "qaug": qaug.astype(NPBF), "wmask": wmask, "sel": sel.astype(NPBF),
        "lamb": np.ascontiguousarray(np.broadcast_to(np.asarray(inputs["b_lambda"][0], f).reshape(1, 256), (128, 256))),
        "subln": np.ascontiguousarray(np.asarray(inputs["b_subln"][0], f).reshape(128, 1)),
    }


def kernel(**inputs):
    if "F" not in _CACHE:
        _CACHE["F"] = build_F()[0]
    nc = _CACHE["F"]
    in_maps = [prep_F(inputs, cidx) for cidx in range(8)]
    res = run_bass_kernel_spmd(nc, in_maps, core_ids=list(range(8)))
    out = np.zeros((B, S, D), np.float32)
    for core in range(8):
        b, role = core // 2, core % 2
        oT = np.asarray(res.results[core]["outT"])
        for j in range(NT):
            i = tile_index(role, j)
            out[b, i * T:(i + 1) * T, :] = oT[:, j * T:(j + 1) * T].T
    return out
```
